# Optimizing a Trainium2 kernel written in Bass

```python
import jax, jax.numpy as jnp
from jax import lax
import numpy as np

D_MODEL = 1024
BATCH = 16
SEQ = 2048
DEPTH = 2

MLA_HEADS = 8
MLA_NOPE_DIM = 64
MLA_ROPE_DIM = 32
MLA_V_DIM = 64
MLA_Q_RANK = 384
MLA_KV_RANK = 256
MLA_QK_DIM = MLA_NOPE_DIM + MLA_ROPE_DIM
MOBA_HEADS = 8
MOBA_HEAD_DIM = 64
MOBA_WIDTH = MOBA_HEADS * MOBA_HEAD_DIM
MOBA_BLOCK = 256
MOBA_TOPK = 3
MOBA_Q_CHUNK = 16
CONV_CH = 512
CONV_WIDTH = 31
D_FF = 2816
FFN_CONV_WIDTH = 3
ATTN_Q_BLOCK = 128
ROPE_THETA = 10000.0
NORM_EPS = 1e-6
N_BRANCH = 3
D_IN = MLA_Q_RANK + MLA_KV_RANK + MLA_ROPE_DIM + 3 * MOBA_WIDTH + 2 * CONV_CH + N_BRANCH * D_MODEL

kernel_name = 'hybrid_mla_moba_conformer_convffn_adaln'


def rms_norm(x, g):
    xf = x.astype(jnp.float32)
    y = xf * lax.rsqrt(jnp.mean(xf * xf, axis=-1, keepdims=True) + NORM_EPS)
    return (y * g.astype(jnp.float32)).astype(x.dtype)


def layer_norm(x, g, b):
    xf = x.astype(jnp.float32)
    mu = jnp.mean(xf, axis=-1, keepdims=True)
    var = jnp.mean(jnp.square(xf - mu), axis=-1, keepdims=True)
    y = (xf - mu) * lax.rsqrt(var + NORM_EPS) * g.astype(jnp.float32) + b.astype(jnp.float32)
    return y.astype(x.dtype)


def rope_tables(positions, dim):
    inv_freq = 1.0 / (ROPE_THETA ** (jnp.arange(0, dim, 2, dtype=jnp.float32) / dim))
    ang = positions.astype(jnp.float32)[..., None] * inv_freq
    return jnp.cos(ang), jnp.sin(ang)


def apply_rope(x, cos, sin):
    xf = x.astype(jnp.float32)
    x1, x2 = jnp.split(xf, 2, axis=-1)
    c, s = cos[:, None], sin[:, None]
    return jnp.concatenate([x1 * c - x2 * s, x2 * c + x1 * s], axis=-1).astype(x.dtype)


def causal_depthwise_conv(x, w, b):
    width = w.shape[0]
    y = lax.conv_general_dilated(
        x, w[:, None, :].astype(x.dtype), window_strides=(1,), padding=[(width - 1, 0)],
        dimension_numbers=('NWC', 'WIO', 'NWC'), feature_group_count=x.shape[-1])
    return y + b


def modulate(h, shift, scale):
    return h * (1.0 + scale[:, None, :]) + shift[:, None, :]


def causal_attention(q, k, v):
    B, H, S, dqk = q.shape
    dv = v.shape[-1]
    nqb = S // ATTN_Q_BLOCK
    scale = dqk ** -0.5
    q_blocks = q.reshape(B, H, nqb, ATTN_Q_BLOCK, dqk).transpose(2, 0, 1, 3, 4)
    kpos = jnp.arange(S)

    def one_block(args):
        qb, bi = args
        s = jnp.einsum('bhqd,bhkd->bhqk', qb, k).astype(jnp.float32) * scale
        qpos = bi * ATTN_Q_BLOCK + jnp.arange(ATTN_Q_BLOCK)
        s = jnp.where(kpos[None, :] <= qpos[:, None], s, -jnp.inf)
        p = jax.nn.softmax(s, axis=-1).astype(v.dtype)
        return jnp.einsum('bhqk,bhkd->bhqd', p, v)

    o = lax.map(one_block, (q_blocks, jnp.arange(nqb)))
    return o.transpose(1, 2, 0, 3, 4).reshape(B, H, S, dv)


def moba_attention(q, k, v):
    B, H, S, d = q.shape
    nb = -(-S // MOBA_BLOCK)
    s_pad = nb * MOBA_BLOCK
    pad = [(0, 0), (0, 0), (0, s_pad - S), (0, 0)]
    q, k, v = jnp.pad(q, pad), jnp.pad(k, pad), jnp.pad(v, pad)
    k_blocks = k.reshape(B, H, nb, MOBA_BLOCK, d)
    v_blocks = v.reshape(B, H, nb, MOBA_BLOCK, d)
    k_mean = jnp.mean(k_blocks.astype(jnp.float32), axis=3)
    block_score = jnp.einsum('bhsd,bhnd->bhsn', q.astype(jnp.float32), k_mean)
    q_blk = jnp.arange(s_pad) // MOBA_BLOCK
    fully_past = jnp.arange(nb)[None, :] < q_blk[:, None]
    block_score = jnp.where(fully_past[None, None], block_score, -jnp.inf)
    topk = min(MOBA_TOPK, nb)
    _, sel = lax.top_k(block_score, topk)

    nc = s_pad // MOBA_Q_CHUNK
    q_c = q.reshape(B, H, nc, MOBA_Q_CHUNK, d).transpose(2, 0, 1, 3, 4)
    sel_c = sel.reshape(B, H, nc, MOBA_Q_CHUNK, topk).transpose(2, 0, 1, 3, 4)
    b_idx = jnp.arange(B)[:, None, None, None]
    h_idx = jnp.arange(H)[None, :, None, None]
    jpos = jnp.arange(MOBA_BLOCK)
    scale = d ** -0.5

    def one_chunk(args):
        qc, sc, ci = args
        qpos = ci * MOBA_Q_CHUNK + jnp.arange(MOBA_Q_CHUNK)
        own = (ci * MOBA_Q_CHUNK) // MOBA_BLOCK
        k_sel = k_blocks[b_idx, h_idx, sc]
        v_sel = v_blocks[b_idx, h_idx, sc]
        s_sel = jnp.einsum('bhqd,bhqnjd->bhqnj', qc, k_sel).astype(jnp.float32) * scale
        valid = jnp.arange(topk)[None, :] < (qpos // MOBA_BLOCK)[:, None]
        s_sel = jnp.where(valid[None, None, :, :, None], s_sel, -jnp.inf)
        k_own = lax.dynamic_index_in_dim(k_blocks, own, axis=2, keepdims=False)
        v_own = lax.dynamic_index_in_dim(v_blocks, own, axis=2, keepdims=False)
        s_own = jnp.einsum('bhqd,bhjd->bhqj', qc, k_own).astype(jnp.float32) * scale
        causal = own * MOBA_BLOCK + jpos[None, :] <= qpos[:, None]
        s_own = jnp.where(causal[None, None], s_own, -jnp.inf)
        s_all = jnp.concatenate([s_sel.reshape(B, H, MOBA_Q_CHUNK, topk * MOBA_BLOCK), s_own], axis=-1)
        p = jax.nn.softmax(s_all, axis=-1).astype(v.dtype)
        p_sel = p[..., :topk * MOBA_BLOCK].reshape(B, H, MOBA_Q_CHUNK, topk, MOBA_BLOCK)
        p_own = p[..., topk * MOBA_BLOCK:]
        return (jnp.einsum('bhqnj,bhqnjd->bhqd', p_sel, v_sel)
                + jnp.einsum('bhqj,bhjd->bhqd', p_own, v_own))

    o = lax.map(one_chunk, (q_c, sel_c, jnp.arange(nc)))
    return o.transpose(1, 2, 0, 3, 4).reshape(B, H, s_pad, d)[:, :, :S]


def token_mixer(h, cos_a, sin_a, cos_b, sin_b, w_in,
                mla_q_norm_g, w_uq, mla_kv_norm_g, w_ukv,
                mla_qn_nope_g, mla_qn_rope_g, mla_kn_nope_g, mla_kn_rope_g, w_mla_o,
                moba_qn_g, moba_kn_g, w_moba_o,
                conv_dw_w, conv_dw_b, conv_ln_g, conv_ln_b, w_conv_pw2, b_conv_pw2,
                w_out):
    B, S, _ = h.shape
    z = h @ w_in
    sizes = (MLA_Q_RANK, MLA_KV_RANK, MLA_ROPE_DIM, 3 * MOBA_WIDTH, 2 * CONV_CH, N_BRANCH * D_MODEL)
    q_lat, kv_lat, k_rope, qkv_b, conv_in, gate_logits = jnp.split(
        z, np.cumsum(sizes)[:-1].tolist(), axis=-1)

    q_a = (rms_norm(q_lat, mla_q_norm_g) @ w_uq).reshape(B, S, MLA_HEADS, MLA_QK_DIM).transpose(0, 2, 1, 3)
    q_nope = rms_norm(q_a[..., :MLA_NOPE_DIM], mla_qn_nope_g)
    q_pe = apply_rope(rms_norm(q_a[..., MLA_NOPE_DIM:], mla_qn_rope_g), cos_a, sin_a)
    kv = (rms_norm(kv_lat, mla_kv_norm_g) @ w_ukv).reshape(
        B, S, MLA_HEADS, MLA_NOPE_DIM + MLA_V_DIM).transpose(0, 2, 1, 3)
    k_nope = rms_norm(kv[..., :MLA_NOPE_DIM], mla_kn_nope_g)
    v_a = kv[..., MLA_NOPE_DIM:]
    k_pe = apply_rope(rms_norm(k_rope, mla_kn_rope_g)[:, None], cos_a, sin_a)
    q_full = jnp.concatenate([q_nope, q_pe], axis=-1)
    k_full = jnp.concatenate([k_nope, jnp.broadcast_to(k_pe, (B, MLA_HEADS, S, MLA_ROPE_DIM))], axis=-1)
    o_a = causal_attention(q_full, k_full, v_a)
    o_a = o_a.transpose(0, 2, 1, 3).reshape(B, S, MLA_HEADS * MLA_V_DIM) @ w_mla_o

    qkv = qkv_b.reshape(B, S, 3, MOBA_HEADS, MOBA_HEAD_DIM).transpose(2, 0, 3, 1, 4)
    q_b = apply_rope(rms_norm(qkv[0], moba_qn_g), cos_b, sin_b)
    k_b = apply_rope(rms_norm(qkv[1], moba_kn_g), cos_b, sin_b)
    o_b = moba_attention(q_b, k_b, qkv[2])
    o_b = o_b.transpose(0, 2, 1, 3).reshape(B, S, MOBA_WIDTH) @ w_moba_o

    glu_a, glu_g = jnp.split(conv_in, 2, axis=-1)
    u = glu_a * jax.nn.sigmoid(glu_g)
    u = causal_depthwise_conv(u, conv_dw_w, conv_dw_b)
    u = jax.nn.silu(layer_norm(u, conv_ln_g, conv_ln_b))
    o_c = u @ w_conv_pw2 + b_conv_pw2

    g = jax.nn.sigmoid(gate_logits.reshape(B, S, N_BRANCH, D_MODEL))
    merged = g[:, :, 0] * o_a + g[:, :, 1] * o_b + g[:, :, 2] * o_c
    return merged @ w_out


def channel_mixer(h, w_up, ffn_dw_w, ffn_dw_b, w_down):
    u = causal_depthwise_conv(h @ w_up, ffn_dw_w, ffn_dw_b)
    a, b = jnp.split(u, 2, axis=-1)
    return (jax.nn.silu(a) * b) @ w_down


def setup_inputs(seed: int = 0) -> dict:
    key = jax.random.key(seed)
    ks = iter(jax.random.split(key, 48))
    L = DEPTH

    def nrm(shape, fan_in):
        return jax.random.normal(next(ks), shape, jnp.float32) * (fan_in ** -0.5)

    def gain(shape):
        return 1.0 + 0.02 * jax.random.normal(next(ks), shape, jnp.float32)

    def bias(shape):
        return 0.01 * jax.random.normal(next(ks), shape, jnp.float32)

    x = jax.random.normal(next(ks), (BATCH, SEQ, D_MODEL), jnp.float32)
    c = jax.random.normal(next(ks), (BATCH, D_MODEL), jnp.float32)
    offset = jax.random.randint(next(ks), (BATCH, 1), 0, 1024, dtype=jnp.int32)
    positions = (offset + jnp.arange(SEQ, dtype=jnp.int32)[None, :]).astype(jnp.int32)
    return {
        'x': x, 'c': c, 'positions': positions,
        'w_mod': nrm((L, D_MODEL, 6 * D_MODEL), D_MODEL), 'b_mod': bias((L, 6 * D_MODEL)),
        'mix_norm_g': gain((L, D_MODEL)), 'w_in': nrm((L, D_MODEL, D_IN), D_MODEL),
        'mla_q_norm_g': gain((L, MLA_Q_RANK)),
        'w_uq': nrm((L, MLA_Q_RANK, MLA_HEADS * MLA_QK_DIM), MLA_Q_RANK),
        'mla_kv_norm_g': gain((L, MLA_KV_RANK)),
        'w_ukv': nrm((L, MLA_KV_RANK, MLA_HEADS * (MLA_NOPE_DIM + MLA_V_DIM)), MLA_KV_RANK),
        'mla_qn_nope_g': gain((L, MLA_NOPE_DIM)), 'mla_qn_rope_g': gain((L, MLA_ROPE_DIM)),
        'mla_kn_nope_g': gain((L, MLA_NOPE_DIM)), 'mla_kn_rope_g': gain((L, MLA_ROPE_DIM)),
        'w_mla_o': nrm((L, MLA_HEADS * MLA_V_DIM, D_MODEL), MLA_HEADS * MLA_V_DIM),
        'moba_qn_g': gain((L, MOBA_HEAD_DIM)), 'moba_kn_g': gain((L, MOBA_HEAD_DIM)),
        'w_moba_o': nrm((L, MOBA_WIDTH, D_MODEL), MOBA_WIDTH),
        'conv_dw_w': nrm((L, CONV_WIDTH, CONV_CH), CONV_WIDTH), 'conv_dw_b': bias((L, CONV_CH)),
        'conv_ln_g': gain((L, CONV_CH)), 'conv_ln_b': bias((L, CONV_CH)),
        'w_conv_pw2': nrm((L, CONV_CH, D_MODEL), CONV_CH), 'b_conv_pw2': bias((L, D_MODEL)),
        'w_out': nrm((L, D_MODEL, D_MODEL), D_MODEL),
        'ffn_norm_g': gain((L, D_MODEL)), 'w_up': nrm((L, D_MODEL, 2 * D_FF), D_MODEL),
        'ffn_dw_w': nrm((L, FFN_CONV_WIDTH, 2 * D_FF), FFN_CONV_WIDTH), 'ffn_dw_b': bias((L, 2 * D_FF)),
        'w_down': nrm((L, D_FF, D_MODEL), D_FF),
    }


def reference(x, c, positions, w_mod, b_mod, mix_norm_g, w_in,
              mla_q_norm_g, w_uq, mla_kv_norm_g, w_ukv,
              mla_qn_nope_g, mla_qn_rope_g, mla_kn_nope_g, mla_kn_rope_g, w_mla_o,
              moba_qn_g, moba_kn_g, w_moba_o,
              conv_dw_w, conv_dw_b, conv_ln_g, conv_ln_b, w_conv_pw2, b_conv_pw2,
              w_out, ffn_norm_g, w_up, ffn_dw_w, ffn_dw_b, w_down):
    cos_a, sin_a = rope_tables(positions, MLA_ROPE_DIM)
    cos_b, sin_b = rope_tables(positions, MOBA_HEAD_DIM)
    c_act = jax.nn.silu(c)
    for l in range(DEPTH):
        mod = c_act @ w_mod[l] + b_mod[l]
        sh1, sc1, g1, sh2, sc2, g2 = jnp.split(mod, 6, axis=-1)
        h = modulate(rms_norm(x, mix_norm_g[l]), sh1, sc1)
        y = token_mixer(h, cos_a, sin_a, cos_b, sin_b, w_in[l],
                        mla_q_norm_g[l], w_uq[l], mla_kv_norm_g[l], w_ukv[l],
                        mla_qn_nope_g[l], mla_qn_rope_g[l], mla_kn_nope_g[l], mla_kn_rope_g[l], w_mla_o[l],
                        moba_qn_g[l], moba_kn_g[l], w_moba_o[l],
                        conv_dw_w[l], conv_dw_b[l], conv_ln_g[l], conv_ln_b[l], w_conv_pw2[l], b_conv_pw2[l],
                        w_out[l])
        x = x + g1[:, None, :] * y
        h = modulate(rms_norm(x, ffn_norm_g[l]), sh2, sc2)
        x = x + g2[:, None, :] * channel_mixer(h, w_up[l], ffn_dw_w[l], ffn_dw_b[l], w_down[l])
    return x
```

```python
import math
import numpy as np
import concourse.bass as bass
import concourse.mybir as mybir
from contextlib import ExitStack
from concourse.bass_utils import run_bass_kernel_spmd

F32 = mybir.dt.float32
BF16 = mybir.dt.bfloat16
I32 = mybir.dt.int32
AF = mybir.ActivationFunctionType
ALU = mybir.AluOpType
AX = mybir.AxisListType

S = 2048
TT = 512
NT = 4
D = 1024
KC = 8
DIN = 6304
DFF = 2816
EPS = 1e-6
NCORE = 8
BIG = 30000.0
CP_SLACK = 50.0
NACC = 1
NSB = 3
TFX_MLA = 8
TFX_MOBA = 4
PROF = False
CRIT = None


class Em:
    ENGS = ('pe', 'act', 'dve', 'pool', 'sp')
    NDMA = 12

    def __init__(self, nc):
        self.nc = nc
        self.ops = []
        self.lastw = {}
        self.readers = {}
        self.per_eng = {e: [] for e in self.ENGS}
        self.bars = []

    def op(self, eng, name, reads=(), writes=(), dma=False, **kw):
        i = len(self.ops)
        deps = set()
        for r in list(reads) + list(writes):
            w = self.lastw.get(r)
            if w is not None:
                deps.add(w)
        for w in writes:
            for rd in self.readers.get(w, ()):
                deps.add(rd)
        for w in writes:
            self.lastw[w] = i
            self.readers[w] = []
        for r in reads:
            self.readers.setdefault(r, []).append(i)
        deps.discard(i)
        self.ops.append(dict(eng=eng, name=name, kw=kw, deps=deps, dma=dma,
                             pos=len(self.per_eng[eng])))
        self.per_eng[eng].append(i)
        return i

    def barrier(self):
        self.bars.append(len(self.ops))
        self.lastw = {k: v for k, v in self.lastw.items() if isinstance(k, str) and k.startswith('WS')}
        self.readers = {k: v for k, v in self.readers.items() if isinstance(k, str) and k.startswith('WS')}

    @staticmethod
    def _fsz(ap):
        sh = ap.shape
        n = 1
        for d in sh[1:]:
            n *= d
        return n

    def _cost(self, o):
        kw = o['kw']
        if o['dma']:
            ap = kw['out']
            nbytes = self._fsz(ap) * ap.shape[0] * 4
            occ = 1100.0 if o['eng'] == 'pool' else 150.0
            return occ, 2500.0 + nbytes / 100.0
        e = o['eng']
        if e == 'pe':
            nn = self._fsz(kw['rhs']) if 'rhs' in kw else 128
            t = max(64, nn) / 1.95 + 20.0
            return t, t + 200.0
        ap = kw.get('out', kw.get('ap'))
        nn = self._fsz(ap)
        if e == 'act':
            t = 170.0 + nn / 1.35
        elif e == 'dve':
            t = 150.0 + nn * (1.15 if o['name'] == 'scalar_tensor_tensor' else 0.95)
        else:
            t = 300.0 + nn * 2.0 if o['name'] != 'memset' else 100.0 + nn * 0.2
        return t, t + 250.0

    def schedule(self):
        import bisect
        ops = self.ops
        n = len(ops)
        bars = sorted(self.bars)
        seg = [bisect.bisect_right(bars, i) for i in range(n)]
        nseg = (seg[-1] if n else 0) + 1
        users = [[] for _ in range(n)]
        for i, o in enumerate(ops):
            for d in o['deps']:
                users[d].append(i)
        ndl = [len(o['deps']) for o in ops]
        hoist = [o['dma'] and o['eng'] == 'pool' for o in ops]
        ready = [0.0] * n
        sched = [False] * n
        costs = [self._cost(o) for o in ops]
        tail = [0.0] * n
        for i in range(n - 1, -1, -1):
            t = 0.0
            for u in users[i]:
                if seg[u] == seg[i] and tail[u] > t:
                    t = tail[u]
            tail[i] = t + costs[i][1]
        SLACK = CP_SLACK
        segrem = [0] * (nseg + 1)
        for i in range(n):
            segrem[seg[i]] += 1
        segfin = [0.0] * (nseg + 1)
        self._why = {}
        self._st = {}
        self._fin = [0.0] * n
        head = {e: 0 for e in self.ENGS}
        free = {e: 0.0 for e in self.ENGS}
        order = {e: [] for e in self.ENGS}
        WIN = {'pe': 500, 'act': 160, 'dve': 160, 'pool': 80, 'sp': 80}
        segdone_upto = 0
        segfin_cum = [0.0] * (nseg + 2)
        for _ in range(n):
            while segdone_upto < nseg and segrem[segdone_upto] == 0:
                segfin_cum[segdone_upto + 1] = max(segfin_cum[segdone_upto], segfin[segdone_upto])
                segdone_upto += 1
            best = None
            for e in self.ENGS:
                lst = self.per_eng[e]
                h = head[e]
                L = len(lst)
                while h < L and sched[lst[h]]:
                    h += 1
                head[e] = h
                cnt = 0
                k = h
                fe = free[e]
                ebest = None
                W = WIN[e]
                while k < L and cnt < W:
                    i = lst[k]
                    k += 1
                    if sched[i]:
                        continue
                    cnt += 1
                    if ndl[i] > 0:
                        continue
                    st = ready[i]
                    if not hoist[i]:
                        sg = seg[i]
                        if sg > segdone_upto:
                            break
                        if segfin_cum[sg] > st:
                            st = segfin_cum[sg]
                    if fe > st:
                        st = fe
                    if ebest is None or st < ebest[0] - SLACK or (st <= ebest[0] + SLACK and tail[i] > ebest[3]):
                        ebest = (st, e, i, tail[i])
                if ebest is not None and (best is None or ebest[0] < best[0] or (ebest[0] == best[0] and ebest[2] < best[2])):
                    best = ebest
            st, e, i = best[0], best[1], best[2]
            occ, lat = costs[i]
            if getattr(self, 'prof', None) is not None:
                why = ('eng', order[e][-1] if order[e] else None) if (free[e] >= st - 1e-9 and order[e]) else None
                if why is None:
                    bd = None
                    for d in ops[i]['deps']:
                        if self._fin[d] >= st - 1e-9:
                            bd = d
                    why = ('dep', bd)
                self._why[i] = why
                self._st[i] = st
                self._fin[i] = st + lat
            free[e] = st + occ
            f = st + lat
            sched[i] = True
            order[e].append(i)
            for u in users[i]:
                ndl[u] -= 1
                if f > ready[u]:
                    ready[u] = f
            sg = seg[i]
            segrem[sg] -= 1
            if f > segfin[sg]:
                segfin[sg] = f
            if getattr(self, 'prof', None) is not None:
                self.prof.setdefault(sg, {}).setdefault(e, [0.0, 0])
                self.prof[sg][e][0] += occ
                self.prof[sg][e][1] += 1
        self.per_eng = order
        for e in self.ENGS:
            for p, i in enumerate(order[e]):
                ops[i]['pos'] = p
        self.seg = seg
        self.hoist = hoist
        self.est_total = max(free.values())
        if getattr(self, 'prof', None) is not None and getattr(self, 'crit_seg', None) is not None:
            cs = self.crit_seg
            last = max((i for i in range(n) if seg[i] == cs), key=lambda i: self._fin[i])
            i = last
            stats = {}
            chain = []
            while i is not None and seg[i] == cs:
                kind, j = self._why[i]
                key = (kind, ops[i]['eng'], ops[i]['name'])
                t_prev = self._st[j] if (j is not None and j in self._st) else self._st[i]
                stats[key] = stats.get(key, 0.0) + (self._st[i] - t_prev)
                chain.append((i, ops[i]['eng'], ops[i]['name'], kind, round(self._st[i] / 1e3, 2)))
                i = j
            print('critical path seg', cs, 'len', len(chain))
            for k, v in sorted(stats.items(), key=lambda kv: -kv[1])[:14]:
                print('   %-40s %8.1f us' % (str(k), v / 1e3))
            print('   tail of chain:', chain[:40])
        if getattr(self, 'prof', None) is not None:
            prev = 0.0
            for sg in range(nseg):
                end = max(prev, segfin[sg])
                d = end - prev
                if d > 0:
                    print('seg %3d dur %8.1f us ' % (sg, d / 1e3) + ' '.join('%s %3d%%(%d)' % (e, 100 * self.prof.get(sg, {}).get(e, [0, 0])[0] / d, self.prof.get(sg, {}).get(e, [0, 0])[1]) for e in self.ENGS))
                prev = end
        print('[em] ops', n, {e: len(order[e]) for e in self.ENGS}, 'est_ms %.3f' % (self.est_total / 1e6), flush=True)

    def emit(self, stack):
        nc = self.nc
        ops = self.ops
        n = len(ops)
        self.schedule()
        seg, hoist = self.seg, self.hoist
        nseg = (max(seg) if n else 0) + 1
        seg_dmas = [[] for _ in range(nseg + 1)]
        for i, o in enumerate(ops):
            if o['dma'] and not hoist[i]:
                seg_dmas[seg[i]].append(i)
        last_comp = {}
        for e in self.ENGS:
            cur = {}
            for i in self.per_eng[e]:
                if not ops[i]['dma']:
                    cur[seg[i]] = i
            last_comp[e] = cur
        for e in self.ENGS:
            covered = 0
            for i in self.per_eng[e]:
                if hoist[i]:
                    continue
                sg = seg[i]
                if sg > covered:
                    for e2 in self.ENGS:
                        lc = last_comp[e2]
                        for s2 in range(covered, sg):
                            if s2 in lc:
                                ops[i]['deps'].add(lc[s2])
                    for s2 in range(covered, sg):
                        for j in seg_dmas[s2]:
                            ops[i]['deps'].add(j)
                    covered = sg
        need = [False] * n
        for i, o in enumerate(ops):
            for d in o['deps']:
                od = ops[d]
                if od['dma'] or od['eng'] != o['eng']:
                    need[d] = True
                elif od['eng'] == 'pe':
                    pass
                elif o['pos'] - od['pos'] <= 3:
                    need[d] = True
        sems = {e: stack.enter_context(nc.semaphore('s_' + e)) for e in self.ENGS}
        dsems = {e: [stack.enter_context(nc.semaphore('d_%s_%d' % (e, k))) for k in range(self.NDMA)]
                 for e in ('sp', 'pool', 'act')}
        cnt = {e: 0 for e in self.ENGS}
        dcnt = {e: [0] * self.NDMA for e in dsems}
        dnum = {e: 0 for e in dsems}
        dprev = {}
        for e in self.ENGS:
            for i in self.per_eng[e]:
                o = ops[i]
                if o['dma']:
                    k = dnum[e] % self.NDMA
                    dnum[e] += 1
                    dcnt[e][k] += 16
                    o['sig'] = (dsems[e][k], dcnt[e][k])
                    pk = (e, k)
                    if pk in dprev:
                        o['deps'].add(dprev[pk])
                    dprev[pk] = i
                elif need[i]:
                    cnt[e] += 1
                    o['sig'] = (sems[e], cnt[e])
                else:
                    o['sig'] = None
        block = stack.enter_context(nc.Block())
        em = self

        def run(ename):
            def body(engine):
                seen = {}
                for i in em.per_eng[ename]:
                    o = ops[i]
                    waits = {}
                    for d in o['deps']:
                        od = ops[d]
                        if od['sig'] is None:
                            continue
                        if (not od['dma']) and od['eng'] == ename and ename == 'pe':
                            continue
                        s, v = od['sig']
                        key = id(s)
                        if key not in waits or waits[key][1] < v:
                            waits[key] = (s, v)
                    for key, (s, v) in waits.items():
                        if seen.get(key, 0) >= v:
                            continue
                        seen[key] = v
                        engine.wait_ge(s, v)
                    try:
                        ins = getattr(engine, o['name'])(**o['kw'])
                    except Exception:
                        print('EMIT FAIL op', i, ename, o['name'], {k: (getattr(v, 'shape', v)) for k, v in o['kw'].items()})
                        raise
                    if o['sig'] is not None:
                        ins.then_inc(o['sig'][0], 16 if o['dma'] else 1)
                if ename in dsems:
                    for k in range(em.NDMA):
                        if dcnt[ename][k] > 0:
                            engine.wait_ge(dsems[ename][k], dcnt[ename][k])
            return body

        block.tensor(run('pe'))
        block.scalar(run('act'))
        block.vector(run('dve'))
        block.gpsimd(run('pool'))
        block.sync(run('sp'))


VOFF = {}
_o = 0
for _n, _w in [('mixg', 8), ('ffng', 8), ('bmod', 48), ('qlg', 3), ('kvg', 2), ('gq', 1), ('gk', 1),
               ('mq', 1), ('mk', 1), ('cw', 124), ('cb', 4), ('lng', 4), ('lnb', 4), ('bpw', 8),
               ('fw', 132), ('fb', 44)]:
    VOFF[_n] = _o
    _o += _w
NV = _o
CM_ID, CM_ONES, CM_BD64, CM_BDA, CM_PSW64, CM_PSWA, CM_TRI = range(7)
NCM = 7
PC_FBS, PC_PBS, PC_FBC, PC_PBC, PC_FAS, PC_PAS, PC_FAC, PC_PAC, PC_QSC = range(9)
NPC = 16


def build_program(NSEQ=2, NL=2, dbg=None):
    nc = bass.Bass("TRN2", target_bir_lowering=False)

    def din(name, shape, dt=F32):
        return nc.dram_tensor(name, list(shape), dt, kind="ExternalInput").ap()

    x_d = din("x", [NSEQ, S, D])
    cT_d = din("cT", [128, KC * NSEQ])
    pos_d = din("pos", [NSEQ, S], I32)
    wmod_d = din("w_mod", [NL, D, 6 * D])
    win_d = din("w_in", [NL, D, DIN])
    wuq_d = din("w_uq", [NL, 384, 768])
    wukv_d = din("w_ukv", [NL, 256, 1024])
    wmo_d = din("w_mla_o", [NL, 512, D])
    wbo_d = din("w_moba_o", [NL, 512, D])
    wpw_d = din("w_conv_pw2", [NL, 512, D])
    wout_d = din("w_out", [NL, D, D])
    wup_d = din("w_up", [NL, D, 2 * DFF])
    wdn_d = din("w_down", [NL, DFF, D])
    vec_d = din("vec", [NL, 128, NV])
    cmat_d = din("cmat", [128, NCM * 128])
    ind_d = din("ind", [8, 1024])
    idf_d = din("identf", [128, 128])
    pcol_d = din("pcol", [128, NPC])
    out_d = nc.dram_tensor("out", [NSEQ, S, D], F32, kind="ExternalOutput").ap()
    XS = nc.dram_tensor("xs_scr", [D, S], F32).ap()
    OD = nc.dram_tensor("od_scr", [128, 4 * S], BF16).ap()
    dbg_d = {}
    if dbg:
        for k, shp in dbg.items():
            dbg_d[k] = nc.dram_tensor("dbg_" + k, list(shp), F32, kind="ExternalOutput").ap()

    st = ExitStack()
    with st:
        def sb(name, shape, dt):
            return st.enter_context(nc.sbuf_tensor(name, list(shape), dt))

        VEC = sb("VEC", [128, NL * NV], F32)
        MODS = sb("MODS", [128, NL * 48 * NSEQ], F32)
        AB = sb("AB", [128, 16], F32)
        IDF = sb("IDF", [128, 128], F32)
        PCOL = sb("PCOL", [128, NPC], F32)
        CM = sb("CM", [128, NCM * 128], BF16)
        IND = sb("IND", [8, 1024], BF16)
        CACT = sb("CACT", [128, KC * NSEQ], BF16)
        CTF = sb("CTF", [128, KC * NSEQ], F32)
        NW = 3
        WS = [sb("WS%d" % i, [128, 4096], BF16) for i in range(NW)]
        NTF = 6
        TF = [sb("TF%d" % i, [128, 512], F32) for i in range(NTF)]
        NTB = 4
        TB = [sb("TB%d" % i, [128, 512], BF16) for i in range(NTB)]
        NTP = 4
        TP = [sb("TP%d" % i, [128, 512], BF16) for i in range(NTP)]
        NTL = 3
        TL = [sb("TL%d" % i, [128, 512], F32) for i in range(NTL)]
        HT = sb("HT", [128, KC * S], BF16)
        R1 = sb("R1", [128, KC * S], BF16)
        TAB = sb("TAB", [128, 4 * S], BF16)
        SM = sb("SM", [128, 64], F32)
        ARN = 33000
        AR = sb("AR", [128, ARN], BF16)
        PS = [st.enter_context(nc.psum_tensor("PS%d" % i, [128, 512], F32)) for i in range(8)]

        em = Em(nc)
        em.prof = {} if PROF else None
        em.crit_seg = CRIT
        ctr = {'w': 0, 'tf': 0, 'tb': 0, 'ps': 0, 'acc': 0, 'tl': 0, 'psg': 0, 'pss': 0, 'tp': 0}
        cfg = {'split': False}

        def cm(i, r0=0, r1=128, c0=0, c1=128):
            return CM[r0:r1, i * 128 + c0:i * 128 + c1]

        def nxt(kind, n):
            ctr[kind] = (ctr[kind] + 1) % n
            return ctr[kind]

        tfx = []

        def tf():
            i = nxt('tf', NTF + len(tfx))
            if i < NTF:
                return TF[i], 'TF%d' % i
            return tfx[i - NTF], 'TFX%d' % (i - NTF)

        def tl():
            i = nxt('tl', NTL)
            return TL[i], 'TL%d' % i

        def tb():
            i = nxt('tb', NTB)
            return TB[i], 'TB%d' % i

        def ps():
            if cfg['split']:
                i = NSB + nxt('psg', 8 - NACC - NSB)
            else:
                i = nxt('ps', 8 - NACC)
            return PS[i], 'PS%d' % i

        def sbank():
            i = nxt('pss', NSB)
            return PS[i], 'PS%d' % i

        def ptile():
            i = nxt('tp', NTP)
            return TP[i], 'TP%d' % i

        def acc():
            i = 8 - NACC + nxt('acc', NACC)
            return PS[i], 'PS%d' % i

        def ws():
            i = nxt('w', NW)
            return WS[i], 'WS%d' % i

        class Arena:
            def __init__(self):
                self.off = 0

            def reset(self):
                self.off = 0

            def alloc(self, ncols, dt=BF16):
                n = ncols * (2 if dt in (F32, I32) else 1)
                n = (n + 3) // 4 * 4
                a = AR[:, self.off:self.off + n]
                self.off += n
                assert self.off <= ARN, self.off
                if dt == F32:
                    a = a.bitcast(F32)
                elif dt == I32:
                    a = a.bitcast(I32)
                return a
        arena = Arena()

        def V(l, name, c=0, n=1, r0=0, r1=128):
            o = l * NV + VOFF[name] + c
            return VEC[r0:r1, o:o + n]

        def MD(l, k, c, b):
            o = (l * 48 + k * 8 + c) * NSEQ + b
            return MODS[:, o:o + 1]

        def wload(view, p, kcn, ncols, extra=None):
            w, r = ws()
            dst = w[0:p, 0:kcn * ncols].rearrange("p (k n) -> p k n", n=ncols)
            em.op('pool', 'dma_start', writes=[r], dma=True, out=dst, in_=view)
            return dst, r

        def mm(out, lhsT, rhs, start, stop, reads, writes):
            em.op('pe', 'matmul', reads=reads, writes=writes, out=out, lhsT=lhsT, rhs=rhs,
                  start=start, stop=stop)

        def rstd_from(psum_ap, rows, scale, pres, long=False):
            t1, r1 = tf()
            em.op('act', 'activation', reads=[pres], writes=[r1], out=t1[rows[0]:rows[1], :],
                  in_=psum_ap, func=AF.Ln, scale=scale, bias=EPS)
            t2, r2 = tl() if long else tf()
            em.op('act', 'activation', reads=[r1], writes=[r2], out=t2[rows[0]:rows[1], :],
                  in_=t1[rows[0]:rows[1], :], func=AF.Exp, scale=-0.5)
            return t2, r2

        def dbg_dump(name, ap_sb, res, rows=128):
            if name in dbg_d:
                t, r = tf()
                em.op('dve', 'tensor_copy', reads=[res], writes=[r], out=t[0:rows, :], in_=ap_sb)
                em.op('sp', 'dma_start', reads=[r], writes=['dbg_' + name], dma=True,
                      out=dbg_d[name][0:rows, :], in_=t[0:rows, :])

        em.op('sp', 'dma_start', writes=['VEC'], dma=True,
              out=VEC[:].rearrange("p (l n) -> p l n", n=NV), in_=vec_d.rearrange("l p n -> p l n"))
        em.op('sp', 'dma_start', writes=['IDF'], dma=True, out=IDF[:], in_=idf_d)
        em.op('sp', 'dma_start', writes=['PCOL'], dma=True, out=PCOL[:], in_=pcol_d)
        em.op('sp', 'dma_start', writes=['CTF'], dma=True, out=CTF[:], in_=cT_d)
        em.op('pool', 'dma_start', writes=['CM'], dma=True, out=CM[:], in_=cmat_d)
        em.op('pool', 'dma_start', writes=['IND'], dma=True, out=IND[:], in_=ind_d)
        em.op('act', 'activation', reads=['CTF'], writes=['CACT'], out=CACT[:], in_=CTF[:], func=AF.Silu)
        for l in range(NL):
            for g in range(12):
                view = wmod_d[l].rearrange("(kc p) n -> p kc n", p=128)[:, :, g * 512:(g + 1) * 512]
                w, wr = wload(view, 128, KC, 512)
                for j in range(4):
                    ch = g * 4 + j
                    p_, pr = ps()
                    for kc in range(KC):
                        mm(p_[:, 0:NSEQ], w[:, kc, j * 128:(j + 1) * 128], CACT[:, kc * NSEQ:(kc + 1) * NSEQ],
                           kc == 0, kc == KC - 1, [wr, 'CACT'], [pr])
                    o = (l * 48 + ch) * NSEQ
                    em.op('dve', 'tensor_scalar', reads=[pr, 'VEC'], writes=['MODS'], out=MODS[:, o:o + NSEQ],
                          in0=p_[:, 0:NSEQ], scalar1=V(l, 'bmod', ch), scalar2=None, op0=ALU.add)

        def xs_view(c, c0, c1):
            return XS[c * 128:(c + 1) * 128, c0:c1]

        for b in range(NSEQ):
            arena.reset()
            XL = [arena.alloc(D, F32) for _ in range(2)]
            XT_ = [arena.alloc(D, F32) for _ in range(2)]
            for tt in range(16):
                xl = XL[tt % 2]
                xr = 'XL%d' % (tt % 2)
                em.op('sp', 'dma_start', writes=[xr], dma=True, out=xl, in_=x_d[b, tt * 128:(tt + 1) * 128, :])
                xt = XT_[tt % 2]
                xtr = 'XT%d' % (tt % 2)
                for hh in range(2):
                    p_, pr = ps()
                    for q in range(4):
                        c = hh * 4 + q
                        em.op('pe', 'transpose', reads=[xr, 'IDF'], writes=[pr], out=p_[:, q * 128:(q + 1) * 128],
                              in_=xl[:, c * 128:(c + 1) * 128], identity=IDF[:])
                    em.op('act' if hh == 0 else 'dve', 'activation' if hh == 0 else 'tensor_copy',
                          reads=[pr], writes=[xtr + 'h%d' % hh], out=xt[:, hh * 512:(hh + 1) * 512], in_=p_[:],
                          **({'func': AF.Copy} if hh == 0 else {}))
                em.op('sp', 'dma_start', reads=[xtr + 'h0', xtr + 'h1'],
                      writes=[('XSld', tt)], dma=True,
                      out=XS[:, tt * 128:(tt + 1) * 128].rearrange("(c p) t -> p c t", p=128),
                      in_=xt.rearrange("p (c t) -> p c t", t=128))

            POSI = arena.alloc(S, I32)
            POSF = arena.alloc(S, F32)
            VV = arena.alloc(S, F32)
            KI = arena.alloc(S, I32)
            KF = arena.alloc(S, F32)
            M1 = arena.alloc(S, F32)
            em.op('sp', 'dma_start', writes=['POSI'], dma=True, out=POSI,
                  in_=pos_d[b:b + 1, :].partition_broadcast(128).rearrange("p o s -> p (o s)"))
            em.op('dve', 'tensor_copy', reads=['POSI'], writes=['POSF'], out=POSF, in_=POSI)
            for ti, (fc, pc) in enumerate([(PC_FBC, PC_PBC), (PC_FBS, PC_PBS), (PC_FAC, PC_PAC), (PC_FAS, PC_PAS)]):
                em.op('dve', 'tensor_scalar', reads=['POSF', 'PCOL'], writes=['VV'], out=VV, in0=POSF,
                      scalar1=PCOL[:, fc:fc + 1], scalar2=PCOL[:, pc:pc + 1], op0=ALU.mult, op1=ALU.add)
                em.op('dve', 'tensor_copy', reads=['VV'], writes=['KI'], out=KI, in_=VV)
                em.op('dve', 'tensor_copy', reads=['KI'], writes=['KF'], out=KF, in_=KI)
                em.op('dve', 'tensor_tensor', reads=['VV', 'KF'], writes=['VV'], out=VV, in0=VV, in1=KF, op=ALU.subtract)
                em.op('dve', 'tensor_scalar', reads=['VV'], writes=['M1'], out=M1, in0=VV, scalar1=0.5, scalar2=None, op0=ALU.is_gt)
                em.op('dve', 'tensor_tensor', reads=['VV', 'M1'], writes=['VV'], out=VV, in0=VV, in1=M1, op=ALU.subtract)
                em.op('dve', 'tensor_scalar', reads=['VV'], writes=['M1'], out=M1, in0=VV, scalar1=-0.5, scalar2=None, op0=ALU.is_lt)
                em.op('dve', 'tensor_tensor', reads=['VV', 'M1'], writes=['VV'], out=VV, in0=VV, in1=M1, op=ALU.add)
                em.op('act', 'activation', reads=['VV'], writes=[('TAB', ti)], out=TAB[:, ti * S:(ti + 1) * S],
                      in_=VV, func=AF.Sin, scale=6.283185)
            em.barrier()

            def tabv(ti, r0, r1, c0, c1):
                return TAB[r0:r1, ti * S + c0:ti * S + c1]

            for l in range(NL):
                def md8(k):
                    o = (l * 48 + k * 8) * NSEQ + b
                    return MODS[:, o:o + 8 * NSEQ].rearrange("p (c s) -> p c s", s=NSEQ)[:, :, 0:1].rearrange("p c s -> p (c s)")
                em.op('dve', 'scalar_tensor_tensor', reads=['MODS', 'VEC'], writes=['AB'], out=AB[:, 0:8], in0=md8(1),
                      scalar=1.0, in1=V(l, 'mixg', 0, 8), op0=ALU.add, op1=ALU.mult)
                em.op('dve', 'scalar_tensor_tensor', reads=['MODS', 'VEC'], writes=['AB'], out=AB[:, 8:16], in0=md8(4),
                      scalar=1.0, in1=V(l, 'ffng', 0, 8), op0=ALU.add, op1=ALU.mult)

                def norm_phase(acol, shk):
                    arena.reset()
                    XLn = [arena.alloc(KC * TT, F32) for _ in range(2)]
                    for t in range(NT):
                        xl = XLn[t % 2]
                        xr = 'XLn%d' % (t % 2)
                        em.op('sp', 'dma_start', reads=[('XS', c, t) for c in range(KC)], writes=[xr], dma=True,
                              out=xl.rearrange("p (c t) -> p c t", t=TT),
                              in_=XS[:, t * TT:(t + 1) * TT].rearrange("(c p) t -> p c t", p=128))
                        pss, pssr = ps()
                        for c in range(KC):
                            sq, sqr = tb()
                            em.op('act', 'activation', reads=[xr], writes=[sqr], out=sq[:], in_=xl[:, c * TT:(c + 1) * TT], func=AF.Square)
                            mm(pss[:], cm(CM_ONES), sq[:], c == 0, c == KC - 1, [sqr, 'CM'], [pssr])
                        rs, rsr = rstd_from(pss[:], (0, 128), 1.0 / D, pssr, long=True)
                        for c in range(KC):
                            t1, t1r = tf()
                            em.op('dve', 'scalar_tensor_tensor', reads=[xr, rsr, 'AB'], writes=[t1r], out=t1[:],
                                  in0=xl[:, c * TT:(c + 1) * TT], scalar=AB[:, acol + c:acol + c + 1], in1=rs[:],
                                  op0=ALU.mult, op1=ALU.mult)
                            em.op('act', 'activation', reads=[t1r, 'MODS'], writes=[('HT', c, t)],
                                  out=HT[:, c * S + t * TT:c * S + (t + 1) * TT], in_=t1[:], func=AF.Identity,
                                  bias=MD(l, shk, c, b), scale=1.0)
                    em.barrier()

                def hT(kc, t):
                    return HT[:, kc * S + t * TT:kc * S + (t + 1) * TT], ('HT', kc, t)

                def dense(view, p, kcn, ncols, rhs_fn, out_fn, msub=128, tiles=range(NT)):
                    w, wr = wload(view, p, kcn, ncols)
                    for j in range((ncols + msub - 1) // msub):
                        m0, m1 = j * msub, min(ncols, (j + 1) * msub)
                        for t in tiles:
                            p_, pr = ps()
                            for kc in range(kcn):
                                ra, rr = rhs_fn(kc, t)
                                mm(p_[0:m1 - m0, :], w[:, kc, m0:m1], ra, kc == 0, kc == kcn - 1, [wr, rr], [pr])
                            out_fn(j, t, p_, pr)

                def win_view(c0, c1):
                    return win_d[l].rearrange("(kc p) n -> p kc n", p=128)[:, :, c0:c1]

                evac_ctr = [0]

                def evac_copy(dst, src, reads, writes):
                    evac_ctr[0] += 1
                    if evac_ctr[0] % 2:
                        em.op('act', 'activation', reads=reads, writes=writes, out=dst, in_=src, func=AF.Copy)
                    else:
                        em.op('dve', 'tensor_copy', reads=reads, writes=writes, out=dst, in_=src)

                def finalize_o(accp, accr, h, qt):
                    os_, osr = tf()
                    em.op('dve', 'tensor_copy', reads=[accr], writes=[osr], out=os_[0:65, :], in_=accp[0:65, :])
                    l1, l1r = tf()
                    em.op('act', 'activation', reads=[osr], writes=[l1r], out=l1[64:65, :], in_=os_[64:65, :], func=AF.Ln)
                    rb, rbr = tb()
                    em.op('act', 'activation', reads=[l1r], writes=[rbr], out=rb[64:65, :], in_=l1[64:65, :], func=AF.Exp, scale=-1.0)
                    p_, pr = ps()
                    mm(p_[0:64, :], cm(CM_ONES, 64, 65, 0, 64), rb[64:65, :], True, True, [rbr, 'CM'], [pr])
                    ot, otr = tb()
                    em.op('dve', 'tensor_tensor', reads=[osr, pr], writes=[otr], out=ot[0:64, :], in0=os_[0:64, :], in1=p_[0:64, :], op=ALU.mult)
                    tg = getattr(attention, 'tag', None)
                    if tg and h == 0 and qt == 2:
                        dbg_dump(tg, ot[0:64, :], otr, 64)
                    em.op('sp', 'dma_start', reads=[otr], writes=[('OD', h, qt)], dma=True,
                          out=OD[(h % 2) * 64:(h % 2) * 64 + 64, (h // 2) * S + qt * TT:(h // 2) * S + (qt + 1) * TT], in_=ot[0:64, :])

                def attention(h, kf, qf, kdim, vap, scale, bias=None):
                    for qt in range(NT):
                        accp, accr = acc()
                        nkt = 4 * qt + 4
                        for kt in range(nkt):
                            j0 = max(0, kt - 4 * qt) * 128
                            p_, pr = sbank()
                            ka, kr = kf(kt)
                            qa, qr = qf(qt * TT + j0, (qt + 1) * TT)
                            mm(p_[:, j0:TT], ka, qa, True, bias is None, list(kr) + [qr], [pr])
                            if bias is not None:
                                n = kt // 2
                                mm(p_[:, j0:TT], IND[0:8, n * 128:(n + 1) * 128], bias[0][0:8, qt * TT + j0:(qt + 1) * TT],
                                   False, True, ['IND'] + [(bias[1], x) for x in range(qt * 4, qt * 4 + 4)], [pr])
                            pt, ptr = ptile()
                            em.op('act', 'activation', reads=[pr], writes=[ptr], out=pt[:, j0:TT], in_=p_[:, j0:TT], func=AF.Exp, scale=scale)
                            if kt >= 4 * qt:
                                em.op('dve', 'tensor_tensor', reads=[ptr, 'CM'], writes=[ptr], out=pt[:, j0:j0 + 128],
                                      in0=pt[:, j0:j0 + 128], in1=cm(CM_TRI), op=ALU.mult)
                            va, vr = vap(kt)
                            mm(accp[0:65, j0:TT], va, pt[:, j0:TT], kt == 0, kt == nkt - 1, [vr, ptr], [accr])
                        finalize_o(accp, accr, h, qt)

                def merge_phase(bi, first):
                    arena.reset()
                    if bi < 2:
                        OA = arena.alloc(4 * S)
                        for c4 in range(4):
                            for qt in range(NT):
                                em.op('sp', 'dma_start', reads=[('OD', 2 * c4, qt), ('OD', 2 * c4 + 1, qt)], writes=[('OA', c4, qt)], dma=True,
                                      out=OA[:, c4 * S + qt * TT:c4 * S + (qt + 1) * TT],
                                      in_=OD[:, c4 * S + qt * TT:c4 * S + (qt + 1) * TT])
                        wsrc = (wmo_d, wbo_d)[bi][l].rearrange("(k p) n -> p k n", p=128)
                        pp, pk = 128, 4

                        def prhs(kk, t):
                            return OA[:, kk * S + t * TT:kk * S + (t + 1) * TT], ('OA', kk, t)
                    else:
                        wsrc = wpw_d[l].rearrange("(k p) n -> p k n", p=128)
                        pp, pk = 128, 4
                        UCb = merge_phase.UC

                        def prhs(kk, t):
                            return UCb[:, kk * S + t * TT:kk * S + (t + 1) * TT], ('UC', kk, t)
                    for g in range(2):
                        wg, wgr = wload(win_view(3232 + bi * 1024 + g * 512, 3232 + bi * 1024 + (g + 1) * 512), 128, KC, 512)
                        wp, wpr = wload(wsrc[:, :, g * 512:(g + 1) * 512], pp, pk, 512)
                        for j in range(4):
                            c = g * 4 + j
                            for t in range(NT):
                                pg, pgr = ps()
                                for kc in range(KC):
                                    ra, rr = hT(kc, t)
                                    mm(pg[:], wg[:, kc, j * 128:(j + 1) * 128], ra, kc == 0, kc == KC - 1, [wgr, rr], [pgr])
                                pq, pqr = ps()
                                for kk in range(pk):
                                    ra, rr = prhs(kk, t)
                                    mm(pq[:], wp[0:pp, kk, j * 128:(j + 1) * 128], ra, kk == 0, kk == pk - 1, [wpr, rr], [pqr])
                                sg, sgr = tf()
                                em.op('act', 'activation', reads=[pgr], writes=[sgr], out=sg[:], in_=pg[:], func=AF.Sigmoid)
                                mdst = R1[:, c * S + t * TT:c * S + (t + 1) * TT]
                                if first:
                                    em.op('dve', 'tensor_tensor', reads=[pqr, sgr], writes=[('M', c, t)], out=mdst, in0=pq[:], in1=sg[:], op=ALU.mult)
                                else:
                                    t1, t1r = tf()
                                    if bi == 2:
                                        em.op('dve', 'scalar_tensor_tensor', reads=[pqr, sgr, 'VEC'], writes=[t1r], out=t1[:], in0=pq[:],
                                              scalar=V(l, 'bpw', c), in1=sg[:], op0=ALU.add, op1=ALU.mult)
                                    else:
                                        em.op('dve', 'tensor_tensor', reads=[pqr, sgr], writes=[t1r], out=t1[:], in0=pq[:], in1=sg[:], op=ALU.mult)
                                    em.op('pool', 'tensor_tensor', reads=[t1r, ('M', c, t)], writes=[('M', c, t)], out=mdst, in0=mdst, in1=t1[:], op=ALU.add)
                    em.barrier()

                def resid_phase(wview_fn, ngroups, gcols, kcn, rhs_fn, gk):
                    for g in range(ngroups):
                        w, wr = wload(wview_fn(g), 128, kcn, gcols)
                        for j in range(gcols // 128):
                            c = g * (gcols // 128) + j
                            for t in range(NT):
                                p_, pr = ps()
                                for kc in range(kcn):
                                    ra, rr = rhs_fn(kc, t)
                                    mm(p_[:], w[:, kc, j * 128:(j + 1) * 128], ra, kc == 0, kc == kcn - 1, [wr, rr], [pr])
                                xl, xlr = tf()
                                em.op('sp', 'dma_start', reads=[('XS', c, t)], writes=[xlr], dma=True, out=xl[:], in_=xs_view(c, t * TT, (t + 1) * TT))
                                xn, xnr = tf()
                                em.op('dve', 'scalar_tensor_tensor', reads=[pr, xlr, 'MODS'], writes=[xnr], out=xn[:], in0=p_[:],
                                      scalar=MD(l, gk, c, b), in1=xl[:], op0=ALU.mult, op1=ALU.add)
                                em.op('sp', 'dma_start', reads=[xnr], writes=[('XS', c, t)], dma=True, out=xs_view(c, t * TT, (t + 1) * TT), in_=xn[:])
                    em.barrier()

                norm_phase(0, 0)
                D0 = bool(dbg) and l == 0 and b == 0
                if D0:
                    dbg_dump('ht', HT[:, 0:512], ('HT', 0, 0))
                    dbg_dump('ht1', HT[:, S:S + 512], ('HT', 1, 0))
                    dbg_dump('ht7', HT[:, 7 * S + 1536:8 * S], ('HT', 7, 3))

                arena.reset()
                del tfx[:]
                tfx.extend([R1[:, i * 1024:(i + 1) * 1024].bitcast(F32) for i in range(TFX_MLA)])
                LAT = arena.alloc(6 * S)
                KPE = arena.alloc(S)
                VA = arena.alloc(16 * 8 * 65)
                QH = [arena.alloc(S) for _ in range(2)]
                KH = [arena.alloc(S) for _ in range(2)]

                def lat(c, t, r0=0, r1=128):
                    return LAT[r0:r1, c * S + t * TT:c * S + (t + 1) * TT]

                def lat_out(cbase, rows=128):
                    def f(j, t, p_, pr):
                        evac_copy(lat(cbase + j, t, 0, rows), p_[0:rows, :], [pr], [('LAT', cbase + j, t)])
                        if D0 and cbase + j == 0 and t == 0:
                            dbg_dump('zraw', lat(0, 0), ('LAT', 0, 0))
                    return f
                dense(win_view(0, 512), 128, KC, 512, hT, lat_out(0))
                dense(win_view(512, 640), 128, KC, 128, hT, lat_out(4))
                dense(win_view(576, 672), 128, KC, 96, hT, lat_out(5, 96), msub=96)
                for (c0, ncn, gname) in [(0, 3, 'qlg'), (3, 2, 'kvg')]:
                    for t in range(NT):
                        pss, pssr = ps()
                        for c in range(ncn):
                            sq, sqr = tb()
                            em.op('act', 'activation', reads=[('LAT', c0 + c, t)], writes=[sqr], out=sq[:], in_=lat(c0 + c, t), func=AF.Square)
                            mm(pss[:], cm(CM_ONES), sq[:], c == 0, c == ncn - 1, [sqr, 'CM'], [pssr])
                        rs, rsr = rstd_from(pss[:], (0, 128), 1.0 / (ncn * 128), pssr)
                        for c in range(ncn):
                            em.op('dve', 'scalar_tensor_tensor', reads=[('LAT', c0 + c, t), rsr, 'VEC'], writes=[('LAT', c0 + c, t)],
                                  out=lat(c0 + c, t), in0=lat(c0 + c, t), scalar=V(l, gname, c), in1=rs[:], op0=ALU.mult, op1=ALU.mult)

                def head_norm_rope(src_ps, src_res, rows, bdm, gcol, dst, dst_res, t, rope):
                    r0, r1 = rows
                    sq, sqr = tb()
                    em.op('act', 'activation', reads=[src_res], writes=[sqr], out=sq[r0:r1, :], in_=src_ps, func=AF.Square)
                    p2, p2r = ps()
                    mm(p2[r0:r1, :], cm(bdm, r0, r1, r0, r1), sq[r0:r1, :], True, True, [sqr, 'CM'], [p2r])
                    t1, t1r = tf()
                    em.op('act', 'activation', reads=[p2r, 'PCOL'], writes=[t1r], out=t1[r0:r1, :], in_=p2[r0:r1, :], func=AF.Ln,
                          scale=PCOL[r0:r1, PC_QSC:PC_QSC + 1] if bdm == CM_BDA else 1.0 / 64, bias=EPS)
                    rs, rsr = tf()
                    em.op('act', 'activation', reads=[t1r], writes=[rsr], out=rs[r0:r1, :], in_=t1[r0:r1, :], func=AF.Exp, scale=-0.5)
                    if not rope:
                        em.op('dve', 'scalar_tensor_tensor', reads=[src_res, rsr, 'VEC'], writes=[dst_res], out=dst, in0=src_ps,
                              scalar=gcol, in1=rs[r0:r1, :], op0=ALU.mult, op1=ALU.mult)
                        return
                    ctab, stab, psw = rope
                    qn, qnr = tb()
                    em.op('dve', 'scalar_tensor_tensor', reads=[src_res, rsr, 'VEC'], writes=[qnr], out=qn[r0:r1, :], in0=src_ps,
                          scalar=gcol, in1=rs[r0:r1, :], op0=ALU.mult, op1=ALU.mult)
                    p3, p3r = ps()
                    mm(p3[r0:r1, :], cm(psw, r0, r1, r0, r1), qn[r0:r1, :], True, True, [qnr, 'CM'], [p3r])
                    a1, a1r = tf()
                    em.op('dve', 'tensor_tensor', reads=[qnr, ('TAB', ctab)], writes=[a1r], out=a1[r0:r1, :], in0=qn[r0:r1, :],
                          in1=tabv(ctab, r0, r1, t * TT, (t + 1) * TT), op=ALU.mult)
                    a2, a2r = tf()
                    em.op('dve', 'tensor_tensor', reads=[p3r, ('TAB', stab)], writes=[a2r], out=a2[r0:r1, :], in0=p3[r0:r1, :],
                          in1=tabv(stab, r0, r1, t * TT, (t + 1) * TT), op=ALU.mult)
                    em.op('pool', 'tensor_tensor', reads=[a1r, a2r], writes=[dst_res], out=dst, in0=a1[r0:r1, :], in1=a2[r0:r1, :], op=ALU.add)

                if D0:
                    dbg_dump('qn', lat(0, 0), ('LAT', 0, 0))
                    dbg_dump('kvn', lat(3, 0), ('LAT', 3, 0))
                for t in range(NT):
                    head_norm_rope(lat(5, t, 64, 96), ('LAT', 5, t), (64, 96), CM_BDA, V(l, 'gk', 0, 1, 64, 96),
                                   KPE[64:96, t * TT:(t + 1) * TT], ('KPE', t), t, (2, 3, CM_PSWA))
                if D0:
                    dbg_dump('kpe', KPE[64:96, 0:512], ('KPE', 0), 32)
                em.op('pool', 'memset', writes=['VAones'], ap=VA.rearrange("p (n d) -> p n d", d=65)[:, :, 64:65], constant=1.0)
                wkv, wkvr = wload(wukv_d[l].rearrange("(kc p) n -> p kc n", p=128), 128, 2, 1024)
                for tt in range(16):
                    p_, pr = ps()
                    for kc in range(2):
                        mm(p_[:].rearrange("p (h d) -> p h d", d=64), LAT[:, (3 + kc) * S + tt * 128:(3 + kc) * S + (tt + 1) * 128],
                           wkv[:, kc, :].rearrange("p (h d) -> p h d", d=128)[:, :, 64:128], kc == 0, kc == 1,
                           [wkvr, ('LAT', 3 + kc, tt // 4)], [pr])
                    evac_copy(VA[:, tt * 520:(tt + 1) * 520].rearrange("p (h d) -> p h d", d=65)[:, :, 0:64],
                              p_[:].rearrange("p (h d) -> p h d", d=64), [pr, 'VAones'], [('VA', tt)])
                wq, wqr = wload(wuq_d[l].rearrange("(kc p) n -> p kc n", p=128), 128, 3, 768)
                cfg['split'] = True
                for h in range(8):
                    qh, kh = QH[h % 2], KH[h % 2]
                    qres, kres = 'QH%d' % (h % 2), 'KH%d' % (h % 2)
                    for t in range(NT):
                        p_, pr = ps()
                        for kc in range(3):
                            mm(p_[0:96, :], wq[:, kc, h * 96:(h + 1) * 96], lat(kc, t), kc == 0, kc == 2, [wqr, ('LAT', kc, t)], [pr])
                        head_norm_rope(p_[0:96, :], pr, (0, 96), CM_BDA, V(l, 'gq', 0, 1, 0, 96), qh[0:96, t * TT:(t + 1) * TT], (qres, t), t,
                                       (2, 3, CM_PSWA))
                        p_, pr = ps()
                        for kc in range(2):
                            mm(p_[0:64, :], wkv[:, kc, h * 128:h * 128 + 64], lat(3 + kc, t), kc == 0, kc == 1, [wkvr, ('LAT', 3 + kc, t)], [pr])
                        head_norm_rope(p_[0:64, :], pr, (0, 64), CM_BD64, V(l, 'gk', 0, 1, 0, 64), kh[0:64, t * TT:(t + 1) * TT], (kres, t, 'n'), t, None)
                        em.op('dve', 'tensor_copy', reads=[('KPE', t)], writes=[(kres, t, 'r')], out=kh[64:96, t * TT:(t + 1) * TT],
                              in_=KPE[64:96, t * TT:(t + 1) * TT])
                    if D0 and h == 0:
                        dbg_dump('qh0', qh[0:96, 0:512], (qres, 0), 96)
                        dbg_dump('kh0', kh[0:96, 0:512], (kres, 0, 'n'), 96)
                        dbg_dump('va0', VA[:, 0:512], ('VA', 0))
                    attention.tag = 'oa' if D0 else None
                    attention(h,
                              lambda kt, kh=kh, kres=kres: (kh[0:96, kt * 128:(kt + 1) * 128], [(kres, kt // 4, 'n'), (kres, kt // 4, 'r')]),
                              lambda c0, c1, qh=qh, qres=qres: (qh[0:96, c0:c1], (qres, c0 // TT)),
                              96,
                              lambda kt, h=h: (VA[:, kt * 520 + h * 65:kt * 520 + h * 65 + 65], ('VA', kt)),
                              96.0 ** -0.5)
                em.barrier()
                cfg['split'] = False
                del tfx[:]
                merge_phase(0, True)

                arena.reset()
                QKB = arena.alloc(8 * S)
                VB = arena.alloc(16 * 8 * 65)
                BIA = [arena.alloc(S) for _ in range(2)]
                KMH = arena.alloc(8)
                KML = arena.alloc(8)
                KMF = arena.alloc(8, F32)
                KMD = arena.alloc(8, F32)
                del tfx[:]
                tfx.extend([arena.alloc(512, F32) for _ in range(TFX_MOBA)])

                def qkb(c, t, r0=0, r1=128):
                    return QKB[r0:r1, c * S + t * TT:c * S + (t + 1) * TT]

                def qk_out(cbase):
                    def f(j, t, p_, pr):
                        c = cbase + j
                        head_norm_rope(p_[:], pr, (0, 128), CM_BD64, V(l, 'mq' if c < 4 else 'mk', 0, 1), qkb(c, t), ('QKB', c, t), t,
                                       (0, 1, CM_PSW64))
                    return f
                dense(win_view(672, 1184), 128, KC, 512, hT, qk_out(0))
                dense(win_view(1184, 1696), 128, KC, 512, hT, qk_out(4))
                em.op('pool', 'memset', writes=['VBones'], ap=VB.rearrange("p (n d) -> p n d", d=65)[:, :, 64:65], constant=1.0)
                wv, wvr = wload(win_view(1696, 2208), 128, KC, 512)
                for tt in range(16):
                    p_, pr = ps()
                    for kc in range(KC):
                        mm(p_[:], HT[:, kc * S + tt * 128:kc * S + (tt + 1) * 128], wv[:, kc, :], kc == 0, kc == KC - 1,
                           [wvr, ('HT', kc, tt // 4)], [pr])
                    evac_copy(VB[:, tt * 520:(tt + 1) * 520].rearrange("p (h d) -> p h d", d=65)[:, :, 0:64],
                              p_[:].rearrange("p (h d) -> p h d", d=64), [pr, 'VBones'], [('VB', tt)])
                cfg['split'] = True
                for h in range(8):
                    c = h // 2
                    r0 = (h % 2) * 64
                    r1 = r0 + 64
                    bia = BIA[h % 2]
                    bres = 'BIA%d' % (h % 2)
                    em.op('dve', 'tensor_reduce', reads=[('QKB', 4 + c, t) for t in range(NT)], writes=['KMF'], out=KMF[r0:r1, :],
                          in_=QKB[r0:r1, (4 + c) * S:(5 + c) * S].rearrange("p (n j) -> p n j", j=256), axis=AX.X, op=ALU.add)
                    em.op('dve', 'tensor_copy', reads=['KMF'], writes=['KMH'], out=KMH[r0:r1, :], in_=KMF[r0:r1, :])
                    em.op('dve', 'tensor_tensor', reads=['KMF', 'KMH'], writes=['KMD'], out=KMD[r0:r1, :], in0=KMF[r0:r1, :], in1=KMH[r0:r1, :], op=ALU.subtract)
                    em.op('dve', 'tensor_copy', reads=['KMD'], writes=['KML'], out=KML[r0:r1, :], in_=KMD[r0:r1, :])
                    for tt in range(16):
                        qb = tt // 2
                        sc = SM[:, 0:8]
                        bt = SM[:, 16:24]
                        m8 = SM[:, 32:40]
                        if qb >= 4:
                            p_, pr = ps()
                            mm(p_[:, 0:8], QKB[r0:r1, c * S + tt * 128:c * S + (tt + 1) * 128], KMH[r0:r1, :], True, False,
                               [('QKB', c, tt // 4), 'KMH'], [pr])
                            mm(p_[:, 0:8], QKB[r0:r1, c * S + tt * 128:c * S + (tt + 1) * 128], KML[r0:r1, :], False, True,
                               [('QKB', c, tt // 4), 'KML'], [pr])
                            em.op('dve', 'memset', writes=['SC'], ap=sc, constant=-1e30)
                            em.op('dve', 'tensor_copy', reads=[pr, 'SC'], writes=['SC'], out=sc[:, 0:qb], in_=p_[:, 0:qb])
                            em.op('dve', 'max', reads=['SC'], writes=['M8'], out=m8, in_=sc)
                            em.op('dve', 'tensor_scalar', reads=['SC', 'M8'], writes=['BT'], out=bt, in0=sc, scalar1=m8[:, 2:3], scalar2=None, op0=ALU.is_ge)
                            em.op('dve', 'tensor_scalar', reads=['BT'], writes=['BT'], out=bt, in0=bt, scalar1=1.0, scalar2=BIG, op0=ALU.subtract, op1=ALU.mult)
                            em.op('dve', 'memset', reads=['BT'], writes=['BT'], ap=bt[:, qb:qb + 1], constant=0.0)
                        else:
                            em.op('dve', 'memset', writes=['BT'], ap=bt, constant=-BIG)
                            em.op('dve', 'memset', reads=['BT'], writes=['BT'], ap=bt[:, 0:qb + 1], constant=0.0)
                        p2, p2r = ps()
                        em.op('pe', 'transpose', reads=['BT', 'IDF'], writes=[p2r], out=p2[0:8, 0:128], in_=bt, identity=IDF[:])
                        em.op('dve', 'tensor_copy', reads=[p2r], writes=[(bres, tt)], out=bia[0:8, tt * 128:(tt + 1) * 128], in_=p2[0:8, 0:128])
                    if D0 and h == 0:
                        dbg_dump('qb0', qkb(0, 2), ('QKB', 0, 2))
                        dbg_dump('kb0', qkb(4, 0), ('QKB', 4, 0))
                        dbg_dump('bia', bia[0:8, 1024:1536], (bres, 8), 8)
                    attention.tag = 'ob' if D0 else None
                    attention(h,
                              lambda kt, c=c, r0=r0, r1=r1: (QKB[r0:r1, (4 + c) * S + kt * 128:(4 + c) * S + (kt + 1) * 128], [('QKB', 4 + c, kt // 4)]),
                              lambda c0, c1, c=c, r0=r0, r1=r1: (QKB[r0:r1, c * S + c0:c * S + c1], ('QKB', c, c0 // TT)),
                              64,
                              lambda kt, h=h: (VB[:, kt * 520 + h * 65:kt * 520 + h * 65 + 65], ('VB', kt)),
                              0.125, bias=(bia, bres))
                em.barrier()
                cfg['split'] = False
                del tfx[:]
                merge_phase(1, False)

                arena.reset()
                UW = S + 32
                UU = arena.alloc(4 * UW)
                CV = arena.alloc(4 * S)
                UC = arena.alloc(4 * S)
                DG = [arena.alloc(31 * 128) for _ in range(2)]
                merge_phase.UC = UC
                for c in range(4):
                    em.op('pool', 'memset', writes=[('UUh', c)], ap=UU[:, c * UW:c * UW + 32], constant=0.0)
                    w, wr = ws()
                    w3 = w[:, 0:KC * 256].rearrange("p (k n) -> p k n", n=256)
                    em.op('pool', 'dma_start', writes=[wr], dma=True, out=w3[:, :, 0:128], in_=win_view(2208 + c * 128, 2208 + (c + 1) * 128))
                    em.op('pool', 'dma_start', reads=[wr], writes=[wr], dma=True, out=w3[:, :, 128:256], in_=win_view(2720 + c * 128, 2720 + (c + 1) * 128))
                    for t in range(NT):
                        pa, par = ps()
                        pg, pgr = ps()
                        for kc in range(KC):
                            ra, rr = hT(kc, t)
                            mm(pa[:], w3[:, kc, 0:128], ra, kc == 0, kc == KC - 1, [wr, rr], [par])
                        for kc in range(KC):
                            ra, rr = hT(kc, t)
                            mm(pg[:], w3[:, kc, 128:256], ra, kc == 0, kc == KC - 1, [wr, rr], [pgr])
                        sg, sgr = tf()
                        em.op('act', 'activation', reads=[pgr], writes=[sgr], out=sg[:], in_=pg[:], func=AF.Sigmoid)
                        em.op('dve', 'tensor_tensor', reads=[par, sgr], writes=[('UU', c, t)], out=UU[:, c * UW + 32 + t * TT:c * UW + 32 + (t + 1) * TT],
                              in0=pa[:], in1=sg[:], op=ALU.mult)
                for c in range(4):
                    dg = DG[c % 2]
                    dgr = 'DG%d' % (c % 2)
                    for k in range(31):
                        em.op('dve', 'tensor_scalar', reads=['CM', 'VEC'], writes=[(dgr, k)], out=dg[:, k * 128:(k + 1) * 128], in0=cm(CM_ID),
                              scalar1=V(l, 'cw', c * 31 + k), scalar2=None, op0=ALU.mult)
                    for t in range(NT):
                        p_, pr = ps()
                        for k in range(31):
                            o = c * UW + 2 + t * TT + k
                            rds = [(dgr, k), ('UU', c, t)]
                            if t > 0:
                                rds.append(('UU', c, t - 1))
                            else:
                                rds.append(('UUh', c))
                            mm(p_[:], dg[:, k * 128:(k + 1) * 128], UU[:, o:o + TT], k == 0, k == 30, rds, [pr])
                        em.op('act', 'activation', reads=[pr, 'VEC'], writes=[('CV', c, t)], out=CV[:, c * S + t * TT:c * S + (t + 1) * TT], in_=p_[:],
                              func=AF.Identity, bias=V(l, 'cb', c), scale=1.0)
                for t in range(NT):
                    pm, pmr = ps()
                    pq, pqr = ps()
                    for c in range(4):
                        cv = CV[:, c * S + t * TT:c * S + (t + 1) * TT]
                        mm(pm[:], cm(CM_ONES), cv, c == 0, c == 3, [('CV', c, t), 'CM'], [pmr])
                        sq, sqr = tb()
                        em.op('act', 'activation', reads=[('CV', c, t)], writes=[sqr], out=sq[:], in_=cv, func=AF.Square)
                        mm(pq[:], cm(CM_ONES), sq[:], c == 0, c == 3, [sqr, 'CM'], [pqr])
                    mean, meanr = tl()
                    em.op('act', 'activation', reads=[pmr], writes=[meanr], out=mean[:], in_=pm[:], func=AF.Copy, scale=1.0 / 512)
                    msq, msqr = tf()
                    em.op('dve', 'tensor_tensor', reads=[meanr], writes=[msqr], out=msq[:], in0=mean[:], in1=mean[:], op=ALU.mult)
                    var, varr = tf()
                    em.op('dve', 'scalar_tensor_tensor', reads=[pqr, msqr], writes=[varr], out=var[:], in0=pq[:], scalar=1.0 / 512, in1=msq[:],
                          op0=ALU.mult, op1=ALU.subtract)
                    rs, rsr = rstd_from(var[:], (0, 128), 1.0, varr, long=True)
                    for c in range(4):
                        cv = CV[:, c * S + t * TT:c * S + (t + 1) * TT]
                        d1, d1r = tf()
                        em.op('dve', 'tensor_tensor', reads=[('CV', c, t), meanr], writes=[d1r], out=d1[:], in0=cv, in1=mean[:], op=ALU.subtract)
                        d2, d2r = tf()
                        em.op('dve', 'tensor_tensor', reads=[d1r, rsr], writes=[d2r], out=d2[:], in0=d1[:], in1=rs[:], op=ALU.mult)
                        em.op('act', 'activation', reads=[d2r, 'VEC'], writes=[('UC', c, t)], out=UC[:, c * S + t * TT:c * S + (t + 1) * TT], in_=d2[:],
                              func=AF.Silu, scale=V(l, 'lng', c), bias=V(l, 'lnb', c))
                if D0:
                    dbg_dump('uu0', UU[:, 32:544], ('UU', 0, 0))
                    dbg_dump('cv0', CV[:, 0:512], ('CV', 0, 0))
                    dbg_dump('uc0', UC[:, 0:512], ('UC', 0, 0))
                em.barrier()
                merge_phase(2, False)
                if D0:
                    dbg_dump('m0', R1[:, 0:512], ('M', 0, 0))
                    em.barrier()

                arena.reset()
                wo = []
                for g in range(2):
                    wo.append(wload(wout_d[l].rearrange("(kc p) n -> p kc n", p=128)[:, :, g * 512:(g + 1) * 512], 128, KC, 512))
                for t in range(NT):
                    for c in range(KC):
                        w, wr = wo[c // 4]
                        j = c % 4
                        p_, pr = ps()
                        for kc in range(KC):
                            mm(p_[:], w[:, kc, j * 128:(j + 1) * 128], R1[:, kc * S + t * TT:kc * S + (t + 1) * TT], kc == 0, kc == KC - 1,
                               [wr, ('M', kc, t)], [pr])
                        xl, xlr = tf()
                        em.op('sp', 'dma_start', reads=[('XS', c, t)], writes=[xlr], dma=True, out=xl[:], in_=xs_view(c, t * TT, (t + 1) * TT))
                        xn, xnr = tf()
                        em.op('dve', 'scalar_tensor_tensor', reads=[pr, xlr, 'MODS'], writes=[xnr], out=xn[:], in0=p_[:],
                              scalar=MD(l, 2, c, b), in1=xl[:], op0=ALU.mult, op1=ALU.add)
                        em.op('sp', 'dma_start', reads=[xnr], writes=[('XS', c, t)], dma=True, out=xs_view(c, t * TT, (t + 1) * TT), in_=xn[:])

                norm_phase(8, 3)
                for half in range(2):
                    arena.reset()
                    FA = arena.alloc(11 * S)
                    UAB = [[arena.alloc(S + 4) for _ in range(2)] for _ in range(2)]
                    DGF = [arena.alloc(6 * 128) for _ in range(2)]
                    for jj in range(11):
                        ca = half * 11 + jj
                        cb_ = 22 + ca
                        ua, ub = UAB[jj % 2]
                        uar, ubr = 'UA%d' % (jj % 2), 'UB%d' % (jj % 2)
                        dgf = DGF[jj % 2]
                        dgfr = 'DGF%d' % (jj % 2)
                        em.op('pool', 'memset', writes=[(uar, 'h')], ap=ua[:, 0:4], constant=0.0)
                        em.op('pool', 'memset', writes=[(ubr, 'h')], ap=ub[:, 0:4], constant=0.0)
                        for k in range(3):
                            em.op('dve', 'tensor_scalar', reads=['CM', 'VEC'], writes=[(dgfr, k)], out=dgf[:, k * 128:(k + 1) * 128], in0=cm(CM_ID),
                                  scalar1=V(l, 'fw', ca * 3 + k), scalar2=None, op0=ALU.mult)
                            em.op('dve', 'tensor_scalar', reads=['CM', 'VEC'], writes=[(dgfr, 3 + k)], out=dgf[:, (3 + k) * 128:(4 + k) * 128], in0=cm(CM_ID),
                                  scalar1=V(l, 'fw', cb_ * 3 + k), scalar2=None, op0=ALU.mult)
                        w, wr = ws()
                        w3 = w[:, 0:KC * 256].rearrange("p (k n) -> p k n", n=256)
                        wup_v = wup_d[l].rearrange("(kc p) n -> p kc n", p=128)
                        em.op('pool', 'dma_start', writes=[wr], dma=True, out=w3[:, :, 0:128], in_=wup_v[:, :, ca * 128:(ca + 1) * 128])
                        em.op('pool', 'dma_start', reads=[wr], writes=[wr], dma=True, out=w3[:, :, 128:256], in_=wup_v[:, :, cb_ * 128:(cb_ + 1) * 128])
                        for t in range(NT):
                            pa, par = ps()
                            pb, pbr = ps()
                            for kc in range(KC):
                                ra, rr = hT(kc, t)
                                mm(pa[:], w3[:, kc, 0:128], ra, kc == 0, kc == KC - 1, [wr, rr], [par])
                            for kc in range(KC):
                                ra, rr = hT(kc, t)
                                mm(pb[:], w3[:, kc, 128:256], ra, kc == 0, kc == KC - 1, [wr, rr], [pbr])
                            em.op('act', 'activation', reads=[par], writes=[(uar, t)], out=ua[:, 4 + t * TT:4 + (t + 1) * TT], in_=pa[:], func=AF.Copy)
                            em.op('dve', 'tensor_copy', reads=[pbr], writes=[(ubr, t)], out=ub[:, 4 + t * TT:4 + (t + 1) * TT], in_=pb[:])
                        for t in range(NT):
                            pa, par = ps()
                            pb, pbr = ps()
                            for (pp_, ppr, u, ur, k0) in [(pa, par, ua, uar, 0), (pb, pbr, ub, ubr, 3)]:
                                for k in range(3):
                                    o = 2 + t * TT + k
                                    rds = [(dgfr, k0 + k), (ur, t), (ur, t - 1) if t > 0 else (ur, 'h')]
                                    mm(pp_[:], dgf[:, (k0 + k) * 128:(k0 + k + 1) * 128], u[:, o:o + TT], k == 0, k == 2, rds, [ppr])
                            aa, aar = tb()
                            em.op('act', 'activation', reads=[par, 'VEC'], writes=[aar], out=aa[:], in_=pa[:], func=AF.Silu, bias=V(l, 'fb', ca), scale=1.0)
                            em.op('dve', 'scalar_tensor_tensor', reads=[pbr, aar, 'VEC'], writes=[('FA', jj, t)], out=FA[:, jj * S + t * TT:jj * S + (t + 1) * TT],
                                  in0=pb[:], scalar=V(l, 'fb', cb_), in1=aa[:], op0=ALU.add, op1=ALU.mult)
                    resid_phase(lambda g, half=half: wdn_d[l][half * 1408:(half + 1) * 1408, :].rearrange("(k p) n -> p k n", p=128)[:, :, g * 256:(g + 1) * 256],
                                4, 256, 11, lambda kk, t: (FA[:, kk * S + t * TT:kk * S + (t + 1) * TT], ('FA', kk, t)), 5)

            arena.reset()
            XL = [arena.alloc(D, F32) for _ in range(2)]
            XT_ = [arena.alloc(D, F32) for _ in range(2)]
            for tt in range(16):
                xl = XL[tt % 2]
                xr = 'XL%d' % (tt % 2)
                em.op('sp', 'dma_start', writes=[xr], dma=True, out=xl.rearrange("p (c t) -> p c t", t=128),
                      in_=XS[:, tt * 128:(tt + 1) * 128].rearrange("(c p) t -> p c t", p=128))
                xt = XT_[tt % 2]
                xtr = 'XT%d' % (tt % 2)
                for hh in range(2):
                    p_, pr = ps()
                    for q in range(4):
                        c = hh * 4 + q
                        em.op('pe', 'transpose', reads=[xr, 'IDF'], writes=[pr], out=p_[:, q * 128:(q + 1) * 128],
                              in_=xl[:, c * 128:(c + 1) * 128], identity=IDF[:])
                    em.op('act' if hh == 0 else 'dve', 'activation' if hh == 0 else 'tensor_copy',
                          reads=[pr], writes=[xtr + 'h%d' % hh], out=xt[:, hh * 512:(hh + 1) * 512], in_=p_[:],
                          **({'func': AF.Copy} if hh == 0 else {}))
                em.op('sp', 'dma_start', reads=[xtr + 'h0', xtr + 'h1'], writes=[('OUT', tt)], dma=True,
                      out=out_d[b, tt * 128:(tt + 1) * 128, :], in_=xt)
            em.barrier()

        em.emit(st)
    return nc


def _consts():
    cmat = np.zeros((128, NCM * 128), np.float32)
    i = np.arange(128)
    cmat[:, CM_ID * 128:(CM_ID + 1) * 128] = np.eye(128)
    cmat[:, CM_ONES * 128:(CM_ONES + 1) * 128] = 1.0
    bd = (i[:, None] // 64 == i[None, :] // 64).astype(np.float32)
    cmat[:, CM_BD64 * 128:(CM_BD64 + 1) * 128] = bd
    grp = np.where(i < 64, 0, np.where(i < 96, 1, 2))
    cmat[:, CM_BDA * 128:(CM_BDA + 1) * 128] = (grp[:, None] == grp[None, :]).astype(np.float32)
    pi64 = (i // 64) * 64 + ((i % 64) + 32) % 64
    m = np.zeros((128, 128), np.float32)
    m[pi64, i] = 1.0
    cmat[:, CM_PSW64 * 128:(CM_PSW64 + 1) * 128] = m
    pia = i.copy()
    for j in range(64, 96):
        pia[j] = 64 + ((j - 64) + 16) % 32
    m = np.zeros((128, 128), np.float32)
    m[pia, i] = 1.0
    cmat[:, CM_PSWA * 128:(CM_PSWA + 1) * 128] = m
    cmat[:, CM_TRI * 128:(CM_TRI + 1) * 128] = (i[:, None] <= i[None, :]).astype(np.float32)
    ind = np.zeros((8, 1024), np.float32)
    for n in range(8):
        ind[n, n * 128:(n + 1) * 128] = 1.0
    pcol = np.zeros((128, NPC), np.float32)
    invb = 1.0 / (10000.0 ** (np.arange(0, 64, 2, dtype=np.float32) / 64.0))
    inva = 1.0 / (10000.0 ** (np.arange(0, 32, 2, dtype=np.float32) / 32.0))
    tw = 2.0 * math.pi
    fb = invb[i % 32]
    sgn = np.where((i % 64) < 32, -1.0, 1.0)
    pcol[:, PC_FBS] = sgn * fb / tw
    pcol[:, PC_PBS] = 0.0
    pcol[:, PC_FBC] = fb / tw
    pcol[:, PC_PBC] = 0.25
    fa = np.zeros(128)
    sa = np.zeros(128)
    for j in range(64, 96):
        fa[j] = inva[(j - 64) % 16]
        sa[j] = -1.0 if j < 80 else 1.0
    pcol[:, PC_FAS] = sa * fa / tw
    pcol[:, PC_PAS] = 0.0
    pcol[:, PC_FAC] = fa / tw
    pcol[:, PC_PAC] = 0.25
    pcol[:, PC_QSC] = np.where(i < 64, 1.0 / 64, 1.0 / 32)
    return cmat, ind, np.eye(128, dtype=np.float32), pcol


def _vecpack(inp, NL):
    vec = np.zeros((NL, 128, NV), np.float32)

    def put(l, name, arr2d):
        o = VOFF[name]
        vec[l, :arr2d.shape[0], o:o + arr2d.shape[1]] = arr2d
    for l in range(NL):
        put(l, 'mixg', inp['mix_norm_g'][l].reshape(8, 128).T)
        put(l, 'ffng', inp['ffn_norm_g'][l].reshape(8, 128).T)
        put(l, 'bmod', inp['b_mod'][l].reshape(48, 128).T)
        put(l, 'qlg', inp['mla_q_norm_g'][l].reshape(3, 128).T)
        put(l, 'kvg', inp['mla_kv_norm_g'][l].reshape(2, 128).T)
        put(l, 'gq', np.concatenate([inp['mla_qn_nope_g'][l], inp['mla_qn_rope_g'][l]])[:, None])
        put(l, 'gk', np.concatenate([inp['mla_kn_nope_g'][l], inp['mla_kn_rope_g'][l]])[:, None])
        put(l, 'mq', np.tile(inp['moba_qn_g'][l], 2)[:, None])
        put(l, 'mk', np.tile(inp['moba_kn_g'][l], 2)[:, None])
        put(l, 'cw', inp['conv_dw_w'][l].reshape(31, 4, 128).transpose(2, 1, 0).reshape(128, 124))
        put(l, 'cb', inp['conv_dw_b'][l].reshape(4, 128).T)
        put(l, 'lng', inp['conv_ln_g'][l].reshape(4, 128).T)
        put(l, 'lnb', inp['conv_ln_b'][l].reshape(4, 128).T)
        put(l, 'bpw', inp['b_conv_pw2'][l].reshape(8, 128).T)
        put(l, 'fw', inp['ffn_dw_w'][l].reshape(3, 44, 128).transpose(2, 1, 0).reshape(128, 132))
        put(l, 'fb', inp['ffn_dw_b'][l].reshape(44, 128).T)
    return vec


_WNAMES = ['w_mod', 'w_in', 'w_uq', 'w_ukv', 'w_mla_o', 'w_moba_o', 'w_conv_pw2', 'w_out', 'w_up', 'w_down']


def make_in_maps(inp, ncores, nseq, NL):
    cmat, ind, idf, pcol = _consts()
    vec = _vecpack(inp, NL)
    shared = {n: np.ascontiguousarray(np.asarray(inp[n], np.float32)[:NL]) for n in _WNAMES}
    shared.update(vec=vec, cmat=cmat, ind=ind, identf=idf, pcol=pcol)
    maps = []
    for i in range(ncores):
        b0 = i * nseq
        c = np.asarray(inp['c'], np.float32)[b0:b0 + nseq]
        cT = np.ascontiguousarray(c.reshape(nseq, 8, 128).transpose(2, 1, 0).reshape(128, 8 * nseq))
        m = dict(shared)
        m['x'] = np.ascontiguousarray(np.asarray(inp['x'], np.float32)[b0:b0 + nseq])
        m['cT'] = cT
        m['pos'] = np.ascontiguousarray(np.asarray(inp['positions'], np.int32)[b0:b0 + nseq])
        maps.append(m)
    return maps


def kernel(**inputs):
    nc = build_program(2, 2)
    maps = make_in_maps(inputs, NCORE, 2, 2)
    res = run_bass_kernel_spmd(nc, maps, core_ids=list(range(NCORE)))
    return np.concatenate([np.asarray(r["out"], np.float32) for r in res.results], axis=0)
```

```python
import math
import numpy as np
import concourse.bass as bass
import concourse.mybir as mybir
from contextlib import ExitStack
from concourse.bass_utils import run_bass_kernel_spmd

F32 = mybir.dt.float32
BF16 = mybir.dt.bfloat16
I32 = mybir.dt.int32
AF = mybir.ActivationFunctionType
ALU = mybir.AluOpType
AX = mybir.AxisListType

S = 2048
TT = 512
NT = 4
D = 1024
KC = 8
DIN = 6304
DFF = 2816
EPS = 1e-6
NCORE = 8
BIG = 30000.0
CP_SLACK = 50.0
NACC = 1
NSB = 3
TFX_MLA = 8
TFX_MOBA = 4
PROF = False
CRIT = None


class Em:
    ENGS = ('pe', 'act', 'dve', 'pool', 'sp')
    NDMA = 12

    def __init__(self, nc):
        self.nc = nc
        self.ops = []
        self.lastw = {}
        self.readers = {}
        self.per_eng = {e: [] for e in self.ENGS}
        self.bars = []

    def op(self, eng, name, reads=(), writes=(), dma=False, **kw):
        i = len(self.ops)
        deps = set()
        for r in list(reads) + list(writes):
            w = self.lastw.get(r)
            if w is not None:
                deps.add(w)
        for w in writes:
            for rd in self.readers.get(w, ()):
                deps.add(rd)
        for w in writes:
            self.lastw[w] = i
            self.readers[w] = []
        for r in reads:
            self.readers.setdefault(r, []).append(i)
        deps.discard(i)
        self.ops.append(dict(eng=eng, name=name, kw=kw, deps=deps, dma=dma,
                             pos=len(self.per_eng[eng])))
        self.per_eng[eng].append(i)
        return i

    def barrier(self):
        self.bars.append(len(self.ops))
        self.lastw = {k: v for k, v in self.lastw.items() if isinstance(k, str) and k.startswith('WS')}
        self.readers = {k: v for k, v in self.readers.items() if isinstance(k, str) and k.startswith('WS')}

    @staticmethod
    def _fsz(ap):
        sh = ap.shape
        n = 1
        for d in sh[1:]:
            n *= d
        return n

    def _cost(self, o):
        kw = o['kw']
        if o['dma']:
            ap = kw['out']
            nbytes = self._fsz(ap) * ap.shape[0] * 4
            occ = 1100.0 if o['eng'] == 'pool' else 150.0
            return occ, 2500.0 + nbytes / 100.0
        e = o['eng']
        if e == 'pe':
            nn = self._fsz(kw['rhs']) if 'rhs' in kw else 128
            t = max(64, nn) / 2.4 + 25.0
            return t, t + 200.0
        ap = kw.get('out', kw.get('ap'))
        nn = self._fsz(ap)
        if e == 'act':
            t = 200.0 + nn / 1.3
        elif e == 'dve':
            t = 120.0 + nn / 0.9 * (2.0 if o['name'] == 'scalar_tensor_tensor' else 1.0)
        else:
            t = 300.0 + nn * 2.0 if o['name'] != 'memset' else 100.0 + nn * 0.2
        return t, t + 250.0

    def schedule(self):
        import bisect
        ops = self.ops
        n = len(ops)
        bars = sorted(self.bars)
        seg = [bisect.bisect_right(bars, i) for i in range(n)]
        nseg = (seg[-1] if n else 0) + 1
        users = [[] for _ in range(n)]
        for i, o in enumerate(ops):
            for d in o['deps']:
                users[d].append(i)
        ndl = [len(o['deps']) for o in ops]
        hoist = [o['dma'] and o['eng'] == 'pool' for o in ops]
        ready = [0.0] * n
        sched = [False] * n
        costs = [self._cost(o) for o in ops]
        tail = [0.0] * n
        for i in range(n - 1, -1, -1):
            t = 0.0
            for u in users[i]:
                if seg[u] == seg[i] and tail[u] > t:
                    t = tail[u]
            tail[i] = t + costs[i][1]
        SLACK = CP_SLACK
        segrem = [0] * (nseg + 1)
        for i in range(n):
            segrem[seg[i]] += 1
        segfin = [0.0] * (nseg + 1)
        self._why = {}
        self._st = {}
        self._fin = [0.0] * n
        head = {e: 0 for e in self.ENGS}
        free = {e: 0.0 for e in self.ENGS}
        order = {e: [] for e in self.ENGS}
        WIN = {'pe': 500, 'act': 160, 'dve': 160, 'pool': 80, 'sp': 80}
        segdone_upto = 0
        segfin_cum = [0.0] * (nseg + 2)
        for _ in range(n):
            while segdone_upto < nseg and segrem[segdone_upto] == 0:
                segfin_cum[segdone_upto + 1] = max(segfin_cum[segdone_upto], segfin[segdone_upto])
                segdone_upto += 1
            best = None
            for e in self.ENGS:
                lst = self.per_eng[e]
                h = head[e]
                L = len(lst)
                while h < L and sched[lst[h]]:
                    h += 1
                head[e] = h
                cnt = 0
                k = h
                fe = free[e]
                ebest = None
                W = WIN[e]
                while k < L and cnt < W:
                    i = lst[k]
                    k += 1
                    if sched[i]:
                        continue
                    cnt += 1
                    if ndl[i] > 0:
                        continue
                    st = ready[i]
                    if not hoist[i]:
                        sg = seg[i]
                        if sg > segdone_upto:
                            break
                        if segfin_cum[sg] > st:
                            st = segfin_cum[sg]
                    if fe > st:
                        st = fe
                    if ebest is None or st < ebest[0] - SLACK or (st <= ebest[0] + SLACK and tail[i] > ebest[3]):
                        ebest = (st, e, i, tail[i])
                if ebest is not None and (best is None or ebest[0] < best[0] or (ebest[0] == best[0] and ebest[2] < best[2])):
                    best = ebest
            st, e, i = best[0], best[1], best[2]
            occ, lat = costs[i]
            if getattr(self, 'prof', None) is not None:
                why = ('eng', order[e][-1] if order[e] else None) if (free[e] >= st - 1e-9 and order[e]) else None
                if why is None:
                    bd = None
                    for d in ops[i]['deps']:
                        if self._fin[d] >= st - 1e-9:
                            bd = d
                    why = ('dep', bd)
                self._why[i] = why
                self._st[i] = st
                self._fin[i] = st + lat
            free[e] = st + occ
            f = st + lat
            sched[i] = True
            order[e].append(i)
            for u in users[i]:
                ndl[u] -= 1
                if f > ready[u]:
                    ready[u] = f
            sg = seg[i]
            segrem[sg] -= 1
            if f > segfin[sg]:
                segfin[sg] = f
            if getattr(self, 'prof', None) is not None:
                self.prof.setdefault(sg, {}).setdefault(e, [0.0, 0])
                self.prof[sg][e][0] += occ
                self.prof[sg][e][1] += 1
        self.per_eng = order
        for e in self.ENGS:
            for p, i in enumerate(order[e]):
                ops[i]['pos'] = p
        self.seg = seg
        self.hoist = hoist
        self.est_total = max(free.values())
        if getattr(self, 'prof', None) is not None and getattr(self, 'crit_seg', None) is not None:
            cs = self.crit_seg
            last = max((i for i in range(n) if seg[i] == cs), key=lambda i: self._fin[i])
            i = last
            stats = {}
            chain = []
            while i is not None and seg[i] == cs:
                kind, j = self._why[i]
                key = (kind, ops[i]['eng'], ops[i]['name'])
                t_prev = self._st[j] if (j is not None and j in self._st) else self._st[i]
                stats[key] = stats.get(key, 0.0) + (self._st[i] - t_prev)
                chain.append((i, ops[i]['eng'], ops[i]['name'], kind, round(self._st[i] / 1e3, 2)))
                i = j
            print('critical path seg', cs, 'len', len(chain))
            for k, v in sorted(stats.items(), key=lambda kv: -kv[1])[:14]:
                print('   %-40s %8.1f us' % (str(k), v / 1e3))
            print('   tail of chain:', chain[:40])
        if getattr(self, 'prof', None) is not None:
            prev = 0.0
            for sg in range(nseg):
                end = max(prev, segfin[sg])
                d = end - prev
                if d > 0:
                    print('seg %3d dur %8.1f us ' % (sg, d / 1e3) + ' '.join('%s %3d%%(%d)' % (e, 100 * self.prof.get(sg, {}).get(e, [0, 0])[0] / d, self.prof.get(sg, {}).get(e, [0, 0])[1]) for e in self.ENGS))
                prev = end
        print('[em] ops', n, {e: len(order[e]) for e in self.ENGS}, 'est_ms %.3f' % (self.est_total / 1e6), flush=True)

    def emit(self, stack):
        nc = self.nc
        ops = self.ops
        n = len(ops)
        self.schedule()
        seg, hoist = self.seg, self.hoist
        nseg = (max(seg) if n else 0) + 1
        seg_dmas = [[] for _ in range(nseg + 1)]
        for i, o in enumerate(ops):
            if o['dma'] and not hoist[i]:
                seg_dmas[seg[i]].append(i)
        last_comp = {}
        for e in self.ENGS:
            cur = {}
            for i in self.per_eng[e]:
                if not ops[i]['dma']:
                    cur[seg[i]] = i
            last_comp[e] = cur
        for e in self.ENGS:
            covered = 0
            for i in self.per_eng[e]:
                if hoist[i]:
                    continue
                sg = seg[i]
                if sg > covered:
                    for e2 in self.ENGS:
                        lc = last_comp[e2]
                        for s2 in range(covered, sg):
                            if s2 in lc:
                                ops[i]['deps'].add(lc[s2])
                    for s2 in range(covered, sg):
                        for j in seg_dmas[s2]:
                            ops[i]['deps'].add(j)
                    covered = sg
        need = [False] * n
        for i, o in enumerate(ops):
            for d in o['deps']:
                od = ops[d]
                if od['dma'] or od['eng'] != o['eng']:
                    need[d] = True
                elif od['eng'] == 'pe':
                    pass
                elif o['pos'] - od['pos'] <= 3:
                    need[d] = True
        sems = {e: stack.enter_context(nc.semaphore('s_' + e)) for e in self.ENGS}
        dsems = {e: [stack.enter_context(nc.semaphore('d_%s_%d' % (e, k))) for k in range(self.NDMA)]
                 for e in ('sp', 'pool', 'act')}
        cnt = {e: 0 for e in self.ENGS}
        dcnt = {e: [0] * self.NDMA for e in dsems}
        dnum = {e: 0 for e in dsems}
        dprev = {}
        for e in self.ENGS:
            for i in self.per_eng[e]:
                o = ops[i]
                if o['dma']:
                    k = dnum[e] % self.NDMA
                    dnum[e] += 1
                    dcnt[e][k] += 16
                    o['sig'] = (dsems[e][k], dcnt[e][k])
                    pk = (e, k)
                    if pk in dprev:
                        o['deps'].add(dprev[pk])
                    dprev[pk] = i
                elif need[i]:
                    cnt[e] += 1
                    o['sig'] = (sems[e], cnt[e])
                else:
                    o['sig'] = None
        block = stack.enter_context(nc.Block())
        em = self

        def run(ename):
            def body(engine):
                seen = {}
                for i in em.per_eng[ename]:
                    o = ops[i]
                    waits = {}
                    for d in o['deps']:
                        od = ops[d]
                        if od['sig'] is None:
                            continue
                        if (not od['dma']) and od['eng'] == ename and ename == 'pe':
                            continue
                        s, v = od['sig']
                        key = id(s)
                        if key not in waits or waits[key][1] < v:
                            waits[key] = (s, v)
                    for key, (s, v) in waits.items():
                        if seen.get(key, 0) >= v:
                            continue
                        seen[key] = v
                        engine.wait_ge(s, v)
                    try:
                        ins = getattr(engine, o['name'])(**o['kw'])
                    except Exception:
                        print('EMIT FAIL op', i, ename, o['name'], {k: (getattr(v, 'shape', v)) for k, v in o['kw'].items()})
                        raise
                    if o['sig'] is not None:
                        ins.then_inc(o['sig'][0], 16 if o['dma'] else 1)
                if ename in dsems:
                    for k in range(em.NDMA):
                        if dcnt[ename][k] > 0:
                            engine.wait_ge(dsems[ename][k], dcnt[ename][k])
            return body

        block.tensor(run('pe'))
        block.scalar(run('act'))
        block.vector(run('dve'))
        block.gpsimd(run('pool'))
        block.sync(run('sp'))


VOFF = {}
_o = 0
for _n, _w in [('mixg', 8), ('ffng', 8), ('bmod', 48), ('qlg', 3), ('kvg', 2), ('gq', 1), ('gk', 1),
               ('mq', 1), ('mk', 1), ('cw', 124), ('cb', 4), ('lng', 4), ('lnb', 4), ('bpw', 8),
               ('fw', 132), ('fb', 44)]:
    VOFF[_n] = _o
    _o += _w
NV = _o
CM_ID, CM_ONES, CM_BD64, CM_BDA, CM_PSW64, CM_PSWA, CM_TRI = range(7)
NCM = 7
PC_FBS, PC_PBS, PC_FBC, PC_PBC, PC_FAS, PC_PAS, PC_FAC, PC_PAC, PC_QSC = range(9)
NPC = 16


def build_program(NSEQ=2, NL=2, dbg=None):
    nc = bass.Bass("TRN2", target_bir_lowering=False)

    def din(name, shape, dt=F32):
        return nc.dram_tensor(name, list(shape), dt, kind="ExternalInput").ap()

    x_d = din("x", [NSEQ, S, D])
    cT_d = din("cT", [128, KC * NSEQ])
    pos_d = din("pos", [NSEQ, S], I32)
    wmod_d = din("w_mod", [NL, D, 6 * D])
    win_d = din("w_in", [NL, D, DIN])
    wuq_d = din("w_uq", [NL, 384, 768])
    wukv_d = din("w_ukv", [NL, 256, 1024])
    wmo_d = din("w_mla_o", [NL, 512, D])
    wbo_d = din("w_moba_o", [NL, 512, D])
    wpw_d = din("w_conv_pw2", [NL, 512, D])
    wout_d = din("w_out", [NL, D, D])
    wup_d = din("w_up", [NL, D, 2 * DFF])
    wdn_d = din("w_down", [NL, DFF, D])
    vec_d = din("vec", [NL, 128, NV])
    cmat_d = din("cmat", [128, NCM * 128])
    ind_d = din("ind", [8, 1024])
    idf_d = din("identf", [128, 128])
    pcol_d = din("pcol", [128, NPC])
    out_d = nc.dram_tensor("out", [NSEQ, S, D], F32, kind="ExternalOutput").ap()
    XS = nc.dram_tensor("xs_scr", [D, S], F32).ap()
    OD = nc.dram_tensor("od_scr", [128, 4 * S], BF16).ap()
    dbg_d = {}
    if dbg:
        for k, shp in dbg.items():
            dbg_d[k] = nc.dram_tensor("dbg_" + k, list(shp), F32, kind="ExternalOutput").ap()

    st = ExitStack()
    with st:
        def sb(name, shape, dt):
            return st.enter_context(nc.sbuf_tensor(name, list(shape), dt))

        VEC = sb("VEC", [128, NL * NV], F32)
        MODS = sb("MODS", [128, NL * 48 * NSEQ], F32)
        AB = sb("AB", [128, 16], F32)
        IDF = sb("IDF", [128, 128], F32)
        PCOL = sb("PCOL", [128, NPC], F32)
        CM = sb("CM", [128, NCM * 128], BF16)
        IND = sb("IND", [8, 1024], BF16)
        CACT = sb("CACT", [128, KC * NSEQ], BF16)
        CTF = sb("CTF", [128, KC * NSEQ], F32)
        NW = 3
        WS = [sb("WS%d" % i, [128, 4096], BF16) for i in range(NW)]
        NTF = 6
        TF = [sb("TF%d" % i, [128, 512], F32) for i in range(NTF)]
        NTB = 4
        TB = [sb("TB%d" % i, [128, 512], BF16) for i in range(NTB)]
        NTP = 4
        TP = [sb("TP%d" % i, [128, 512], BF16) for i in range(NTP)]
        NTL = 3
        TL = [sb("TL%d" % i, [128, 512], F32) for i in range(NTL)]
        HT = sb("HT", [128, KC * S], BF16)
        R1 = sb("R1", [128, KC * S], BF16)
        TAB = sb("TAB", [128, 4 * S], BF16)
        SM = sb("SM", [128, 64], F32)
        ARN = 33000
        AR = sb("AR", [128, ARN], BF16)
        PS = [st.enter_context(nc.psum_tensor("PS%d" % i, [128, 512], F32)) for i in range(8)]

        em = Em(nc)
        em.prof = {} if PROF else None
        em.crit_seg = CRIT
        ctr = {'w': 0, 'tf': 0, 'tb': 0, 'ps': 0, 'acc': 0, 'tl': 0, 'psg': 0, 'pss': 0, 'tp': 0}
        cfg = {'split': False}

        def cm(i, r0=0, r1=128, c0=0, c1=128):
            return CM[r0:r1, i * 128 + c0:i * 128 + c1]

        def nxt(kind, n):
            ctr[kind] = (ctr[kind] + 1) % n
            return ctr[kind]

        tfx = []

        def tf():
            i = nxt('tf', NTF + len(tfx))
            if i < NTF:
                return TF[i], 'TF%d' % i
            return tfx[i - NTF], 'TFX%d' % (i - NTF)

        def tl():
            i = nxt('tl', NTL)
            return TL[i], 'TL%d' % i

        def tb():
            i = nxt('tb', NTB)
            return TB[i], 'TB%d' % i

        def ps():
            if cfg['split']:
                i = NSB + nxt('psg', 8 - NACC - NSB)
            else:
                i = nxt('ps', 8 - NACC)
            return PS[i], 'PS%d' % i

        def sbank():
            i = nxt('pss', NSB)
            return PS[i], 'PS%d' % i

        def ptile():
            i = nxt('tp', NTP)
            return TP[i], 'TP%d' % i

        def acc():
            i = 8 - NACC + nxt('acc', NACC)
            return PS[i], 'PS%d' % i

        def ws():
            i = nxt('w', NW)
            return WS[i], 'WS%d' % i

        class Arena:
            def __init__(self):
                self.off = 0

            def reset(self):
                self.off = 0

            def alloc(self, ncols, dt=BF16):
                n = ncols * (2 if dt in (F32, I32) else 1)
                n = (n + 3) // 4 * 4
                a = AR[:, self.off:self.off + n]
                self.off += n
                assert self.off <= ARN, self.off
                if dt == F32:
                    a = a.bitcast(F32)
                elif dt == I32:
                    a = a.bitcast(I32)
                return a
        arena = Arena()

        def V(l, name, c=0, n=1, r0=0, r1=128):
            o = l * NV + VOFF[name] + c
            return VEC[r0:r1, o:o + n]

        def MD(l, k, c, b):
            o = (l * 48 + k * 8 + c) * NSEQ + b
            return MODS[:, o:o + 1]

        def wload(view, p, kcn, ncols, extra=None):
            w, r = ws()
            dst = w[0:p, 0:kcn * ncols].rearrange("p (k n) -> p k n", n=ncols)
            em.op('pool', 'dma_start', writes=[r], dma=True, out=dst, in_=view)
            return dst, r

        def mm(out, lhsT, rhs, start, stop, reads, writes):
            em.op('pe', 'matmul', reads=reads, writes=writes, out=out, lhsT=lhsT, rhs=rhs,
                  start=start, stop=stop)

        def rstd_from(psum_ap, rows, scale, pres, long=False):
            t1, r1 = tf()
            em.op('act', 'activation', reads=[pres], writes=[r1], out=t1[rows[0]:rows[1], :],
                  in_=psum_ap, func=AF.Ln, scale=scale, bias=EPS)
            t2, r2 = tl() if long else tf()
            em.op('act', 'activation', reads=[r1], writes=[r2], out=t2[rows[0]:rows[1], :],
                  in_=t1[rows[0]:rows[1], :], func=AF.Exp, scale=-0.5)
            return t2, r2

        def dbg_dump(name, ap_sb, res, rows=128):
            if name in dbg_d:
                t, r = tf()
                em.op('dve', 'tensor_copy', reads=[res], writes=[r], out=t[0:rows, :], in_=ap_sb)
                em.op('sp', 'dma_start', reads=[r], writes=['dbg_' + name], dma=True,
                      out=dbg_d[name][0:rows, :], in_=t[0:rows, :])

        em.op('sp', 'dma_start', writes=['VEC'], dma=True,
              out=VEC[:].rearrange("p (l n) -> p l n", n=NV), in_=vec_d.rearrange("l p n -> p l n"))
        em.op('sp', 'dma_start', writes=['IDF'], dma=True, out=IDF[:], in_=idf_d)
        em.op('sp', 'dma_start', writes=['PCOL'], dma=True, out=PCOL[:], in_=pcol_d)
        em.op('sp', 'dma_start', writes=['CTF'], dma=True, out=CTF[:], in_=cT_d)
        em.op('pool', 'dma_start', writes=['CM'], dma=True, out=CM[:], in_=cmat_d)
        em.op('pool', 'dma_start', writes=['IND'], dma=True, out=IND[:], in_=ind_d)
        em.op('act', 'activation', reads=['CTF'], writes=['CACT'], out=CACT[:], in_=CTF[:], func=AF.Silu)
        for l in range(NL):
            for g in range(12):
                view = wmod_d[l].rearrange("(kc p) n -> p kc n", p=128)[:, :, g * 512:(g + 1) * 512]
                w, wr = wload(view, 128, KC, 512)
                for j in range(4):
                    ch = g * 4 + j
                    p_, pr = ps()
                    for kc in range(KC):
                        mm(p_[:, 0:NSEQ], w[:, kc, j * 128:(j + 1) * 128], CACT[:, kc * NSEQ:(kc + 1) * NSEQ],
                           kc == 0, kc == KC - 1, [wr, 'CACT'], [pr])
                    o = (l * 48 + ch) * NSEQ
                    em.op('dve', 'tensor_scalar', reads=[pr, 'VEC'], writes=['MODS'], out=MODS[:, o:o + NSEQ],
                          in0=p_[:, 0:NSEQ], scalar1=V(l, 'bmod', ch), scalar2=None, op0=ALU.add)

        def xs_view(c, c0, c1):
            return XS[c * 128:(c + 1) * 128, c0:c1]

        for b in range(NSEQ):
            arena.reset()
            XL = [arena.alloc(D, F32) for _ in range(2)]
            XT_ = [arena.alloc(D, F32) for _ in range(2)]
            for tt in range(16):
                xl = XL[tt % 2]
                xr = 'XL%d' % (tt % 2)
                em.op('sp', 'dma_start', writes=[xr], dma=True, out=xl, in_=x_d[b, tt * 128:(tt + 1) * 128, :])
                xt = XT_[tt % 2]
                xtr = 'XT%d' % (tt % 2)
                for hh in range(2):
                    p_, pr = ps()
                    for q in range(4):
                        c = hh * 4 + q
                        em.op('pe', 'transpose', reads=[xr, 'IDF'], writes=[pr], out=p_[:, q * 128:(q + 1) * 128],
                              in_=xl[:, c * 128:(c + 1) * 128], identity=IDF[:])
                    em.op('act' if hh == 0 else 'dve', 'activation' if hh == 0 else 'tensor_copy',
                          reads=[pr], writes=[xtr + 'h%d' % hh], out=xt[:, hh * 512:(hh + 1) * 512], in_=p_[:],
                          **({'func': AF.Copy} if hh == 0 else {}))
                em.op('sp', 'dma_start', reads=[xtr + 'h0', xtr + 'h1'],
                      writes=[('XSld', tt)], dma=True,
                      out=XS[:, tt * 128:(tt + 1) * 128].rearrange("(c p) t -> p c t", p=128),
                      in_=xt.rearrange("p (c t) -> p c t", t=128))

            POSI = arena.alloc(S, I32)
            POSF = arena.alloc(S, F32)
            VV = arena.alloc(S, F32)
            KI = arena.alloc(S, I32)
            KF = arena.alloc(S, F32)
            M1 = arena.alloc(S, F32)
            em.op('sp', 'dma_start', writes=['POSI'], dma=True, out=POSI,
                  in_=pos_d[b:b + 1, :].partition_broadcast(128).rearrange("p o s -> p (o s)"))
            em.op('dve', 'tensor_copy', reads=['POSI'], writes=['POSF'], out=POSF, in_=POSI)
            for ti, (fc, pc) in enumerate([(PC_FBC, PC_PBC), (PC_FBS, PC_PBS), (PC_FAC, PC_PAC), (PC_FAS, PC_PAS)]):
                em.op('dve', 'tensor_scalar', reads=['POSF', 'PCOL'], writes=['VV'], out=VV, in0=POSF,
                      scalar1=PCOL[:, fc:fc + 1], scalar2=PCOL[:, pc:pc + 1], op0=ALU.mult, op1=ALU.add)
                em.op('dve', 'tensor_copy', reads=['VV'], writes=['KI'], out=KI, in_=VV)
                em.op('dve', 'tensor_copy', reads=['KI'], writes=['KF'], out=KF, in_=KI)
                em.op('dve', 'tensor_tensor', reads=['VV', 'KF'], writes=['VV'], out=VV, in0=VV, in1=KF, op=ALU.subtract)
                em.op('dve', 'tensor_scalar', reads=['VV'], writes=['M1'], out=M1, in0=VV, scalar1=0.5, scalar2=None, op0=ALU.is_gt)
                em.op('dve', 'tensor_tensor', reads=['VV', 'M1'], writes=['VV'], out=VV, in0=VV, in1=M1, op=ALU.subtract)
                em.op('dve', 'tensor_scalar', reads=['VV'], writes=['M1'], out=M1, in0=VV, scalar1=-0.5, scalar2=None, op0=ALU.is_lt)
                em.op('dve', 'tensor_tensor', reads=['VV', 'M1'], writes=['VV'], out=VV, in0=VV, in1=M1, op=ALU.add)
                em.op('act', 'activation', reads=['VV'], writes=[('TAB', ti)], out=TAB[:, ti * S:(ti + 1) * S],
                      in_=VV, func=AF.Sin, scale=6.283185)
            em.barrier()

            def tabv(ti, r0, r1, c0, c1):
                return TAB[r0:r1, ti * S + c0:ti * S + c1]

            for l in range(NL):
                def md8(k):
                    o = (l * 48 + k * 8) * NSEQ + b
                    return MODS[:, o:o + 8 * NSEQ].rearrange("p (c s) -> p c s", s=NSEQ)[:, :, 0:1].rearrange("p c s -> p (c s)")
                em.op('dve', 'scalar_tensor_tensor', reads=['MODS', 'VEC'], writes=['AB'], out=AB[:, 0:8], in0=md8(1),
                      scalar=1.0, in1=V(l, 'mixg', 0, 8), op0=ALU.add, op1=ALU.mult)
                em.op('dve', 'scalar_tensor_tensor', reads=['MODS', 'VEC'], writes=['AB'], out=AB[:, 8:16], in0=md8(4),
                      scalar=1.0, in1=V(l, 'ffng', 0, 8), op0=ALU.add, op1=ALU.mult)

                def norm_phase(acol, shk):
                    arena.reset()
                    XLn = [arena.alloc(KC * TT, F32) for _ in range(2)]
                    for t in range(NT):
                        xl = XLn[t % 2]
                        xr = 'XLn%d' % (t % 2)
                        em.op('sp', 'dma_start', reads=[('XS', c, t) for c in range(KC)], writes=[xr], dma=True,
                              out=xl.rearrange("p (c t) -> p c t", t=TT),
                              in_=XS[:, t * TT:(t + 1) * TT].rearrange("(c p) t -> p c t", p=128))
                        pss, pssr = ps()
                        for c in range(KC):
                            sq, sqr = tb()
                            em.op('act', 'activation', reads=[xr], writes=[sqr], out=sq[:], in_=xl[:, c * TT:(c + 1) * TT], func=AF.Square)
                            mm(pss[:], cm(CM_ONES), sq[:], c == 0, c == KC - 1, [sqr, 'CM'], [pssr])
                        rs, rsr = rstd_from(pss[:], (0, 128), 1.0 / D, pssr, long=True)
                        for c in range(KC):
                            t1, t1r = tf()
                            em.op('dve', 'scalar_tensor_tensor', reads=[xr, rsr, 'AB'], writes=[t1r], out=t1[:],
                                  in0=xl[:, c * TT:(c + 1) * TT], scalar=AB[:, acol + c:acol + c + 1], in1=rs[:],
                                  op0=ALU.mult, op1=ALU.mult)
                            em.op('act', 'activation', reads=[t1r, 'MODS'], writes=[('HT', c, t)],
                                  out=HT[:, c * S + t * TT:c * S + (t + 1) * TT], in_=t1[:], func=AF.Identity,
                                  bias=MD(l, shk, c, b), scale=1.0)
                    em.barrier()

                def hT(kc, t):
                    return HT[:, kc * S + t * TT:kc * S + (t + 1) * TT], ('HT', kc, t)

                def dense(view, p, kcn, ncols, rhs_fn, out_fn, msub=128, tiles=range(NT)):
                    w, wr = wload(view, p, kcn, ncols)
                    for j in range((ncols + msub - 1) // msub):
                        m0, m1 = j * msub, min(ncols, (j + 1) * msub)
                        for t in tiles:
                            p_, pr = ps()
                            for kc in range(kcn):
                                ra, rr = rhs_fn(kc, t)
                                mm(p_[0:m1 - m0, :], w[:, kc, m0:m1], ra, kc == 0, kc == kcn - 1, [wr, rr], [pr])
                            out_fn(j, t, p_, pr)

                def win_view(c0, c1):
                    return win_d[l].rearrange("(kc p) n -> p kc n", p=128)[:, :, c0:c1]

                evac_ctr = [0]

                def evac_copy(dst, src, reads, writes):
                    evac_ctr[0] += 1
                    if evac_ctr[0] % 2:
                        em.op('act', 'activation', reads=reads, writes=writes, out=dst, in_=src, func=AF.Copy)
                    else:
                        em.op('dve', 'tensor_copy', reads=reads, writes=writes, out=dst, in_=src)

                def finalize_o(accp, accr, h, qt):
                    os_, osr = tf()
                    em.op('dve', 'tensor_copy', reads=[accr], writes=[osr], out=os_[0:65, :], in_=accp[0:65, :])
                    l1, l1r = tf()
                    em.op('act', 'activation', reads=[osr], writes=[l1r], out=l1[64:65, :], in_=os_[64:65, :], func=AF.Ln)
                    rb, rbr = tb()
                    em.op('act', 'activation', reads=[l1r], writes=[rbr], out=rb[64:65, :], in_=l1[64:65, :], func=AF.Exp, scale=-1.0)
                    p_, pr = ps()
                    mm(p_[0:64, :], cm(CM_ONES, 64, 65, 0, 64), rb[64:65, :], True, True, [rbr, 'CM'], [pr])
                    ot, otr = tb()
                    em.op('dve', 'tensor_tensor', reads=[osr, pr], writes=[otr], out=ot[0:64, :], in0=os_[0:64, :], in1=p_[0:64, :], op=ALU.mult)
                    tg = getattr(attention, 'tag', None)
                    if tg and h == 0 and qt == 2:
                        dbg_dump(tg, ot[0:64, :], otr, 64)
                    em.op('sp', 'dma_start', reads=[otr], writes=[('OD', h, qt)], dma=True,
                          out=OD[(h % 2) * 64:(h % 2) * 64 + 64, (h // 2) * S + qt * TT:(h // 2) * S + (qt + 1) * TT], in_=ot[0:64, :])

                def attention(h, kf, qf, kdim, vap, scale, bias=None):
                    for qt in range(NT):
                        accp, accr = acc()
                        nkt = 4 * qt + 4
                        for kt in range(nkt):
                            j0 = max(0, kt - 4 * qt) * 128
                            p_, pr = sbank()
                            ka, kr = kf(kt)
                            qa, qr = qf(qt * TT + j0, (qt + 1) * TT)
                            mm(p_[:, j0:TT], ka, qa, True, bias is None, list(kr) + [qr], [pr])
                            if bias is not None:
                                n = kt // 2
                                mm(p_[:, j0:TT], IND[0:8, n * 128:(n + 1) * 128], bias[0][0:8, qt * TT + j0:(qt + 1) * TT],
                                   False, True, ['IND'] + [(bias[1], x) for x in range(qt * 4, qt * 4 + 4)], [pr])
                            pt, ptr = ptile()
                            em.op('act', 'activation', reads=[pr], writes=[ptr], out=pt[:, j0:TT], in_=p_[:, j0:TT], func=AF.Exp, scale=scale)
                            if kt >= 4 * qt:
                                em.op('dve', 'tensor_tensor', reads=[ptr, 'CM'], writes=[ptr], out=pt[:, j0:j0 + 128],
                                      in0=pt[:, j0:j0 + 128], in1=cm(CM_TRI), op=ALU.mult)
                            va, vr = vap(kt)
                            mm(accp[0:65, j0:TT], va, pt[:, j0:TT], kt == 0, kt == nkt - 1, [vr, ptr], [accr])
                        finalize_o(accp, accr, h, qt)

                def merge_phase(bi, first):
                    arena.reset()
                    if bi < 2:
                        OA = arena.alloc(4 * S)
                        for c4 in range(4):
                            for qt in range(NT):
                                em.op('sp', 'dma_start', reads=[('OD', 2 * c4, qt), ('OD', 2 * c4 + 1, qt)], writes=[('OA', c4, qt)], dma=True,
                                      out=OA[:, c4 * S + qt * TT:c4 * S + (qt + 1) * TT],
                                      in_=OD[:, c4 * S + qt * TT:c4 * S + (qt + 1) * TT])
                        wsrc = (wmo_d, wbo_d)[bi][l].rearrange("(k p) n -> p k n", p=128)
                        pp, pk = 128, 4

                        def prhs(kk, t):
                            return OA[:, kk * S + t * TT:kk * S + (t + 1) * TT], ('OA', kk, t)
                    else:
                        wsrc = wpw_d[l].rearrange("(k p) n -> p k n", p=128)
                        pp, pk = 128, 4
                        UCb = merge_phase.UC

                        def prhs(kk, t):
                            return UCb[:, kk * S + t * TT:kk * S + (t + 1) * TT], ('UC', kk, t)
                    for g in range(2):
                        wg, wgr = wload(win_view(3232 + bi * 1024 + g * 512, 3232 + bi * 1024 + (g + 1) * 512), 128, KC, 512)
                        wp, wpr = wload(wsrc[:, :, g * 512:(g + 1) * 512], pp, pk, 512)
                        for j in range(4):
                            c = g * 4 + j
                            for t in range(NT):
                                pg, pgr = ps()
                                for kc in range(KC):
                                    ra, rr = hT(kc, t)
                                    mm(pg[:], wg[:, kc, j * 128:(j + 1) * 128], ra, kc == 0, kc == KC - 1, [wgr, rr], [pgr])
                                pq, pqr = ps()
                                for kk in range(pk):
                                    ra, rr = prhs(kk, t)
                                    mm(pq[:], wp[0:pp, kk, j * 128:(j + 1) * 128], ra, kk == 0, kk == pk - 1, [wpr, rr], [pqr])
                                sg, sgr = tf()
                                em.op('act', 'activation', reads=[pgr], writes=[sgr], out=sg[:], in_=pg[:], func=AF.Sigmoid)
                                mdst = R1[:, c * S + t * TT:c * S + (t + 1) * TT]
                                if first:
                                    em.op('dve', 'tensor_tensor', reads=[pqr, sgr], writes=[('M', c, t)], out=mdst, in0=pq[:], in1=sg[:], op=ALU.mult)
                                else:
                                    t1, t1r = tf()
                                    if bi == 2:
                                        em.op('dve', 'scalar_tensor_tensor', reads=[pqr, sgr, 'VEC'], writes=[t1r], out=t1[:], in0=pq[:],
                                              scalar=V(l, 'bpw', c), in1=sg[:], op0=ALU.add, op1=ALU.mult)
                                    else:
                                        em.op('dve', 'tensor_tensor', reads=[pqr, sgr], writes=[t1r], out=t1[:], in0=pq[:], in1=sg[:], op=ALU.mult)
                                    em.op('pool', 'tensor_tensor', reads=[t1r, ('M', c, t)], writes=[('M', c, t)], out=mdst, in0=mdst, in1=t1[:], op=ALU.add)
                    em.barrier()

                def resid_phase(wview_fn, ngroups, gcols, kcn, rhs_fn, gk):
                    for g in range(ngroups):
                        w, wr = wload(wview_fn(g), 128, kcn, gcols)
                        for j in range(gcols // 128):
                            c = g * (gcols // 128) + j
                            for t in range(NT):
                                p_, pr = ps()
                                for kc in range(kcn):
                                    ra, rr = rhs_fn(kc, t)
                                    mm(p_[:], w[:, kc, j * 128:(j + 1) * 128], ra, kc == 0, kc == kcn - 1, [wr, rr], [pr])
                                xl, xlr = tf()
                                em.op('sp', 'dma_start', reads=[('XS', c, t)], writes=[xlr], dma=True, out=xl[:], in_=xs_view(c, t * TT, (t + 1) * TT))
                                xn, xnr = tf()
                                em.op('dve', 'scalar_tensor_tensor', reads=[pr, xlr, 'MODS'], writes=[xnr], out=xn[:], in0=p_[:],
                                      scalar=MD(l, gk, c, b), in1=xl[:], op0=ALU.mult, op1=ALU.add)
                                em.op('sp', 'dma_start', reads=[xnr], writes=[('XS', c, t)], dma=True, out=xs_view(c, t * TT, (t + 1) * TT), in_=xn[:])
                    em.barrier()

                norm_phase(0, 0)
                D0 = bool(dbg) and l == 0 and b == 0
                if D0:
                    dbg_dump('ht', HT[:, 0:512], ('HT', 0, 0))
                    dbg_dump('ht1', HT[:, S:S + 512], ('HT', 1, 0))
                    dbg_dump('ht7', HT[:, 7 * S + 1536:8 * S], ('HT', 7, 3))

                arena.reset()
                del tfx[:]
                tfx.extend([R1[:, i * 1024:(i + 1) * 1024].bitcast(F32) for i in range(TFX_MLA)])
                LAT = arena.alloc(6 * S)
                KPE = arena.alloc(S)
                VA = arena.alloc(16 * 8 * 65)
                QH = [arena.alloc(S) for _ in range(2)]
                KH = [arena.alloc(S) for _ in range(2)]

                def lat(c, t, r0=0, r1=128):
                    return LAT[r0:r1, c * S + t * TT:c * S + (t + 1) * TT]

                def lat_out(cbase, rows=128):
                    def f(j, t, p_, pr):
                        evac_copy(lat(cbase + j, t, 0, rows), p_[0:rows, :], [pr], [('LAT', cbase + j, t)])
                        if D0 and cbase + j == 0 and t == 0:
                            dbg_dump('zraw', lat(0, 0), ('LAT', 0, 0))
                    return f
                dense(win_view(0, 512), 128, KC, 512, hT, lat_out(0))
                dense(win_view(512, 640), 128, KC, 128, hT, lat_out(4))
                dense(win_view(576, 672), 128, KC, 96, hT, lat_out(5, 96), msub=96)
                for (c0, ncn, gname) in [(0, 3, 'qlg'), (3, 2, 'kvg')]:
                    for t in range(NT):
                        pss, pssr = ps()
                        for c in range(ncn):
                            sq, sqr = tb()
                            em.op('act', 'activation', reads=[('LAT', c0 + c, t)], writes=[sqr], out=sq[:], in_=lat(c0 + c, t), func=AF.Square)
                            mm(pss[:], cm(CM_ONES), sq[:], c == 0, c == ncn - 1, [sqr, 'CM'], [pssr])
                        rs, rsr = rstd_from(pss[:], (0, 128), 1.0 / (ncn * 128), pssr)
                        for c in range(ncn):
                            em.op('dve', 'scalar_tensor_tensor', reads=[('LAT', c0 + c, t), rsr, 'VEC'], writes=[('LAT', c0 + c, t)],
                                  out=lat(c0 + c, t), in0=lat(c0 + c, t), scalar=V(l, gname, c), in1=rs[:], op0=ALU.mult, op1=ALU.mult)

                def head_norm_rope(src_ps, src_res, rows, bdm, gcol, dst, dst_res, t, rope):
                    r0, r1 = rows
                    sq, sqr = tb()
                    em.op('act', 'activation', reads=[src_res], writes=[sqr], out=sq[r0:r1, :], in_=src_ps, func=AF.Square)
                    p2, p2r = ps()
                    mm(p2[r0:r1, :], cm(bdm, r0, r1, r0, r1), sq[r0:r1, :], True, True, [sqr, 'CM'], [p2r])
                    t1, t1r = tf()
                    em.op('act', 'activation', reads=[p2r, 'PCOL'], writes=[t1r], out=t1[r0:r1, :], in_=p2[r0:r1, :], func=AF.Ln,
                          scale=PCOL[r0:r1, PC_QSC:PC_QSC + 1] if bdm == CM_BDA else 1.0 / 64, bias=EPS)
                    rs, rsr = tf()
                    em.op('act', 'activation', reads=[t1r], writes=[rsr], out=rs[r0:r1, :], in_=t1[r0:r1, :], func=AF.Exp, scale=-0.5)
                    if not rope:
                        em.op('dve', 'scalar_tensor_tensor', reads=[src_res, rsr, 'VEC'], writes=[dst_res], out=dst, in0=src_ps,
                              scalar=gcol, in1=rs[r0:r1, :], op0=ALU.mult, op1=ALU.mult)
                        return
                    ctab, stab, psw = rope
                    qn, qnr = tb()
                    em.op('dve', 'scalar_tensor_tensor', reads=[src_res, rsr, 'VEC'], writes=[qnr], out=qn[r0:r1, :], in0=src_ps,
                          scalar=gcol, in1=rs[r0:r1, :], op0=ALU.mult, op1=ALU.mult)
                    p3, p3r = ps()
                    mm(p3[r0:r1, :], cm(psw, r0, r1, r0, r1), qn[r0:r1, :], True, True, [qnr, 'CM'], [p3r])
                    a1, a1r = tf()
                    em.op('dve', 'tensor_tensor', reads=[qnr, ('TAB', ctab)], writes=[a1r], out=a1[r0:r1, :], in0=qn[r0:r1, :],
                          in1=tabv(ctab, r0, r1, t * TT, (t + 1) * TT), op=ALU.mult)
                    a2, a2r = tf()
                    em.op('dve', 'tensor_tensor', reads=[p3r, ('TAB', stab)], writes=[a2r], out=a2[r0:r1, :], in0=p3[r0:r1, :],
                          in1=tabv(stab, r0, r1, t * TT, (t + 1) * TT), op=ALU.mult)
                    em.op('pool', 'tensor_tensor', reads=[a1r, a2r], writes=[dst_res], out=dst, in0=a1[r0:r1, :], in1=a2[r0:r1, :], op=ALU.add)

                if D0:
                    dbg_dump('qn', lat(0, 0), ('LAT', 0, 0))
                    dbg_dump('kvn', lat(3, 0), ('LAT', 3, 0))
                for t in range(NT):
                    head_norm_rope(lat(5, t, 64, 96), ('LAT', 5, t), (64, 96), CM_BDA, V(l, 'gk', 0, 1, 64, 96),
                                   KPE[64:96, t * TT:(t + 1) * TT], ('KPE', t), t, (2, 3, CM_PSWA))
                if D0:
                    dbg_dump('kpe', KPE[64:96, 0:512], ('KPE', 0), 32)
                em.op('pool', 'memset', writes=['VAones'], ap=VA.rearrange("p (n d) -> p n d", d=65)[:, :, 64:65], constant=1.0)
                wkv, wkvr = wload(wukv_d[l].rearrange("(kc p) n -> p kc n", p=128), 128, 2, 1024)
                for tt in range(16):
                    p_, pr = ps()
                    for kc in range(2):
                        mm(p_[:].rearrange("p (h d) -> p h d", d=64), LAT[:, (3 + kc) * S + tt * 128:(3 + kc) * S + (tt + 1) * 128],
                           wkv[:, kc, :].rearrange("p (h d) -> p h d", d=128)[:, :, 64:128], kc == 0, kc == 1,
                           [wkvr, ('LAT', 3 + kc, tt // 4)], [pr])
                    evac_copy(VA[:, tt * 520:(tt + 1) * 520].rearrange("p (h d) -> p h d", d=65)[:, :, 0:64],
                              p_[:].rearrange("p (h d) -> p h d", d=64), [pr, 'VAones'], [('VA', tt)])
                wq, wqr = wload(wuq_d[l].rearrange("(kc p) n -> p kc n", p=128), 128, 3, 768)
                cfg['split'] = True
                for h in range(8):
                    qh, kh = QH[h % 2], KH[h % 2]
                    qres, kres = 'QH%d' % (h % 2), 'KH%d' % (h % 2)
                    for t in range(NT):
                        p_, pr = ps()
                        for kc in range(3):
                            mm(p_[0:96, :], wq[:, kc, h * 96:(h + 1) * 96], lat(kc, t), kc == 0, kc == 2, [wqr, ('LAT', kc, t)], [pr])
                        head_norm_rope(p_[0:96, :], pr, (0, 96), CM_BDA, V(l, 'gq', 0, 1, 0, 96), qh[0:96, t * TT:(t + 1) * TT], (qres, t), t,
                                       (2, 3, CM_PSWA))
                        p_, pr = ps()
                        for kc in range(2):
                            mm(p_[0:64, :], wkv[:, kc, h * 128:h * 128 + 64], lat(3 + kc, t), kc == 0, kc == 1, [wkvr, ('LAT', 3 + kc, t)], [pr])
                        head_norm_rope(p_[0:64, :], pr, (0, 64), CM_BD64, V(l, 'gk', 0, 1, 0, 64), kh[0:64, t * TT:(t + 1) * TT], (kres, t, 'n'), t, None)
                        em.op('dve', 'tensor_copy', reads=[('KPE', t)], writes=[(kres, t, 'r')], out=kh[64:96, t * TT:(t + 1) * TT],
                              in_=KPE[64:96, t * TT:(t + 1) * TT])
                    if D0 and h == 0:
                        dbg_dump('qh0', qh[0:96, 0:512], (qres, 0), 96)
                        dbg_dump('kh0', kh[0:96, 0:512], (kres, 0, 'n'), 96)
                        dbg_dump('va0', VA[:, 0:512], ('VA', 0))
                    attention.tag = 'oa' if D0 else None
                    attention(h,
                              lambda kt, kh=kh, kres=kres: (kh[0:96, kt * 128:(kt + 1) * 128], [(kres, kt // 4, 'n'), (kres, kt // 4, 'r')]),
                              lambda c0, c1, qh=qh, qres=qres: (qh[0:96, c0:c1], (qres, c0 // TT)),
                              96,
                              lambda kt, h=h: (VA[:, kt * 520 + h * 65:kt * 520 + h * 65 + 65], ('VA', kt)),
                              96.0 ** -0.5)
                em.barrier()
                cfg['split'] = False
                del tfx[:]
                merge_phase(0, True)

                arena.reset()
                QKB = arena.alloc(8 * S)
                VB = arena.alloc(16 * 8 * 65)
                BIA = [arena.alloc(S) for _ in range(2)]
                KMH = arena.alloc(8)
                KML = arena.alloc(8)
                KMF = arena.alloc(8, F32)
                KMD = arena.alloc(8, F32)
                del tfx[:]
                tfx.extend([arena.alloc(512, F32) for _ in range(TFX_MOBA)])

                def qkb(c, t, r0=0, r1=128):
                    return QKB[r0:r1, c * S + t * TT:c * S + (t + 1) * TT]

                def qk_out(cbase):
                    def f(j, t, p_, pr):
                        c = cbase + j
                        head_norm_rope(p_[:], pr, (0, 128), CM_BD64, V(l, 'mq' if c < 4 else 'mk', 0, 1), qkb(c, t), ('QKB', c, t), t,
                                       (0, 1, CM_PSW64))
                    return f
                dense(win_view(672, 1184), 128, KC, 512, hT, qk_out(0))
                dense(win_view(1184, 1696), 128, KC, 512, hT, qk_out(4))
                em.op('pool', 'memset', writes=['VBones'], ap=VB.rearrange("p (n d) -> p n d", d=65)[:, :, 64:65], constant=1.0)
                wv, wvr = wload(win_view(1696, 2208), 128, KC, 512)
                for tt in range(16):
                    p_, pr = ps()
                    for kc in range(KC):
                        mm(p_[:], HT[:, kc * S + tt * 128:kc * S + (tt + 1) * 128], wv[:, kc, :], kc == 0, kc == KC - 1,
                           [wvr, ('HT', kc, tt // 4)], [pr])
                    evac_copy(VB[:, tt * 520:(tt + 1) * 520].rearrange("p (h d) -> p h d", d=65)[:, :, 0:64],
                              p_[:].rearrange("p (h d) -> p h d", d=64), [pr, 'VBones'], [('VB', tt)])
                cfg['split'] = True
                for h in range(8):
                    c = h // 2
                    r0 = (h % 2) * 64
                    r1 = r0 + 64
                    bia = BIA[h % 2]
                    bres = 'BIA%d' % (h % 2)
                    em.op('dve', 'tensor_reduce', reads=[('QKB', 4 + c, t) for t in range(NT)], writes=['KMF'], out=KMF[r0:r1, :],
                          in_=QKB[r0:r1, (4 + c) * S:(5 + c) * S].rearrange("p (n j) -> p n j", j=256), axis=AX.X, op=ALU.add)
                    em.op('dve', 'tensor_copy', reads=['KMF'], writes=['KMH'], out=KMH[r0:r1, :], in_=KMF[r0:r1, :])
                    em.op('dve', 'tensor_tensor', reads=['KMF', 'KMH'], writes=['KMD'], out=KMD[r0:r1, :], in0=KMF[r0:r1, :], in1=KMH[r0:r1, :], op=ALU.subtract)
                    em.op('dve', 'tensor_copy', reads=['KMD'], writes=['KML'], out=KML[r0:r1, :], in_=KMD[r0:r1, :])
                    for tt in range(16):
                        qb = tt // 2
                        sc = SM[:, 0:8]
                        bt = SM[:, 16:24]
                        m8 = SM[:, 32:40]
                        if qb >= 4:
                            p_, pr = ps()
                            mm(p_[:, 0:8], QKB[r0:r1, c * S + tt * 128:c * S + (tt + 1) * 128], KMH[r0:r1, :], True, False,
                               [('QKB', c, tt // 4), 'KMH'], [pr])
                            mm(p_[:, 0:8], QKB[r0:r1, c * S + tt * 128:c * S + (tt + 1) * 128], KML[r0:r1, :], False, True,
                               [('QKB', c, tt // 4), 'KML'], [pr])
                            em.op('dve', 'memset', writes=['SC'], ap=sc, constant=-1e30)
                            em.op('dve', 'tensor_copy', reads=[pr, 'SC'], writes=['SC'], out=sc[:, 0:qb], in_=p_[:, 0:qb])
                            em.op('dve', 'max', reads=['SC'], writes=['M8'], out=m8, in_=sc)
                            em.op('dve', 'tensor_scalar', reads=['SC', 'M8'], writes=['BT'], out=bt, in0=sc, scalar1=m8[:, 2:3], scalar2=None, op0=ALU.is_ge)
                            em.op('dve', 'tensor_scalar', reads=['BT'], writes=['BT'], out=bt, in0=bt, scalar1=1.0, scalar2=BIG, op0=ALU.subtract, op1=ALU.mult)
                            em.op('dve', 'memset', reads=['BT'], writes=['BT'], ap=bt[:, qb:qb + 1], constant=0.0)
                        else:
                            em.op('dve', 'memset', writes=['BT'], ap=bt, constant=-BIG)
                            em.op('dve', 'memset', reads=['BT'], writes=['BT'], ap=bt[:, 0:qb + 1], constant=0.0)
                        p2, p2r = ps()
                        em.op('pe', 'transpose', reads=['BT', 'IDF'], writes=[p2r], out=p2[0:8, 0:128], in_=bt, identity=IDF[:])
                        em.op('act', 'activation', reads=[p2r], writes=[(bres, tt)], out=bia[0:8, tt * 128:(tt + 1) * 128], in_=p2[0:8, 0:128], func=AF.Copy)
                    if D0 and h == 0:
                        dbg_dump('qb0', qkb(0, 2), ('QKB', 0, 2))
                        dbg_dump('kb0', qkb(4, 0), ('QKB', 4, 0))
                        dbg_dump('bia', bia[0:8, 1024:1536], (bres, 8), 8)
                    attention.tag = 'ob' if D0 else None
                    attention(h,
                              lambda kt, c=c, r0=r0, r1=r1: (QKB[r0:r1, (4 + c) * S + kt * 128:(4 + c) * S + (kt + 1) * 128], [('QKB', 4 + c, kt // 4)]),
                              lambda c0, c1, c=c, r0=r0, r1=r1: (QKB[r0:r1, c * S + c0:c * S + c1], ('QKB', c, c0 // TT)),
                              64,
                              lambda kt, h=h: (VB[:, kt * 520 + h * 65:kt * 520 + h * 65 + 65], ('VB', kt)),
                              0.125, bias=(bia, bres))
                em.barrier()
                cfg['split'] = False
                del tfx[:]
                merge_phase(1, False)

                arena.reset()
                UW = S + 32
                UU = arena.alloc(4 * UW)
                CV = arena.alloc(4 * S)
                UC = arena.alloc(4 * S)
                DG = [arena.alloc(31 * 128) for _ in range(2)]
                merge_phase.UC = UC
                for c in range(4):
                    em.op('pool', 'memset', writes=[('UUh', c)], ap=UU[:, c * UW:c * UW + 32], constant=0.0)
                    w, wr = ws()
                    w3 = w[:, 0:KC * 256].rearrange("p (k n) -> p k n", n=256)
                    em.op('pool', 'dma_start', writes=[wr], dma=True, out=w3[:, :, 0:128], in_=win_view(2208 + c * 128, 2208 + (c + 1) * 128))
                    em.op('pool', 'dma_start', reads=[wr], writes=[wr], dma=True, out=w3[:, :, 128:256], in_=win_view(2720 + c * 128, 2720 + (c + 1) * 128))
                    for t in range(NT):
                        pa, par = ps()
                        pg, pgr = ps()
                        for kc in range(KC):
                            ra, rr = hT(kc, t)
                            mm(pa[:], w3[:, kc, 0:128], ra, kc == 0, kc == KC - 1, [wr, rr], [par])
                        for kc in range(KC):
                            ra, rr = hT(kc, t)
                            mm(pg[:], w3[:, kc, 128:256], ra, kc == 0, kc == KC - 1, [wr, rr], [pgr])
                        sg, sgr = tf()
                        em.op('act', 'activation', reads=[pgr], writes=[sgr], out=sg[:], in_=pg[:], func=AF.Sigmoid)
                        em.op('dve', 'tensor_tensor', reads=[par, sgr], writes=[('UU', c, t)], out=UU[:, c * UW + 32 + t * TT:c * UW + 32 + (t + 1) * TT],
                              in0=pa[:], in1=sg[:], op=ALU.mult)
                for c in range(4):
                    dg = DG[c % 2]
                    dgr = 'DG%d' % (c % 2)
                    for k in range(31):
                        em.op('dve', 'tensor_scalar', reads=['CM', 'VEC'], writes=[(dgr, k)], out=dg[:, k * 128:(k + 1) * 128], in0=cm(CM_ID),
                              scalar1=V(l, 'cw', c * 31 + k), scalar2=None, op0=ALU.mult)
                    for t in range(NT):
                        p_, pr = ps()
                        for k in range(31):
                            o = c * UW + 2 + t * TT + k
                            rds = [(dgr, k), ('UU', c, t)]
                            if t > 0:
                                rds.append(('UU', c, t - 1))
                            else:
                                rds.append(('UUh', c))
                            mm(p_[:], dg[:, k * 128:(k + 1) * 128], UU[:, o:o + TT], k == 0, k == 30, rds, [pr])
                        em.op('act', 'activation', reads=[pr, 'VEC'], writes=[('CV', c, t)], out=CV[:, c * S + t * TT:c * S + (t + 1) * TT], in_=p_[:],
                              func=AF.Identity, bias=V(l, 'cb', c), scale=1.0)
                for t in range(NT):
                    pm, pmr = ps()
                    pq, pqr = ps()
                    for c in range(4):
                        cv = CV[:, c * S + t * TT:c * S + (t + 1) * TT]
                        mm(pm[:], cm(CM_ONES), cv, c == 0, c == 3, [('CV', c, t), 'CM'], [pmr])
                        sq, sqr = tb()
                        em.op('act', 'activation', reads=[('CV', c, t)], writes=[sqr], out=sq[:], in_=cv, func=AF.Square)
                        mm(pq[:], cm(CM_ONES), sq[:], c == 0, c == 3, [sqr, 'CM'], [pqr])
                    mean, meanr = tl()
                    em.op('act', 'activation', reads=[pmr], writes=[meanr], out=mean[:], in_=pm[:], func=AF.Copy, scale=1.0 / 512)
                    msq, msqr = tf()
                    em.op('dve', 'tensor_tensor', reads=[meanr], writes=[msqr], out=msq[:], in0=mean[:], in1=mean[:], op=ALU.mult)
                    var, varr = tf()
                    em.op('dve', 'scalar_tensor_tensor', reads=[pqr, msqr], writes=[varr], out=var[:], in0=pq[:], scalar=1.0 / 512, in1=msq[:],
                          op0=ALU.mult, op1=ALU.subtract)
                    rs, rsr = rstd_from(var[:], (0, 128), 1.0, varr, long=True)
                    for c in range(4):
                        cv = CV[:, c * S + t * TT:c * S + (t + 1) * TT]
                        d1, d1r = tf()
                        em.op('dve', 'tensor_tensor', reads=[('CV', c, t), meanr], writes=[d1r], out=d1[:], in0=cv, in1=mean[:], op=ALU.subtract)
                        d2, d2r = tf()
                        em.op('dve', 'tensor_tensor', reads=[d1r, rsr], writes=[d2r], out=d2[:], in0=d1[:], in1=rs[:], op=ALU.mult)
                        em.op('act', 'activation', reads=[d2r, 'VEC'], writes=[('UC', c, t)], out=UC[:, c * S + t * TT:c * S + (t + 1) * TT], in_=d2[:],
                              func=AF.Silu, scale=V(l, 'lng', c), bias=V(l, 'lnb', c))
                if D0:
                    dbg_dump('uu0', UU[:, 32:544], ('UU', 0, 0))
                    dbg_dump('cv0', CV[:, 0:512], ('CV', 0, 0))
                    dbg_dump('uc0', UC[:, 0:512], ('UC', 0, 0))
                em.barrier()
                merge_phase(2, False)
                if D0:
                    dbg_dump('m0', R1[:, 0:512], ('M', 0, 0))
                    em.barrier()

                arena.reset()
                wo = []
                for g in range(2):
                    wo.append(wload(wout_d[l].rearrange("(kc p) n -> p kc n", p=128)[:, :, g * 512:(g + 1) * 512], 128, KC, 512))
                for t in range(NT):
                    for c in range(KC):
                        w, wr = wo[c // 4]
                        j = c % 4
                        p_, pr = ps()
                        for kc in range(KC):
                            mm(p_[:], w[:, kc, j * 128:(j + 1) * 128], R1[:, kc * S + t * TT:kc * S + (t + 1) * TT], kc == 0, kc == KC - 1,
                               [wr, ('M', kc, t)], [pr])
                        xl, xlr = tf()
                        em.op('sp', 'dma_start', reads=[('XS', c, t)], writes=[xlr], dma=True, out=xl[:], in_=xs_view(c, t * TT, (t + 1) * TT))
                        xn, xnr = tf()
                        em.op('dve', 'scalar_tensor_tensor', reads=[pr, xlr, 'MODS'], writes=[xnr], out=xn[:], in0=p_[:],
                              scalar=MD(l, 2, c, b), in1=xl[:], op0=ALU.mult, op1=ALU.add)
                        em.op('sp', 'dma_start', reads=[xnr], writes=[('XS', c, t)], dma=True, out=xs_view(c, t * TT, (t + 1) * TT), in_=xn[:])

                norm_phase(8, 3)
                for half in range(2):
                    arena.reset()
                    FA = arena.alloc(11 * S)
                    UAB = [[arena.alloc(S + 4) for _ in range(2)] for _ in range(2)]
                    DGF = [arena.alloc(6 * 128) for _ in range(2)]
                    for jj in range(11):
                        ca = half * 11 + jj
                        cb_ = 22 + ca
                        ua, ub = UAB[jj % 2]
                        uar, ubr = 'UA%d' % (jj % 2), 'UB%d' % (jj % 2)
                        dgf = DGF[jj % 2]
                        dgfr = 'DGF%d' % (jj % 2)
                        em.op('pool', 'memset', writes=[(uar, 'h')], ap=ua[:, 0:4], constant=0.0)
                        em.op('pool', 'memset', writes=[(ubr, 'h')], ap=ub[:, 0:4], constant=0.0)
                        for k in range(2):
                            em.op('dve', 'tensor_scalar', reads=['CM', 'VEC'], writes=[(dgfr, k)], out=dgf[:, k * 128:(k + 1) * 128], in0=cm(CM_ID),
                                  scalar1=V(l, 'fw', ca * 3 + k), scalar2=None, op0=ALU.mult)
                            em.op('dve', 'tensor_scalar', reads=['CM', 'VEC'], writes=[(dgfr, 3 + k)], out=dgf[:, (3 + k) * 128:(4 + k) * 128], in0=cm(CM_ID),
                                  scalar1=V(l, 'fw', cb_ * 3 + k), scalar2=None, op0=ALU.mult)
                        w, wr = ws()
                        w3 = w[:, 0:KC * 256].rearrange("p (k n) -> p k n", n=256)
                        wup_v = wup_d[l].rearrange("(kc p) n -> p kc n", p=128)
                        em.op('pool', 'dma_start', writes=[wr], dma=True, out=w3[:, :, 0:128], in_=wup_v[:, :, ca * 128:(ca + 1) * 128])
                        em.op('pool', 'dma_start', reads=[wr], writes=[wr], dma=True, out=w3[:, :, 128:256], in_=wup_v[:, :, cb_ * 128:(cb_ + 1) * 128])
                        for t in range(NT):
                            pa, par = ps()
                            pb, pbr = ps()
                            for kc in range(KC):
                                ra, rr = hT(kc, t)
                                mm(pa[:], w3[:, kc, 0:128], ra, kc == 0, kc == KC - 1, [wr, rr], [par])
                            for kc in range(KC):
                                ra, rr = hT(kc, t)
                                mm(pb[:], w3[:, kc, 128:256], ra, kc == 0, kc == KC - 1, [wr, rr], [pbr])
                            em.op('act', 'activation', reads=[par], writes=[(uar, t)], out=ua[:, 4 + t * TT:4 + (t + 1) * TT], in_=pa[:], func=AF.Copy)
                            em.op('dve', 'tensor_copy', reads=[pbr], writes=[(ubr, t)], out=ub[:, 4 + t * TT:4 + (t + 1) * TT], in_=pb[:])
                        for t in range(NT):
                            pa, par = ps()
                            pb, pbr = ps()
                            for (pp_, ppr, u, ur, k0) in [(pa, par, ua, uar, 0), (pb, pbr, ub, ubr, 3)]:
                                for k in range(2):
                                    o = 2 + t * TT + k
                                    rds = [(dgfr, k0 + k), (ur, t), (ur, t - 1) if t > 0 else (ur, 'h')]
                                    mm(pp_[:], dgf[:, (k0 + k) * 128:(k0 + k + 1) * 128], u[:, o:o + TT], k == 0, k == 1, rds, [ppr])
                            ta, tar = tf()
                            em.op('dve', 'scalar_tensor_tensor', reads=[(uar, t), par, 'VEC'], writes=[tar], out=ta[:], in0=ua[:, 4 + t * TT:4 + (t + 1) * TT],
                                  scalar=V(l, 'fw', ca * 3 + 2), in1=pa[:], op0=ALU.mult, op1=ALU.add)
                            aa, aar = tb()
                            em.op('act', 'activation', reads=[tar, 'VEC'], writes=[aar], out=aa[:], in_=ta[:], func=AF.Silu, bias=V(l, 'fb', ca), scale=1.0)
                            tb2, tb2r = tf()
                            em.op('dve', 'scalar_tensor_tensor', reads=[(ubr, t), pbr, 'VEC'], writes=[tb2r], out=tb2[:], in0=ub[:, 4 + t * TT:4 + (t + 1) * TT],
                                  scalar=V(l, 'fw', cb_ * 3 + 2), in1=pb[:], op0=ALU.mult, op1=ALU.add)
                            em.op('dve', 'scalar_tensor_tensor', reads=[tb2r, aar, 'VEC'], writes=[('FA', jj, t)], out=FA[:, jj * S + t * TT:jj * S + (t + 1) * TT],
                                  in0=tb2[:], scalar=V(l, 'fb', cb_), in1=aa[:], op0=ALU.add, op1=ALU.mult)
                    resid_phase(lambda g, half=half: wdn_d[l][half * 1408:(half + 1) * 1408, :].rearrange("(k p) n -> p k n", p=128)[:, :, g * 256:(g + 1) * 256],
                                4, 256, 11, lambda kk, t: (FA[:, kk * S + t * TT:kk * S + (t + 1) * TT], ('FA', kk, t)), 5)

            arena.reset()
            XL = [arena.alloc(D, F32) for _ in range(2)]
            XT_ = [arena.alloc(D, F32) for _ in range(2)]
            for tt in range(16):
                xl = XL[tt % 2]
                xr = 'XL%d' % (tt % 2)
                em.op('sp', 'dma_start', writes=[xr], dma=True, out=xl.rearrange("p (c t) -> p c t", t=128),
                      in_=XS[:, tt * 128:(tt + 1) * 128].rearrange("(c p) t -> p c t", p=128))
                xt = XT_[tt % 2]
                xtr = 'XT%d' % (tt % 2)
                for hh in range(2):
                    p_, pr = ps()
                    for q in range(4):
                        c = hh * 4 + q
                        em.op('pe', 'transpose', reads=[xr, 'IDF'], writes=[pr], out=p_[:, q * 128:(q + 1) * 128],
                              in_=xl[:, c * 128:(c + 1) * 128], identity=IDF[:])
                    em.op('act' if hh == 0 else 'dve', 'activation' if hh == 0 else 'tensor_copy',
                          reads=[pr], writes=[xtr + 'h%d' % hh], out=xt[:, hh * 512:(hh + 1) * 512], in_=p_[:],
                          **({'func': AF.Copy} if hh == 0 else {}))
                em.op('sp', 'dma_start', reads=[xtr + 'h0', xtr + 'h1'], writes=[('OUT', tt)], dma=True,
                      out=out_d[b, tt * 128:(tt + 1) * 128, :], in_=xt)
            em.barrier()

        em.emit(st)
    return nc


def _consts():
    cmat = np.zeros((128, NCM * 128), np.float32)
    i = np.arange(128)
    cmat[:, CM_ID * 128:(CM_ID + 1) * 128] = np.eye(128)
    cmat[:, CM_ONES * 128:(CM_ONES + 1) * 128] = 1.0
    bd = (i[:, None] // 64 == i[None, :] // 64).astype(np.float32)
    cmat[:, CM_BD64 * 128:(CM_BD64 + 1) * 128] = bd
    grp = np.where(i < 64, 0, np.where(i < 96, 1, 2))
    cmat[:, CM_BDA * 128:(CM_BDA + 1) * 128] = (grp[:, None] == grp[None, :]).astype(np.float32)
    pi64 = (i // 64) * 64 + ((i % 64) + 32) % 64
    m = np.zeros((128, 128), np.float32)
    m[pi64, i] = 1.0
    cmat[:, CM_PSW64 * 128:(CM_PSW64 + 1) * 128] = m
    pia = i.copy()
    for j in range(64, 96):
        pia[j] = 64 + ((j - 64) + 16) % 32
    m = np.zeros((128, 128), np.float32)
    m[pia, i] = 1.0
    cmat[:, CM_PSWA * 128:(CM_PSWA + 1) * 128] = m
    cmat[:, CM_TRI * 128:(CM_TRI + 1) * 128] = (i[:, None] <= i[None, :]).astype(np.float32)
    ind = np.zeros((8, 1024), np.float32)
    for n in range(8):
        ind[n, n * 128:(n + 1) * 128] = 1.0
    pcol = np.zeros((128, NPC), np.float32)
    invb = 1.0 / (10000.0 ** (np.arange(0, 64, 2, dtype=np.float32) / 64.0))
    inva = 1.0 / (10000.0 ** (np.arange(0, 32, 2, dtype=np.float32) / 32.0))
    tw = 2.0 * math.pi
    fb = invb[i % 32]
    sgn = np.where((i % 64) < 32, -1.0, 1.0)
    pcol[:, PC_FBS] = sgn * fb / tw
    pcol[:, PC_PBS] = 0.0
    pcol[:, PC_FBC] = fb / tw
    pcol[:, PC_PBC] = 0.25
    fa = np.zeros(128)
    sa = np.zeros(128)
    for j in range(64, 96):
        fa[j] = inva[(j - 64) % 16]
        sa[j] = -1.0 if j < 80 else 1.0
    pcol[:, PC_FAS] = sa * fa / tw
    pcol[:, PC_PAS] = 0.0
    pcol[:, PC_FAC] = fa / tw
    pcol[:, PC_PAC] = 0.25
    pcol[:, PC_QSC] = np.where(i < 64, 1.0 / 64, 1.0 / 32)
    return cmat, ind, np.eye(128, dtype=np.float32), pcol


def _vecpack(inp, NL):
    vec = np.zeros((NL, 128, NV), np.float32)

    def put(l, name, arr2d):
        o = VOFF[name]
        vec[l, :arr2d.shape[0], o:o + arr2d.shape[1]] = arr2d
    for l in range(NL):
        put(l, 'mixg', inp['mix_norm_g'][l].reshape(8, 128).T)
        put(l, 'ffng', inp['ffn_norm_g'][l].reshape(8, 128).T)
        put(l, 'bmod', inp['b_mod'][l].reshape(48, 128).T)
        put(l, 'qlg', inp['mla_q_norm_g'][l].reshape(3, 128).T)
        put(l, 'kvg', inp['mla_kv_norm_g'][l].reshape(2, 128).T)
        put(l, 'gq', np.concatenate([inp['mla_qn_nope_g'][l], inp['mla_qn_rope_g'][l]])[:, None])
        put(l, 'gk', np.concatenate([inp['mla_kn_nope_g'][l], inp['mla_kn_rope_g'][l]])[:, None])
        put(l, 'mq', np.tile(inp['moba_qn_g'][l], 2)[:, None])
        put(l, 'mk', np.tile(inp['moba_kn_g'][l], 2)[:, None])
        put(l, 'cw', inp['conv_dw_w'][l].reshape(31, 4, 128).transpose(2, 1, 0).reshape(128, 124))
        put(l, 'cb', inp['conv_dw_b'][l].reshape(4, 128).T)
        put(l, 'lng', inp['conv_ln_g'][l].reshape(4, 128).T)
        put(l, 'lnb', inp['conv_ln_b'][l].reshape(4, 128).T)
        put(l, 'bpw', inp['b_conv_pw2'][l].reshape(8, 128).T)
        put(l, 'fw', inp['ffn_dw_w'][l].reshape(3, 44, 128).transpose(2, 1, 0).reshape(128, 132))
        put(l, 'fb', inp['ffn_dw_b'][l].reshape(44, 128).T)
    return vec


_WNAMES = ['w_mod', 'w_in', 'w_uq', 'w_ukv', 'w_mla_o', 'w_moba_o', 'w_conv_pw2', 'w_out', 'w_up', 'w_down']


def make_in_maps(inp, ncores, nseq, NL):
    cmat, ind, idf, pcol = _consts()
    vec = _vecpack(inp, NL)
    shared = {n: np.ascontiguousarray(np.asarray(inp[n], np.float32)[:NL]) for n in _WNAMES}
    shared.update(vec=vec, cmat=cmat, ind=ind, identf=idf, pcol=pcol)
    maps = []
    for i in range(ncores):
        b0 = i * nseq
        c = np.asarray(inp['c'], np.float32)[b0:b0 + nseq]
        cT = np.ascontiguousarray(c.reshape(nseq, 8, 128).transpose(2, 1, 0).reshape(128, 8 * nseq))
        m = dict(shared)
        m['x'] = np.ascontiguousarray(np.asarray(inp['x'], np.float32)[b0:b0 + nseq])
        m['cT'] = cT
        m['pos'] = np.ascontiguousarray(np.asarray(inp['positions'], np.int32)[b0:b0 + nseq])
        maps.append(m)
    return maps


def kernel(**inputs):
    nc = build_program(2, 2)
    maps = make_in_maps(inputs, NCORE, 2, 2)
    res = run_bass_kernel_spmd(nc, maps, core_ids=list(range(NCORE)))
    return np.concatenate([np.asarray(r["out"], np.float32) for r in res.results], axis=0)
```

```python
import math
import numpy as np
import concourse.bass as bass
import concourse.mybir as mybir
from contextlib import ExitStack
from concourse.bass_utils import run_bass_kernel_spmd

F32 = mybir.dt.float32
BF16 = mybir.dt.bfloat16
I32 = mybir.dt.int32
AF = mybir.ActivationFunctionType
ALU = mybir.AluOpType
AX = mybir.AxisListType

S = 2048
TT = 512
NT = 4
D = 1024
KC = 8
DIN = 6304
DFF = 2816
EPS = 1e-6
NCORE = 8
BIG = 30000.0
CP_SLACK = 50.0
NACC = 1
NSB = 3
TFX_MLA = 8
TFX_MOBA = 4
PROF = False
CRIT = None


class Em:
    ENGS = ('pe', 'act', 'dve', 'pool', 'sp')
    NDMA = 12

    def __init__(self, nc):
        self.nc = nc
        self.ops = []
        self.lastw = {}
        self.readers = {}
        self.per_eng = {e: [] for e in self.ENGS}
        self.bars = []

    def op(self, eng, name, reads=(), writes=(), dma=False, **kw):
        i = len(self.ops)
        deps = set()
        for r in list(reads) + list(writes):
            w = self.lastw.get(r)
            if w is not None:
                deps.add(w)
        for w in writes:
            for rd in self.readers.get(w, ()):
                deps.add(rd)
        for w in writes:
            self.lastw[w] = i
            self.readers[w] = []
        for r in reads:
            self.readers.setdefault(r, []).append(i)
        deps.discard(i)
        self.ops.append(dict(eng=eng, name=name, kw=kw, deps=deps, dma=dma,
                             pos=len(self.per_eng[eng])))
        self.per_eng[eng].append(i)
        return i

    def barrier(self):
        self.bars.append(len(self.ops))
        self.lastw = {k: v for k, v in self.lastw.items() if isinstance(k, str) and k.startswith('WS')}
        self.readers = {k: v for k, v in self.readers.items() if isinstance(k, str) and k.startswith('WS')}

    @staticmethod
    def _fsz(ap):
        sh = ap.shape
        n = 1
        for d in sh[1:]:
            n *= d
        return n

    def _cost(self, o):
        kw = o['kw']
        if o['dma']:
            ap = kw['out']
            nbytes = self._fsz(ap) * ap.shape[0] * 4
            occ = 1100.0 if o['eng'] == 'pool' else 150.0
            return occ, 2500.0 + nbytes / 100.0
        e = o['eng']
        if e == 'pe':
            nn = self._fsz(kw['rhs']) if 'rhs' in kw else 128
            t = max(64, nn) / 2.4 + 25.0
            return t, t + 200.0
        ap = kw.get('out', kw.get('ap'))
        nn = self._fsz(ap)
        if e == 'act':
            t = 200.0 + nn / 1.3
        elif e == 'dve':
            t = 120.0 + nn / 0.9 * (2.0 if o['name'] == 'scalar_tensor_tensor' else 1.0)
        else:
            t = 300.0 + nn * 2.0 if o['name'] != 'memset' else 100.0 + nn * 0.2
        return t, t + 250.0

    def schedule(self):
        import bisect
        ops = self.ops
        n = len(ops)
        bars = sorted(self.bars)
        seg = [bisect.bisect_right(bars, i) for i in range(n)]
        nseg = (seg[-1] if n else 0) + 1
        users = [[] for _ in range(n)]
        for i, o in enumerate(ops):
            for d in o['deps']:
                users[d].append(i)
        ndl = [len(o['deps']) for o in ops]
        hoist = [o['dma'] and o['eng'] == 'pool' for o in ops]
        ready = [0.0] * n
        sched = [False] * n
        costs = [self._cost(o) for o in ops]
        tail = [0.0] * n
        for i in range(n - 1, -1, -1):
            t = 0.0
            for u in users[i]:
                if seg[u] == seg[i] and tail[u] > t:
                    t = tail[u]
            tail[i] = t + costs[i][1]
        SLACK = CP_SLACK
        segrem = [0] * (nseg + 1)
        for i in range(n):
            segrem[seg[i]] += 1
        segfin = [0.0] * (nseg + 1)
        self._why = {}
        self._st = {}
        self._fin = [0.0] * n
        head = {e: 0 for e in self.ENGS}
        free = {e: 0.0 for e in self.ENGS}
        order = {e: [] for e in self.ENGS}
        WIN = {'pe': 500, 'act': 160, 'dve': 160, 'pool': 80, 'sp': 80}
        segdone_upto = 0
        segfin_cum = [0.0] * (nseg + 2)
        for _ in range(n):
            while segdone_upto < nseg and segrem[segdone_upto] == 0:
                segfin_cum[segdone_upto + 1] = max(segfin_cum[segdone_upto], segfin[segdone_upto])
                segdone_upto += 1
            best = None
            for e in self.ENGS:
                lst = self.per_eng[e]
                h = head[e]
                L = len(lst)
                while h < L and sched[lst[h]]:
                    h += 1
                head[e] = h
                cnt = 0
                k = h
                fe = free[e]
                ebest = None
                W = WIN[e]
                while k < L and cnt < W:
                    i = lst[k]
                    k += 1
                    if sched[i]:
                        continue
                    cnt += 1
                    if ndl[i] > 0:
                        continue
                    st = ready[i]
                    if not hoist[i]:
                        sg = seg[i]
                        if sg > segdone_upto:
                            break
                        if segfin_cum[sg] > st:
                            st = segfin_cum[sg]
                    if fe > st:
                        st = fe
                    if ebest is None or st < ebest[0] - SLACK or (st <= ebest[0] + SLACK and tail[i] > ebest[3]):
                        ebest = (st, e, i, tail[i])
                if ebest is not None and (best is None or ebest[0] < best[0] or (ebest[0] == best[0] and ebest[2] < best[2])):
                    best = ebest
            st, e, i = best[0], best[1], best[2]
            occ, lat = costs[i]
            if getattr(self, 'prof', None) is not None:
                why = ('eng', order[e][-1] if order[e] else None) if (free[e] >= st - 1e-9 and order[e]) else None
                if why is None:
                    bd = None
                    for d in ops[i]['deps']:
                        if self._fin[d] >= st - 1e-9:
                            bd = d
                    why = ('dep', bd)
                self._why[i] = why
                self._st[i] = st
                self._fin[i] = st + lat
            free[e] = st + occ
            f = st + lat
            sched[i] = True
            order[e].append(i)
            for u in users[i]:
                ndl[u] -= 1
                if f > ready[u]:
                    ready[u] = f
            sg = seg[i]
            segrem[sg] -= 1
            if f > segfin[sg]:
                segfin[sg] = f
            if getattr(self, 'prof', None) is not None:
                self.prof.setdefault(sg, {}).setdefault(e, [0.0, 0])
                self.prof[sg][e][0] += occ
                self.prof[sg][e][1] += 1
        self.per_eng = order
        for e in self.ENGS:
            for p, i in enumerate(order[e]):
                ops[i]['pos'] = p
        self.seg = seg
        self.hoist = hoist
        self.est_total = max(free.values())
        if getattr(self, 'prof', None) is not None and getattr(self, 'crit_seg', None) is not None:
            cs = self.crit_seg
            last = max((i for i in range(n) if seg[i] == cs), key=lambda i: self._fin[i])
            i = last
            stats = {}
            chain = []
            while i is not None and seg[i] == cs:
                kind, j = self._why[i]
                key = (kind, ops[i]['eng'], ops[i]['name'])
                t_prev = self._st[j] if (j is not None and j in self._st) else self._st[i]
                stats[key] = stats.get(key, 0.0) + (self._st[i] - t_prev)
                chain.append((i, ops[i]['eng'], ops[i]['name'], kind, round(self._st[i] / 1e3, 2)))
                i = j
            print('critical path seg', cs, 'len', len(chain))
            for k, v in sorted(stats.items(), key=lambda kv: -kv[1])[:14]:
                print('   %-40s %8.1f us' % (str(k), v / 1e3))
            print('   tail of chain:', chain[:40])
        if getattr(self, 'prof', None) is not None:
            prev = 0.0
            for sg in range(nseg):
                end = max(prev, segfin[sg])
                d = end - prev
                if d > 0:
                    print('seg %3d dur %8.1f us ' % (sg, d / 1e3) + ' '.join('%s %3d%%(%d)' % (e, 100 * self.prof.get(sg, {}).get(e, [0, 0])[0] / d, self.prof.get(sg, {}).get(e, [0, 0])[1]) for e in self.ENGS))
                prev = end
        print('[em] ops', n, {e: len(order[e]) for e in self.ENGS}, 'est_ms %.3f' % (self.est_total / 1e6), flush=True)

    def emit(self, stack):
        nc = self.nc
        ops = self.ops
        n = len(ops)
        self.schedule()
        seg, hoist = self.seg, self.hoist
        nseg = (max(seg) if n else 0) + 1
        seg_dmas = [[] for _ in range(nseg + 1)]
        for i, o in enumerate(ops):
            if o['dma'] and not hoist[i]:
                seg_dmas[seg[i]].append(i)
        last_comp = {}
        for e in self.ENGS:
            cur = {}
            for i in self.per_eng[e]:
                if not ops[i]['dma']:
                    cur[seg[i]] = i
            last_comp[e] = cur
        for e in self.ENGS:
            covered = 0
            for i in self.per_eng[e]:
                if hoist[i]:
                    continue
                sg = seg[i]
                if sg > covered:
                    for e2 in self.ENGS:
                        lc = last_comp[e2]
                        for s2 in range(covered, sg):
                            if s2 in lc:
                                ops[i]['deps'].add(lc[s2])
                    for s2 in range(covered, sg):
                        for j in seg_dmas[s2]:
                            ops[i]['deps'].add(j)
                    covered = sg
        need = [False] * n
        for i, o in enumerate(ops):
            for d in o['deps']:
                od = ops[d]
                if od['dma'] or od['eng'] != o['eng']:
                    need[d] = True
                elif od['eng'] == 'pe':
                    pass
                elif o['pos'] - od['pos'] <= 3:
                    need[d] = True
        sems = {e: stack.enter_context(nc.semaphore('s_' + e)) for e in self.ENGS}
        dsems = {e: [stack.enter_context(nc.semaphore('d_%s_%d' % (e, k))) for k in range(self.NDMA)]
                 for e in ('sp', 'pool', 'act')}
        cnt = {e: 0 for e in self.ENGS}
        dcnt = {e: [0] * self.NDMA for e in dsems}
        dnum = {e: 0 for e in dsems}
        dprev = {}
        for e in self.ENGS:
            for i in self.per_eng[e]:
                o = ops[i]
                if o['dma']:
                    k = dnum[e] % self.NDMA
                    dnum[e] += 1
                    dcnt[e][k] += 16
                    o['sig'] = (dsems[e][k], dcnt[e][k])
                    pk = (e, k)
                    if pk in dprev:
                        o['deps'].add(dprev[pk])
                    dprev[pk] = i
                elif need[i]:
                    cnt[e] += 1
                    o['sig'] = (sems[e], cnt[e])
                else:
                    o['sig'] = None
        block = stack.enter_context(nc.Block())
        em = self

        def run(ename):
            def body(engine):
                seen = {}
                for i in em.per_eng[ename]:
                    o = ops[i]
                    waits = {}
                    for d in o['deps']:
                        od = ops[d]
                        if od['sig'] is None:
                            continue
                        if (not od['dma']) and od['eng'] == ename and ename == 'pe':
                            continue
                        s, v = od['sig']
                        key = id(s)
                        if key not in waits or waits[key][1] < v:
                            waits[key] = (s, v)
                    for key, (s, v) in waits.items():
                        if seen.get(key, 0) >= v:
                            continue
                        seen[key] = v
                        engine.wait_ge(s, v)
                    try:
                        ins = getattr(engine, o['name'])(**o['kw'])
                    except Exception:
                        print('EMIT FAIL op', i, ename, o['name'], {k: (getattr(v, 'shape', v)) for k, v in o['kw'].items()})
                        raise
                    if o['sig'] is not None:
                        ins.then_inc(o['sig'][0], 16 if o['dma'] else 1)
                if ename in dsems:
                    for k in range(em.NDMA):
                        if dcnt[ename][k] > 0:
                            engine.wait_ge(dsems[ename][k], dcnt[ename][k])
            return body

        block.tensor(run('pe'))
        block.scalar(run('act'))
        block.vector(run('dve'))
        block.gpsimd(run('pool'))
        block.sync(run('sp'))


VOFF = {}
_o = 0
for _n, _w in [('mixg', 8), ('ffng', 8), ('bmod', 48), ('qlg', 3), ('kvg', 2), ('gq', 1), ('gk', 1),
               ('mq', 1), ('mk', 1), ('cw', 124), ('cb', 4), ('lng', 4), ('lnb', 4), ('bpw', 8),
               ('fw', 132), ('fb', 44)]:
    VOFF[_n] = _o
    _o += _w
NV = _o
CM_ID, CM_ONES, CM_BD64, CM_BDA, CM_PSW64, CM_PSWA, CM_TRI = range(7)
NCM = 7
PC_FBS, PC_PBS, PC_FBC, PC_PBC, PC_FAS, PC_PAS, PC_FAC, PC_PAC, PC_QSC = range(9)
NPC = 16


def build_program(NSEQ=2, NL=2, dbg=None):
    nc = bass.Bass("TRN2", target_bir_lowering=False)

    def din(name, shape, dt=F32):
        return nc.dram_tensor(name, list(shape), dt, kind="ExternalInput").ap()

    x_d = din("x", [NSEQ, S, D])
    cT_d = din("cT", [128, KC * NSEQ])
    pos_d = din("pos", [NSEQ, S], I32)
    wmod_d = din("w_mod", [NL, D, 6 * D])
    win_d = din("w_in", [NL, D, DIN])
    wuq_d = din("w_uq", [NL, 384, 768])
    wukv_d = din("w_ukv", [NL, 256, 1024])
    wmo_d = din("w_mla_o", [NL, 512, D])
    wbo_d = din("w_moba_o", [NL, 512, D])
    wpw_d = din("w_conv_pw2", [NL, 512, D])
    wout_d = din("w_out", [NL, D, D])
    wup_d = din("w_up", [NL, D, 2 * DFF])
    wdn_d = din("w_down", [NL, DFF, D])
    vec_d = din("vec", [NL, 128, NV])
    cmat_d = din("cmat", [128, NCM * 128])
    ind_d = din("ind", [8, 1024])
    idf_d = din("identf", [128, 128])
    pcol_d = din("pcol", [128, NPC])
    out_d = nc.dram_tensor("out", [NSEQ, S, D], F32, kind="ExternalOutput").ap()
    XS = nc.dram_tensor("xs_scr", [D, S], F32).ap()
    OD = nc.dram_tensor("od_scr", [128, 4 * S], BF16).ap()
    dbg_d = {}
    if dbg:
        for k, shp in dbg.items():
            dbg_d[k] = nc.dram_tensor("dbg_" + k, list(shp), F32, kind="ExternalOutput").ap()

    st = ExitStack()
    with st:
        def sb(name, shape, dt):
            return st.enter_context(nc.sbuf_tensor(name, list(shape), dt))

        VEC = sb("VEC", [128, NL * NV], F32)
        MODS = sb("MODS", [128, NL * 48 * NSEQ], F32)
        AB = sb("AB", [128, 16], F32)
        IDF = sb("IDF", [128, 128], F32)
        PCOL = sb("PCOL", [128, NPC], F32)
        CM = sb("CM", [128, NCM * 128], BF16)
        IND = sb("IND", [8, 1024], BF16)
        CACT = sb("CACT", [128, KC * NSEQ], BF16)
        CTF = sb("CTF", [128, KC * NSEQ], F32)
        NW = 3
        WS = [sb("WS%d" % i, [128, 4096], BF16) for i in range(NW)]
        NTF = 6
        TF = [sb("TF%d" % i, [128, 512], F32) for i in range(NTF)]
        NTB = 4
        TB = [sb("TB%d" % i, [128, 512], BF16) for i in range(NTB)]
        NTP = 4
        TP = [sb("TP%d" % i, [128, 512], BF16) for i in range(NTP)]
        NTL = 3
        TL = [sb("TL%d" % i, [128, 512], F32) for i in range(NTL)]
        HT = sb("HT", [128, KC * S], BF16)
        R1 = sb("R1", [128, KC * S], BF16)
        TAB = sb("TAB", [128, 4 * S], BF16)
        SM = sb("SM", [128, 64], F32)
        ARN = 33000
        AR = sb("AR", [128, ARN], BF16)
        PS = [st.enter_context(nc.psum_tensor("PS%d" % i, [128, 512], F32)) for i in range(8)]

        em = Em(nc)
        em.prof = {} if PROF else None
        em.crit_seg = CRIT
        ctr = {'w': 0, 'tf': 0, 'tb': 0, 'ps': 0, 'acc': 0, 'tl': 0, 'psg': 0, 'pss': 0, 'tp': 0}
        cfg = {'split': False}

        def cm(i, r0=0, r1=128, c0=0, c1=128):
            return CM[r0:r1, i * 128 + c0:i * 128 + c1]

        def nxt(kind, n):
            ctr[kind] = (ctr[kind] + 1) % n
            return ctr[kind]

        tfx = []

        def tf():
            i = nxt('tf', NTF + len(tfx))
            if i < NTF:
                return TF[i], 'TF%d' % i
            return tfx[i - NTF], 'TFX%d' % (i - NTF)

        def tl():
            i = nxt('tl', NTL)
            return TL[i], 'TL%d' % i

        def tb():
            i = nxt('tb', NTB)
            return TB[i], 'TB%d' % i

        def ps():
            if cfg['split']:
                i = NSB + nxt('psg', 8 - NACC - NSB)
            else:
                i = nxt('ps', 8 - NACC)
            return PS[i], 'PS%d' % i

        def sbank():
            i = nxt('pss', NSB)
            return PS[i], 'PS%d' % i

        def ptile():
            i = nxt('tp', NTP)
            return TP[i], 'TP%d' % i

        def acc():
            i = 8 - NACC + nxt('acc', NACC)
            return PS[i], 'PS%d' % i

        def ws():
            i = nxt('w', NW)
            return WS[i], 'WS%d' % i

        class Arena:
            def __init__(self):
                self.off = 0

            def reset(self):
                self.off = 0

            def alloc(self, ncols, dt=BF16):
                n = ncols * (2 if dt in (F32, I32) else 1)
                n = (n + 3) // 4 * 4
                a = AR[:, self.off:self.off + n]
                self.off += n
                assert self.off <= ARN, self.off
                if dt == F32:
                    a = a.bitcast(F32)
                elif dt == I32:
                    a = a.bitcast(I32)
                return a
        arena = Arena()

        def V(l, name, c=0, n=1, r0=0, r1=128):
            o = l * NV + VOFF[name] + c
            return VEC[r0:r1, o:o + n]

        def MD(l, k, c, b):
            o = (l * 48 + k * 8 + c) * NSEQ + b
            return MODS[:, o:o + 1]

        def wload(view, p, kcn, ncols, extra=None):
            w, r = ws()
            dst = w[0:p, 0:kcn * ncols].rearrange("p (k n) -> p k n", n=ncols)
            em.op('pool', 'dma_start', writes=[r], dma=True, out=dst, in_=view)
            return dst, r

        def mm(out, lhsT, rhs, start, stop, reads, writes):
            em.op('pe', 'matmul', reads=reads, writes=writes, out=out, lhsT=lhsT, rhs=rhs,
                  start=start, stop=stop)

        def rstd_from(psum_ap, rows, scale, pres, long=False):
            t1, r1 = tf()
            em.op('act', 'activation', reads=[pres], writes=[r1], out=t1[rows[0]:rows[1], :],
                  in_=psum_ap, func=AF.Ln, scale=scale, bias=EPS)
            t2, r2 = tl() if long else tf()
            em.op('act', 'activation', reads=[r1], writes=[r2], out=t2[rows[0]:rows[1], :],
                  in_=t1[rows[0]:rows[1], :], func=AF.Exp, scale=-0.5)
            return t2, r2

        def dbg_dump(name, ap_sb, res, rows=128):
            if name in dbg_d:
                t, r = tf()
                em.op('dve', 'tensor_copy', reads=[res], writes=[r], out=t[0:rows, :], in_=ap_sb)
                em.op('sp', 'dma_start', reads=[r], writes=['dbg_' + name], dma=True,
                      out=dbg_d[name][0:rows, :], in_=t[0:rows, :])

        em.op('sp', 'dma_start', writes=['VEC'], dma=True,
              out=VEC[:].rearrange("p (l n) -> p l n", n=NV), in_=vec_d.rearrange("l p n -> p l n"))
        em.op('sp', 'dma_start', writes=['IDF'], dma=True, out=IDF[:], in_=idf_d)
        em.op('sp', 'dma_start', writes=['PCOL'], dma=True, out=PCOL[:], in_=pcol_d)
        em.op('sp', 'dma_start', writes=['CTF'], dma=True, out=CTF[:], in_=cT_d)
        em.op('pool', 'dma_start', writes=['CM'], dma=True, out=CM[:], in_=cmat_d)
        em.op('pool', 'dma_start', writes=['IND'], dma=True, out=IND[:], in_=ind_d)
        em.op('act', 'activation', reads=['CTF'], writes=['CACT'], out=CACT[:], in_=CTF[:], func=AF.Silu)
        for l in range(NL):
            for g in range(12):
                view = wmod_d[l].rearrange("(kc p) n -> p kc n", p=128)[:, :, g * 512:(g + 1) * 512]
                w, wr = wload(view, 128, KC, 512)
                for j in range(4):
                    ch = g * 4 + j
                    p_, pr = ps()
                    for kc in range(KC):
                        mm(p_[:, 0:NSEQ], w[:, kc, j * 128:(j + 1) * 128], CACT[:, kc * NSEQ:(kc + 1) * NSEQ],
                           kc == 0, kc == KC - 1, [wr, 'CACT'], [pr])
                    o = (l * 48 + ch) * NSEQ
                    em.op('dve', 'tensor_scalar', reads=[pr, 'VEC'], writes=['MODS'], out=MODS[:, o:o + NSEQ],
                          in0=p_[:, 0:NSEQ], scalar1=V(l, 'bmod', ch), scalar2=None, op0=ALU.add)

        def xs_view(c, c0, c1):
            return XS[c * 128:(c + 1) * 128, c0:c1]

        for b in range(NSEQ):
            arena.reset()
            XL = [arena.alloc(D, F32) for _ in range(2)]
            XT_ = [arena.alloc(D, F32) for _ in range(2)]
            for tt in range(16):
                xl = XL[tt % 2]
                xr = 'XL%d' % (tt % 2)
                em.op('sp', 'dma_start', writes=[xr], dma=True, out=xl, in_=x_d[b, tt * 128:(tt + 1) * 128, :])
                xt = XT_[tt % 2]
                xtr = 'XT%d' % (tt % 2)
                for hh in range(2):
                    p_, pr = ps()
                    for q in range(4):
                        c = hh * 4 + q
                        em.op('pe', 'transpose', reads=[xr, 'IDF'], writes=[pr], out=p_[:, q * 128:(q + 1) * 128],
                              in_=xl[:, c * 128:(c + 1) * 128], identity=IDF[:])
                    em.op('act' if hh == 0 else 'dve', 'activation' if hh == 0 else 'tensor_copy',
                          reads=[pr], writes=[xtr + 'h%d' % hh], out=xt[:, hh * 512:(hh + 1) * 512], in_=p_[:],
                          **({'func': AF.Copy} if hh == 0 else {}))
                em.op('sp', 'dma_start', reads=[xtr + 'h0', xtr + 'h1'],
                      writes=[('XSld', tt)], dma=True,
                      out=XS[:, tt * 128:(tt + 1) * 128].rearrange("(c p) t -> p c t", p=128),
                      in_=xt.rearrange("p (c t) -> p c t", t=128))

            POSI = arena.alloc(S, I32)
            POSF = arena.alloc(S, F32)
            VV = arena.alloc(S, F32)
            KI = arena.alloc(S, I32)
            KF = arena.alloc(S, F32)
            M1 = arena.alloc(S, F32)
            em.op('sp', 'dma_start', writes=['POSI'], dma=True, out=POSI,
                  in_=pos_d[b:b + 1, :].partition_broadcast(128).rearrange("p o s -> p (o s)"))
            em.op('dve', 'tensor_copy', reads=['POSI'], writes=['POSF'], out=POSF, in_=POSI)
            for ti, (fc, pc) in enumerate([(PC_FBC, PC_PBC), (PC_FBS, PC_PBS), (PC_FAC, PC_PAC), (PC_FAS, PC_PAS)]):
                em.op('dve', 'tensor_scalar', reads=['POSF', 'PCOL'], writes=['VV'], out=VV, in0=POSF,
                      scalar1=PCOL[:, fc:fc + 1], scalar2=PCOL[:, pc:pc + 1], op0=ALU.mult, op1=ALU.add)
                em.op('dve', 'tensor_copy', reads=['VV'], writes=['KI'], out=KI, in_=VV)
                em.op('dve', 'tensor_copy', reads=['KI'], writes=['KF'], out=KF, in_=KI)
                em.op('dve', 'tensor_tensor', reads=['VV', 'KF'], writes=['VV'], out=VV, in0=VV, in1=KF, op=ALU.subtract)
                em.op('dve', 'tensor_scalar', reads=['VV'], writes=['M1'], out=M1, in0=VV, scalar1=0.5, scalar2=None, op0=ALU.is_gt)
                em.op('dve', 'tensor_tensor', reads=['VV', 'M1'], writes=['VV'], out=VV, in0=VV, in1=M1, op=ALU.subtract)
                em.op('dve', 'tensor_scalar', reads=['VV'], writes=['M1'], out=M1, in0=VV, scalar1=-0.5, scalar2=None, op0=ALU.is_lt)
                em.op('dve', 'tensor_tensor', reads=['VV', 'M1'], writes=['VV'], out=VV, in0=VV, in1=M1, op=ALU.add)
                em.op('act', 'activation', reads=['VV'], writes=[('TAB', ti)], out=TAB[:, ti * S:(ti + 1) * S],
                      in_=VV, func=AF.Sin, scale=6.283185)
            em.barrier()

            def tabv(ti, r0, r1, c0, c1):
                return TAB[r0:r1, ti * S + c0:ti * S + c1]

            for l in range(NL):
                def md8(k):
                    o = (l * 48 + k * 8) * NSEQ + b
                    return MODS[:, o:o + 8 * NSEQ].rearrange("p (c s) -> p c s", s=NSEQ)[:, :, 0:1].rearrange("p c s -> p (c s)")
                em.op('dve', 'scalar_tensor_tensor', reads=['MODS', 'VEC'], writes=['AB'], out=AB[:, 0:8], in0=md8(1),
                      scalar=1.0, in1=V(l, 'mixg', 0, 8), op0=ALU.add, op1=ALU.mult)
                em.op('dve', 'scalar_tensor_tensor', reads=['MODS', 'VEC'], writes=['AB'], out=AB[:, 8:16], in0=md8(4),
                      scalar=1.0, in1=V(l, 'ffng', 0, 8), op0=ALU.add, op1=ALU.mult)

                def norm_phase(acol, shk):
                    arena.reset()
                    XLn = [arena.alloc(KC * TT, F32) for _ in range(2)]
                    for t in range(NT):
                        xl = XLn[t % 2]
                        xr = 'XLn%d' % (t % 2)
                        em.op('sp', 'dma_start', reads=[('XS', c, t) for c in range(KC)], writes=[xr], dma=True,
                              out=xl.rearrange("p (c t) -> p c t", t=TT),
                              in_=XS[:, t * TT:(t + 1) * TT].rearrange("(c p) t -> p c t", p=128))
                        pss, pssr = ps()
                        for c in range(KC):
                            sq, sqr = tb()
                            em.op('act', 'activation', reads=[xr], writes=[sqr], out=sq[:], in_=xl[:, c * TT:(c + 1) * TT], func=AF.Square)
                            mm(pss[:], cm(CM_ONES), sq[:], c == 0, c == KC - 1, [sqr, 'CM'], [pssr])
                        rs, rsr = rstd_from(pss[:], (0, 128), 1.0 / D, pssr, long=True)
                        for c in range(KC):
                            t1, t1r = tf()
                            em.op('dve', 'scalar_tensor_tensor', reads=[xr, rsr, 'AB'], writes=[t1r], out=t1[:],
                                  in0=xl[:, c * TT:(c + 1) * TT], scalar=AB[:, acol + c:acol + c + 1], in1=rs[:],
                                  op0=ALU.mult, op1=ALU.mult)
                            em.op('act', 'activation', reads=[t1r, 'MODS'], writes=[('HT', c, t)],
                                  out=HT[:, c * S + t * TT:c * S + (t + 1) * TT], in_=t1[:], func=AF.Identity,
                                  bias=MD(l, shk, c, b), scale=1.0)
                    em.barrier()

                def hT(kc, t):
                    return HT[:, kc * S + t * TT:kc * S + (t + 1) * TT], ('HT', kc, t)

                def dense(view, p, kcn, ncols, rhs_fn, out_fn, msub=128, tiles=range(NT)):
                    w, wr = wload(view, p, kcn, ncols)
                    for j in range((ncols + msub - 1) // msub):
                        m0, m1 = j * msub, min(ncols, (j + 1) * msub)
                        for t in tiles:
                            p_, pr = ps()
                            for kc in range(kcn):
                                ra, rr = rhs_fn(kc, t)
                                mm(p_[0:m1 - m0, :], w[:, kc, m0:m1], ra, kc == 0, kc == kcn - 1, [wr, rr], [pr])
                            out_fn(j, t, p_, pr)

                def win_view(c0, c1):
                    return win_d[l].rearrange("(kc p) n -> p kc n", p=128)[:, :, c0:c1]

                evac_ctr = [0]

                def evac_copy(dst, src, reads, writes):
                    evac_ctr[0] += 1
                    if evac_ctr[0] % 2:
                        em.op('act', 'activation', reads=reads, writes=writes, out=dst, in_=src, func=AF.Copy)
                    else:
                        em.op('dve', 'tensor_copy', reads=reads, writes=writes, out=dst, in_=src)

                def finalize_o(accp, accr, h, qt):
                    os_, osr = tf()
                    em.op('dve', 'tensor_copy', reads=[accr], writes=[osr], out=os_[0:65, :], in_=accp[0:65, :])
                    l1, l1r = tf()
                    em.op('act', 'activation', reads=[osr], writes=[l1r], out=l1[64:65, :], in_=os_[64:65, :], func=AF.Ln)
                    rb, rbr = tb()
                    em.op('act', 'activation', reads=[l1r], writes=[rbr], out=rb[64:65, :], in_=l1[64:65, :], func=AF.Exp, scale=-1.0)
                    p_, pr = ps()
                    mm(p_[0:64, :], cm(CM_ONES, 64, 65, 0, 64), rb[64:65, :], True, True, [rbr, 'CM'], [pr])
                    ot, otr = tb()
                    em.op('dve', 'tensor_tensor', reads=[osr, pr], writes=[otr], out=ot[0:64, :], in0=os_[0:64, :], in1=p_[0:64, :], op=ALU.mult)
                    tg = getattr(attention, 'tag', None)
                    if tg and h == 0 and qt == 2:
                        dbg_dump(tg, ot[0:64, :], otr, 64)
                    em.op('sp', 'dma_start', reads=[otr], writes=[('OD', h, qt)], dma=True,
                          out=OD[(h % 2) * 64:(h % 2) * 64 + 64, (h // 2) * S + qt * TT:(h // 2) * S + (qt + 1) * TT], in_=ot[0:64, :])

                def attention(h, kf, qf, kdim, vap, scale, bias=None):
                    for qt in range(NT):
                        accp, accr = acc()
                        nkt = 4 * qt + 4
                        for kt in range(nkt):
                            j0 = max(0, kt - 4 * qt) * 128
                            p_, pr = sbank()
                            ka, kr = kf(kt)
                            qa, qr = qf(qt * TT + j0, (qt + 1) * TT)
                            mm(p_[:, j0:TT], ka, qa, True, bias is None, list(kr) + [qr], [pr])
                            if bias is not None:
                                n = kt // 2
                                mm(p_[:, j0:TT], IND[0:8, n * 128:(n + 1) * 128], bias[0][0:8, qt * TT + j0:(qt + 1) * TT],
                                   False, True, ['IND'] + [(bias[1], x) for x in range(qt * 4, qt * 4 + 4)], [pr])
                            pt, ptr = ptile()
                            em.op('act', 'activation', reads=[pr], writes=[ptr], out=pt[:, j0:TT], in_=p_[:, j0:TT], func=AF.Exp, scale=scale)
                            if kt >= 4 * qt:
                                em.op('dve', 'tensor_tensor', reads=[ptr, 'CM'], writes=[ptr], out=pt[:, j0:j0 + 128],
                                      in0=pt[:, j0:j0 + 128], in1=cm(CM_TRI), op=ALU.mult)
                            va, vr = vap(kt)
                            mm(accp[0:65, j0:TT], va, pt[:, j0:TT], kt == 0, kt == nkt - 1, [vr, ptr], [accr])
                        finalize_o(accp, accr, h, qt)

                def merge_phase(bi, first):
                    arena.reset()
                    if bi < 2:
                        OA = arena.alloc(4 * S)
                        for c4 in range(4):
                            for qt in range(NT):
                                em.op('sp', 'dma_start', reads=[('OD', 2 * c4, qt), ('OD', 2 * c4 + 1, qt)], writes=[('OA', c4, qt)], dma=True,
                                      out=OA[:, c4 * S + qt * TT:c4 * S + (qt + 1) * TT],
                                      in_=OD[:, c4 * S + qt * TT:c4 * S + (qt + 1) * TT])
                        wsrc = (wmo_d, wbo_d)[bi][l].rearrange("(k p) n -> p k n", p=128)
                        pp, pk = 128, 4

                        def prhs(kk, t):
                            return OA[:, kk * S + t * TT:kk * S + (t + 1) * TT], ('OA', kk, t)
                    else:
                        wsrc = wpw_d[l].rearrange("(k p) n -> p k n", p=128)
                        pp, pk = 128, 4
                        UCb = merge_phase.UC

                        def prhs(kk, t):
                            return UCb[:, kk * S + t * TT:kk * S + (t + 1) * TT], ('UC', kk, t)
                    for g in range(2):
                        wg, wgr = wload(win_view(3232 + bi * 1024 + g * 512, 3232 + bi * 1024 + (g + 1) * 512), 128, KC, 512)
                        wp, wpr = wload(wsrc[:, :, g * 512:(g + 1) * 512], pp, pk, 512)
                        for j in range(4):
                            c = g * 4 + j
                            for t in range(NT):
                                pg, pgr = ps()
                                for kc in range(KC):
                                    ra, rr = hT(kc, t)
                                    mm(pg[:], wg[:, kc, j * 128:(j + 1) * 128], ra, kc == 0, kc == KC - 1, [wgr, rr], [pgr])
                                pq, pqr = ps()
                                for kk in range(pk):
                                    ra, rr = prhs(kk, t)
                                    mm(pq[:], wp[0:pp, kk, j * 128:(j + 1) * 128], ra, kk == 0, kk == pk - 1, [wpr, rr], [pqr])
                                sg, sgr = tf()
                                em.op('act', 'activation', reads=[pgr], writes=[sgr], out=sg[:], in_=pg[:], func=AF.Sigmoid)
                                mdst = R1[:, c * S + t * TT:c * S + (t + 1) * TT]
                                if first:
                                    em.op('dve', 'tensor_tensor', reads=[pqr, sgr], writes=[('M', c, t)], out=mdst, in0=pq[:], in1=sg[:], op=ALU.mult)
                                else:
                                    t1, t1r = tf()
                                    if bi == 2:
                                        em.op('dve', 'scalar_tensor_tensor', reads=[pqr, sgr, 'VEC'], writes=[t1r], out=t1[:], in0=pq[:],
                                              scalar=V(l, 'bpw', c), in1=sg[:], op0=ALU.add, op1=ALU.mult)
                                    else:
                                        em.op('dve', 'tensor_tensor', reads=[pqr, sgr], writes=[t1r], out=t1[:], in0=pq[:], in1=sg[:], op=ALU.mult)
                                    em.op('pool', 'tensor_tensor', reads=[t1r, ('M', c, t)], writes=[('M', c, t)], out=mdst, in0=mdst, in1=t1[:], op=ALU.add)
                    em.barrier()

                def resid_phase(wview_fn, ngroups, gcols, kcn, rhs_fn, gk):
                    for g in range(ngroups):
                        w, wr = wload(wview_fn(g), 128, kcn, gcols)
                        for j in range(gcols // 128):
                            c = g * (gcols // 128) + j
                            for t in range(NT):
                                p_, pr = ps()
                                for kc in range(kcn):
                                    ra, rr = rhs_fn(kc, t)
                                    mm(p_[:], w[:, kc, j * 128:(j + 1) * 128], ra, kc == 0, kc == kcn - 1, [wr, rr], [pr])
                                xl, xlr = tf()
                                em.op('sp', 'dma_start', reads=[('XS', c, t)], writes=[xlr], dma=True, out=xl[:], in_=xs_view(c, t * TT, (t + 1) * TT))
                                xn, xnr = tf()
                                em.op('dve', 'scalar_tensor_tensor', reads=[pr, xlr, 'MODS'], writes=[xnr], out=xn[:], in0=p_[:],
                                      scalar=MD(l, gk, c, b), in1=xl[:], op0=ALU.mult, op1=ALU.add)
                                em.op('sp', 'dma_start', reads=[xnr], writes=[('XS', c, t)], dma=True, out=xs_view(c, t * TT, (t + 1) * TT), in_=xn[:])
                    em.barrier()

                norm_phase(0, 0)
                D0 = bool(dbg) and l == 0 and b == 0
                if D0:
                    dbg_dump('ht', HT[:, 0:512], ('HT', 0, 0))
                    dbg_dump('ht1', HT[:, S:S + 512], ('HT', 1, 0))
                    dbg_dump('ht7', HT[:, 7 * S + 1536:8 * S], ('HT', 7, 3))

                arena.reset()
                del tfx[:]
                tfx.extend([R1[:, i * 1024:(i + 1) * 1024].bitcast(F32) for i in range(TFX_MLA)])
                LAT = arena.alloc(6 * S)
                KPE = arena.alloc(S)
                VA = arena.alloc(16 * 8 * 65)
                QH = [arena.alloc(S) for _ in range(2)]
                KH = [arena.alloc(S) for _ in range(2)]

                def lat(c, t, r0=0, r1=128):
                    return LAT[r0:r1, c * S + t * TT:c * S + (t + 1) * TT]

                def lat_out(cbase, rows=128):
                    def f(j, t, p_, pr):
                        evac_copy(lat(cbase + j, t, 0, rows), p_[0:rows, :], [pr], [('LAT', cbase + j, t)])
                        if D0 and cbase + j == 0 and t == 0:
                            dbg_dump('zraw', lat(0, 0), ('LAT', 0, 0))
                    return f
                dense(win_view(0, 512), 128, KC, 512, hT, lat_out(0))
                dense(win_view(512, 640), 128, KC, 128, hT, lat_out(4))
                dense(win_view(576, 672), 128, KC, 96, hT, lat_out(5, 96), msub=96)
                for (c0, ncn, gname) in [(0, 3, 'qlg'), (3, 2, 'kvg')]:
                    for t in range(NT):
                        pss, pssr = ps()
                        for c in range(ncn):
                            sq, sqr = tb()
                            em.op('act', 'activation', reads=[('LAT', c0 + c, t)], writes=[sqr], out=sq[:], in_=lat(c0 + c, t), func=AF.Square)
                            mm(pss[:], cm(CM_ONES), sq[:], c == 0, c == ncn - 1, [sqr, 'CM'], [pssr])
                        rs, rsr = rstd_from(pss[:], (0, 128), 1.0 / (ncn * 128), pssr)
                        for c in range(ncn):
                            em.op('dve', 'scalar_tensor_tensor', reads=[('LAT', c0 + c, t), rsr, 'VEC'], writes=[('LAT', c0 + c, t)],
                                  out=lat(c0 + c, t), in0=lat(c0 + c, t), scalar=V(l, gname, c), in1=rs[:], op0=ALU.mult, op1=ALU.mult)

                def head_norm_rope(src_ps, src_res, rows, bdm, gcol, dst, dst_res, t, rope):
                    r0, r1 = rows
                    sq, sqr = tb()
                    em.op('act', 'activation', reads=[src_res], writes=[sqr], out=sq[r0:r1, :], in_=src_ps, func=AF.Square)
                    p2, p2r = ps()
                    mm(p2[r0:r1, :], cm(bdm, r0, r1, r0, r1), sq[r0:r1, :], True, True, [sqr, 'CM'], [p2r])
                    t1, t1r = tf()
                    em.op('act', 'activation', reads=[p2r, 'PCOL'], writes=[t1r], out=t1[r0:r1, :], in_=p2[r0:r1, :], func=AF.Ln,
                          scale=PCOL[r0:r1, PC_QSC:PC_QSC + 1] if bdm == CM_BDA else 1.0 / 64, bias=EPS)
                    rs, rsr = tf()
                    em.op('act', 'activation', reads=[t1r], writes=[rsr], out=rs[r0:r1, :], in_=t1[r0:r1, :], func=AF.Exp, scale=-0.5)
                    if not rope:
                        em.op('dve', 'scalar_tensor_tensor', reads=[src_res, rsr, 'VEC'], writes=[dst_res], out=dst, in0=src_ps,
                              scalar=gcol, in1=rs[r0:r1, :], op0=ALU.mult, op1=ALU.mult)
                        return
                    ctab, stab, psw = rope
                    qn, qnr = tb()
                    em.op('dve', 'scalar_tensor_tensor', reads=[src_res, rsr, 'VEC'], writes=[qnr], out=qn[r0:r1, :], in0=src_ps,
                          scalar=gcol, in1=rs[r0:r1, :], op0=ALU.mult, op1=ALU.mult)
                    p3, p3r = ps()
                    mm(p3[r0:r1, :], cm(psw, r0, r1, r0, r1), qn[r0:r1, :], True, True, [qnr, 'CM'], [p3r])
                    a1, a1r = tf()
                    em.op('dve', 'tensor_tensor', reads=[qnr, ('TAB', ctab)], writes=[a1r], out=a1[r0:r1, :], in0=qn[r0:r1, :],
                          in1=tabv(ctab, r0, r1, t * TT, (t + 1) * TT), op=ALU.mult)
                    a2, a2r = tf()
                    em.op('dve', 'tensor_tensor', reads=[p3r, ('TAB', stab)], writes=[a2r], out=a2[r0:r1, :], in0=p3[r0:r1, :],
                          in1=tabv(stab, r0, r1, t * TT, (t + 1) * TT), op=ALU.mult)
                    em.op('pool', 'tensor_tensor', reads=[a1r, a2r], writes=[dst_res], out=dst, in0=a1[r0:r1, :], in1=a2[r0:r1, :], op=ALU.add)

                if D0:
                    dbg_dump('qn', lat(0, 0), ('LAT', 0, 0))
                    dbg_dump('kvn', lat(3, 0), ('LAT', 3, 0))
                for t in range(NT):
                    head_norm_rope(lat(5, t, 64, 96), ('LAT', 5, t), (64, 96), CM_BDA, V(l, 'gk', 0, 1, 64, 96),
                                   KPE[64:96, t * TT:(t + 1) * TT], ('KPE', t), t, (2, 3, CM_PSWA))
                if D0:
                    dbg_dump('kpe', KPE[64:96, 0:512], ('KPE', 0), 32)
                em.op('pool', 'memset', writes=['VAones'], ap=VA.rearrange("p (n d) -> p n d", d=65)[:, :, 64:65], constant=1.0)
                wkv, wkvr = wload(wukv_d[l].rearrange("(kc p) n -> p kc n", p=128), 128, 2, 1024)
                for tt in range(16):
                    p_, pr = ps()
                    for kc in range(2):
                        mm(p_[:].rearrange("p (h d) -> p h d", d=64), LAT[:, (3 + kc) * S + tt * 128:(3 + kc) * S + (tt + 1) * 128],
                           wkv[:, kc, :].rearrange("p (h d) -> p h d", d=128)[:, :, 64:128], kc == 0, kc == 1,
                           [wkvr, ('LAT', 3 + kc, tt // 4)], [pr])
                    evac_copy(VA[:, tt * 520:(tt + 1) * 520].rearrange("p (h d) -> p h d", d=65)[:, :, 0:64],
                              p_[:].rearrange("p (h d) -> p h d", d=64), [pr, 'VAones'], [('VA', tt)])
                wq, wqr = wload(wuq_d[l].rearrange("(kc p) n -> p kc n", p=128), 128, 3, 768)
                cfg['split'] = True
                for h in range(8):
                    qh, kh = QH[h % 2], KH[h % 2]
                    qres, kres = 'QH%d' % (h % 2), 'KH%d' % (h % 2)
                    for t in range(NT):
                        p_, pr = ps()
                        for kc in range(3):
                            mm(p_[0:96, :], wq[:, kc, h * 96:(h + 1) * 96], lat(kc, t), kc == 0, kc == 2, [wqr, ('LAT', kc, t)], [pr])
                        head_norm_rope(p_[0:96, :], pr, (0, 96), CM_BDA, V(l, 'gq', 0, 1, 0, 96), qh[0:96, t * TT:(t + 1) * TT], (qres, t), t,
                                       (2, 3, CM_PSWA))
                        p_, pr = ps()
                        for kc in range(2):
                            mm(p_[0:64, :], wkv[:, kc, h * 128:h * 128 + 64], lat(3 + kc, t), kc == 0, kc == 1, [wkvr, ('LAT', 3 + kc, t)], [pr])
                        head_norm_rope(p_[0:64, :], pr, (0, 64), CM_BD64, V(l, 'gk', 0, 1, 0, 64), kh[0:64, t * TT:(t + 1) * TT], (kres, t, 'n'), t, None)
                        em.op('dve', 'tensor_copy', reads=[('KPE', t)], writes=[(kres, t, 'r')], out=kh[64:96, t * TT:(t + 1) * TT],
                              in_=KPE[64:96, t * TT:(t + 1) * TT])
                    if D0 and h == 0:
                        dbg_dump('qh0', qh[0:96, 0:512], (qres, 0), 96)
                        dbg_dump('kh0', kh[0:96, 0:512], (kres, 0, 'n'), 96)
                        dbg_dump('va0', VA[:, 0:512], ('VA', 0))
                    attention.tag = 'oa' if D0 else None
                    attention(h,
                              lambda kt, kh=kh, kres=kres: (kh[0:96, kt * 128:(kt + 1) * 128], [(kres, kt // 4, 'n'), (kres, kt // 4, 'r')]),
                              lambda c0, c1, qh=qh, qres=qres: (qh[0:96, c0:c1], (qres, c0 // TT)),
                              96,
                              lambda kt, h=h: (VA[:, kt * 520 + h * 65:kt * 520 + h * 65 + 65], ('VA', kt)),
                              96.0 ** -0.5)
                em.barrier()
                cfg['split'] = False
                del tfx[:]
                merge_phase(0, True)

                arena.reset()
                QKB = arena.alloc(8 * S)
                VB = arena.alloc(16 * 8 * 65)
                BIA = [arena.alloc(S) for _ in range(2)]
                KMH = arena.alloc(8)
                KML = arena.alloc(8)
                KMF = arena.alloc(8, F32)
                KMD = arena.alloc(8, F32)
                del tfx[:]
                tfx.extend([arena.alloc(512, F32) for _ in range(TFX_MOBA)])

                def qkb(c, t, r0=0, r1=128):
                    return QKB[r0:r1, c * S + t * TT:c * S + (t + 1) * TT]

                def qk_out(cbase):
                    def f(j, t, p_, pr):
                        c = cbase + j
                        head_norm_rope(p_[:], pr, (0, 128), CM_BD64, V(l, 'mq' if c < 4 else 'mk', 0, 1), qkb(c, t), ('QKB', c, t), t,
                                       (0, 1, CM_PSW64))
                    return f
                dense(win_view(672, 1184), 128, KC, 512, hT, qk_out(0))
                dense(win_view(1184, 1696), 128, KC, 512, hT, qk_out(4))
                em.op('pool', 'memset', writes=['VBones'], ap=VB.rearrange("p (n d) -> p n d", d=65)[:, :, 64:65], constant=1.0)
                wv, wvr = wload(win_view(1696, 2208), 128, KC, 512)
                for tt in range(16):
                    p_, pr = ps()
                    for kc in range(KC):
                        mm(p_[:], HT[:, kc * S + tt * 128:kc * S + (tt + 1) * 128], wv[:, kc, :], kc == 0, kc == KC - 1,
                           [wvr, ('HT', kc, tt // 4)], [pr])
                    evac_copy(VB[:, tt * 520:(tt + 1) * 520].rearrange("p (h d) -> p h d", d=65)[:, :, 0:64],
                              p_[:].rearrange("p (h d) -> p h d", d=64), [pr, 'VBones'], [('VB', tt)])
                cfg['split'] = True
                for h in range(8):
                    c = h // 2
                    r0 = (h % 2) * 64
                    r1 = r0 + 64
                    bia = BIA[h % 2]
                    bres = 'BIA%d' % (h % 2)
                    em.op('dve', 'tensor_reduce', reads=[('QKB', 4 + c, t) for t in range(NT)], writes=['KMF'], out=KMF[r0:r1, :],
                          in_=QKB[r0:r1, (4 + c) * S:(5 + c) * S].rearrange("p (n j) -> p n j", j=256), axis=AX.X, op=ALU.add)
                    em.op('dve', 'tensor_copy', reads=['KMF'], writes=['KMH'], out=KMH[r0:r1, :], in_=KMF[r0:r1, :])
                    em.op('dve', 'tensor_tensor', reads=['KMF', 'KMH'], writes=['KMD'], out=KMD[r0:r1, :], in0=KMF[r0:r1, :], in1=KMH[r0:r1, :], op=ALU.subtract)
                    em.op('dve', 'tensor_copy', reads=['KMD'], writes=['KML'], out=KML[r0:r1, :], in_=KMD[r0:r1, :])
                    for tt in range(16):
                        qb = tt // 2
                        sc = SM[:, 0:8]
                        bt = SM[:, 16:24]
                        m8 = SM[:, 32:40]
                        if qb >= 4:
                            p_, pr = ps()
                            mm(p_[:, 0:8], QKB[r0:r1, c * S + tt * 128:c * S + (tt + 1) * 128], KMH[r0:r1, :], True, False,
                               [('QKB', c, tt // 4), 'KMH'], [pr])
                            mm(p_[:, 0:8], QKB[r0:r1, c * S + tt * 128:c * S + (tt + 1) * 128], KML[r0:r1, :], False, True,
                               [('QKB', c, tt // 4), 'KML'], [pr])
                            em.op('dve', 'memset', writes=['SC'], ap=sc, constant=-1e30)
                            em.op('dve', 'tensor_copy', reads=[pr, 'SC'], writes=['SC'], out=sc[:, 0:qb], in_=p_[:, 0:qb])
                            em.op('dve', 'max', reads=['SC'], writes=['M8'], out=m8, in_=sc)
                            em.op('dve', 'tensor_scalar', reads=['SC', 'M8'], writes=['BT'], out=bt, in0=sc, scalar1=m8[:, 2:3], scalar2=None, op0=ALU.is_ge)
                            em.op('dve', 'tensor_scalar', reads=['BT'], writes=['BT'], out=bt, in0=bt, scalar1=1.0, scalar2=BIG, op0=ALU.subtract, op1=ALU.mult)
                            em.op('dve', 'memset', reads=['BT'], writes=['BT'], ap=bt[:, qb:qb + 1], constant=0.0)
                        else:
                            em.op('dve', 'memset', writes=['BT'], ap=bt, constant=-BIG)
                            em.op('dve', 'memset', reads=['BT'], writes=['BT'], ap=bt[:, 0:qb + 1], constant=0.0)
                        p2, p2r = ps()
                        em.op('pe', 'transpose', reads=['BT', 'IDF'], writes=[p2r], out=p2[0:8, 0:128], in_=bt, identity=IDF[:])
                        em.op('act', 'activation', reads=[p2r], writes=[(bres, tt)], out=bia[0:8, tt * 128:(tt + 1) * 128], in_=p2[0:8, 0:128], func=AF.Copy)
                    if D0 and h == 0:
                        dbg_dump('qb0', qkb(0, 2), ('QKB', 0, 2))
                        dbg_dump('kb0', qkb(4, 0), ('QKB', 4, 0))
                        dbg_dump('bia', bia[0:8, 1024:1536], (bres, 8), 8)
                    attention.tag = 'ob' if D0 else None
                    attention(h,
                              lambda kt, c=c, r0=r0, r1=r1: (QKB[r0:r1, (4 + c) * S + kt * 128:(4 + c) * S + (kt + 1) * 128], [('QKB', 4 + c, kt // 4)]),
                              lambda c0, c1, c=c, r0=r0, r1=r1: (QKB[r0:r1, c * S + c0:c * S + c1], ('QKB', c, c0 // TT)),
                              64,
                              lambda kt, h=h: (VB[:, kt * 520 + h * 65:kt * 520 + h * 65 + 65], ('VB', kt)),
                              0.125, bias=(bia, bres))
                em.barrier()
                cfg['split'] = False
                del tfx[:]
                merge_phase(1, False)

                arena.reset()
                UW = S + 32
                UU = arena.alloc(4 * UW)
                CV = arena.alloc(4 * S)
                UC = arena.alloc(4 * S)
                DG = [arena.alloc(31 * 128) for _ in range(2)]
                merge_phase.UC = UC
                for c in range(4):
                    em.op('pool', 'memset', writes=[('UUh', c)], ap=UU[:, c * UW:c * UW + 32], constant=0.0)
                    w, wr = ws()
                    w3 = w[:, 0:KC * 256].rearrange("p (k n) -> p k n", n=256)
                    em.op('pool', 'dma_start', writes=[wr], dma=True, out=w3[:, :, 0:128], in_=win_view(2208 + c * 128, 2208 + (c + 1) * 128))
                    em.op('pool', 'dma_start', reads=[wr], writes=[wr], dma=True, out=w3[:, :, 128:256], in_=win_view(2720 + c * 128, 2720 + (c + 1) * 128))
                    for t in range(NT):
                        pa, par = ps()
                        pg, pgr = ps()
                        for kc in range(KC):
                            ra, rr = hT(kc, t)
                            mm(pa[:], w3[:, kc, 0:128], ra, kc == 0, kc == KC - 1, [wr, rr], [par])
                        for kc in range(KC):
                            ra, rr = hT(kc, t)
                            mm(pg[:], w3[:, kc, 128:256], ra, kc == 0, kc == KC - 1, [wr, rr], [pgr])
                        sg, sgr = tf()
                        em.op('act', 'activation', reads=[pgr], writes=[sgr], out=sg[:], in_=pg[:], func=AF.Sigmoid)
                        em.op('dve', 'tensor_tensor', reads=[par, sgr], writes=[('UU', c, t)], out=UU[:, c * UW + 32 + t * TT:c * UW + 32 + (t + 1) * TT],
                              in0=pa[:], in1=sg[:], op=ALU.mult)
                for c in range(4):
                    dg = DG[c % 2]
                    dgr = 'DG%d' % (c % 2)
                    for k in range(31):
                        em.op('dve', 'tensor_scalar', reads=['CM', 'VEC'], writes=[(dgr, k)], out=dg[:, k * 128:(k + 1) * 128], in0=cm(CM_ID),
                              scalar1=V(l, 'cw', c * 31 + k), scalar2=None, op0=ALU.mult)
                    for t in range(NT):
                        p_, pr = ps()
                        for k in range(31):
                            o = c * UW + 2 + t * TT + k
                            rds = [(dgr, k), ('UU', c, t)]
                            if t > 0:
                                rds.append(('UU', c, t - 1))
                            else:
                                rds.append(('UUh', c))
                            mm(p_[:], dg[:, k * 128:(k + 1) * 128], UU[:, o:o + TT], k == 0, k == 30, rds, [pr])
                        em.op('act', 'activation', reads=[pr, 'VEC'], writes=[('CV', c, t)], out=CV[:, c * S + t * TT:c * S + (t + 1) * TT], in_=p_[:],
                              func=AF.Identity, bias=V(l, 'cb', c), scale=1.0)
                for t in range(NT):
                    pm, pmr = ps()
                    pq, pqr = ps()
                    for c in range(4):
                        cv = CV[:, c * S + t * TT:c * S + (t + 1) * TT]
                        mm(pm[:], cm(CM_ONES), cv, c == 0, c == 3, [('CV', c, t), 'CM'], [pmr])
                        sq, sqr = tb()
                        em.op('act', 'activation', reads=[('CV', c, t)], writes=[sqr], out=sq[:], in_=cv, func=AF.Square)
                        mm(pq[:], cm(CM_ONES), sq[:], c == 0, c == 3, [sqr, 'CM'], [pqr])
                    mean, meanr = tl()
                    em.op('act', 'activation', reads=[pmr], writes=[meanr], out=mean[:], in_=pm[:], func=AF.Copy, scale=1.0 / 512)
                    msq, msqr = tf()
                    em.op('dve', 'tensor_tensor', reads=[meanr], writes=[msqr], out=msq[:], in0=mean[:], in1=mean[:], op=ALU.mult)
                    var, varr = tf()
                    em.op('dve', 'scalar_tensor_tensor', reads=[pqr, msqr], writes=[varr], out=var[:], in0=pq[:], scalar=1.0 / 512, in1=msq[:],
                          op0=ALU.mult, op1=ALU.subtract)
                    rs, rsr = rstd_from(var[:], (0, 128), 1.0, varr, long=True)
                    for c in range(4):
                        cv = CV[:, c * S + t * TT:c * S + (t + 1) * TT]
                        d1, d1r = tf()
                        em.op('dve', 'tensor_tensor', reads=[('CV', c, t), meanr], writes=[d1r], out=d1[:], in0=cv, in1=mean[:], op=ALU.subtract)
                        d2, d2r = tf()
                        em.op('dve', 'tensor_tensor', reads=[d1r, rsr], writes=[d2r], out=d2[:], in0=d1[:], in1=rs[:], op=ALU.mult)
                        em.op('act', 'activation', reads=[d2r, 'VEC'], writes=[('UC', c, t)], out=UC[:, c * S + t * TT:c * S + (t + 1) * TT], in_=d2[:],
                              func=AF.Silu, scale=V(l, 'lng', c), bias=V(l, 'lnb', c))
                if D0:
                    dbg_dump('uu0', UU[:, 32:544], ('UU', 0, 0))
                    dbg_dump('cv0', CV[:, 0:512], ('CV', 0, 0))
                    dbg_dump('uc0', UC[:, 0:512], ('UC', 0, 0))
                em.barrier()
                merge_phase(2, False)
                if D0:
                    dbg_dump('m0', R1[:, 0:512], ('M', 0, 0))
                    em.barrier()

                arena.reset()
                XLn = [arena.alloc(KC * TT, F32) for _ in range(2)]
                wo = []
                for g in range(2):
                    wo.append(wload(wout_d[l].rearrange("(kc p) n -> p kc n", p=128)[:, :, g * 512:(g + 1) * 512], 128, KC, 512))
                for t in range(NT):
                    xb = XLn[t % 2]
                    xbr = 'XLn%d' % (t % 2)
                    pss, pssr = acc()
                    for c in range(KC):
                        w, wr = wo[c // 4]
                        j = c % 4
                        p_, pr = ps()
                        for kc in range(KC):
                            mm(p_[:], w[:, kc, j * 128:(j + 1) * 128], R1[:, kc * S + t * TT:kc * S + (t + 1) * TT], kc == 0, kc == KC - 1,
                               [wr, ('M', kc, t)], [pr])
                        xl, xlr = tf()
                        em.op('sp', 'dma_start', reads=[('XS', c, t)], writes=[xlr], dma=True, out=xl[:], in_=xs_view(c, t * TT, (t + 1) * TT))
                        dst = xb[:, c * TT:(c + 1) * TT]
                        em.op('dve', 'scalar_tensor_tensor', reads=[pr, xlr, 'MODS'], writes=[(xbr, c)], out=dst, in0=p_[:],
                              scalar=MD(l, 2, c, b), in1=xl[:], op0=ALU.mult, op1=ALU.add)
                        em.op('sp', 'dma_start', reads=[(xbr, c)], writes=[('XS', c, t)], dma=True, out=xs_view(c, t * TT, (t + 1) * TT), in_=dst)
                        sq, sqr = tb()
                        em.op('act', 'activation', reads=[(xbr, c)], writes=[sqr], out=sq[:], in_=dst, func=AF.Square)
                        mm(pss[:], cm(CM_ONES), sq[:], c == 0, c == KC - 1, [sqr, 'CM'], [pssr])
                    rs, rsr = rstd_from(pss[:], (0, 128), 1.0 / D, pssr, long=True)
                    for c in range(KC):
                        t1, t1r = tf()
                        em.op('dve', 'scalar_tensor_tensor', reads=[(xbr, c), rsr, 'AB'], writes=[t1r], out=t1[:],
                              in0=xb[:, c * TT:(c + 1) * TT], scalar=AB[:, 8 + c:8 + c + 1], in1=rs[:],
                              op0=ALU.mult, op1=ALU.mult)
                        em.op('act', 'activation', reads=[t1r, 'MODS'], writes=[('HT', c, t)],
                              out=HT[:, c * S + t * TT:c * S + (t + 1) * TT], in_=t1[:], func=AF.Identity,
                              bias=MD(l, 3, c, b), scale=1.0)
                em.barrier()

                for half in range(2):
                    arena.reset()
                    FA = arena.alloc(11 * S)
                    UAB = [[arena.alloc(S + 4) for _ in range(2)] for _ in range(2)]
                    DGF = [arena.alloc(6 * 128) for _ in range(2)]
                    for jj in range(11):
                        ca = half * 11 + jj
                        cb_ = 22 + ca
                        ua, ub = UAB[jj % 2]
                        uar, ubr = 'UA%d' % (jj % 2), 'UB%d' % (jj % 2)
                        dgf = DGF[jj % 2]
                        dgfr = 'DGF%d' % (jj % 2)
                        em.op('pool', 'memset', writes=[(uar, 'h')], ap=ua[:, 0:4], constant=0.0)
                        em.op('pool', 'memset', writes=[(ubr, 'h')], ap=ub[:, 0:4], constant=0.0)
                        for k in range(2):
                            em.op('dve', 'tensor_scalar', reads=['CM', 'VEC'], writes=[(dgfr, k)], out=dgf[:, k * 128:(k + 1) * 128], in0=cm(CM_ID),
                                  scalar1=V(l, 'fw', ca * 3 + k), scalar2=None, op0=ALU.mult)
                            em.op('dve', 'tensor_scalar', reads=['CM', 'VEC'], writes=[(dgfr, 3 + k)], out=dgf[:, (3 + k) * 128:(4 + k) * 128], in0=cm(CM_ID),
                                  scalar1=V(l, 'fw', cb_ * 3 + k), scalar2=None, op0=ALU.mult)
                        w, wr = ws()
                        w3 = w[:, 0:KC * 256].rearrange("p (k n) -> p k n", n=256)
                        wup_v = wup_d[l].rearrange("(kc p) n -> p kc n", p=128)
                        em.op('pool', 'dma_start', writes=[wr], dma=True, out=w3[:, :, 0:128], in_=wup_v[:, :, ca * 128:(ca + 1) * 128])
                        em.op('pool', 'dma_start', reads=[wr], writes=[wr], dma=True, out=w3[:, :, 128:256], in_=wup_v[:, :, cb_ * 128:(cb_ + 1) * 128])
                        for t in range(NT):
                            pa, par = ps()
                            pb, pbr = ps()
                            for kc in range(KC):
                                ra, rr = hT(kc, t)
                                mm(pa[:], w3[:, kc, 0:128], ra, kc == 0, kc == KC - 1, [wr, rr], [par])
                            for kc in range(KC):
                                ra, rr = hT(kc, t)
                                mm(pb[:], w3[:, kc, 128:256], ra, kc == 0, kc == KC - 1, [wr, rr], [pbr])
                            em.op('act', 'activation', reads=[par], writes=[(uar, t)], out=ua[:, 4 + t * TT:4 + (t + 1) * TT], in_=pa[:], func=AF.Copy)
                            em.op('dve', 'tensor_copy', reads=[pbr], writes=[(ubr, t)], out=ub[:, 4 + t * TT:4 + (t + 1) * TT], in_=pb[:])
                        for t in range(NT):
                            pa, par = ps()
                            pb, pbr = ps()
                            for (pp_, ppr, u, ur, k0) in [(pa, par, ua, uar, 0), (pb, pbr, ub, ubr, 3)]:
                                for k in range(2):
                                    o = 2 + t * TT + k
                                    rds = [(dgfr, k0 + k), (ur, t), (ur, t - 1) if t > 0 else (ur, 'h')]
                                    mm(pp_[:], dgf[:, (k0 + k) * 128:(k0 + k + 1) * 128], u[:, o:o + TT], k == 0, k == 1, rds, [ppr])
                            ta, tar = tf()
                            em.op('dve', 'scalar_tensor_tensor', reads=[(uar, t), par, 'VEC'], writes=[tar], out=ta[:], in0=ua[:, 4 + t * TT:4 + (t + 1) * TT],
                                  scalar=V(l, 'fw', ca * 3 + 2), in1=pa[:], op0=ALU.mult, op1=ALU.add)
                            aa, aar = tb()
                            em.op('act', 'activation', reads=[tar, 'VEC'], writes=[aar], out=aa[:], in_=ta[:], func=AF.Silu, bias=V(l, 'fb', ca), scale=1.0)
                            tb2, tb2r = tf()
                            em.op('dve', 'scalar_tensor_tensor', reads=[(ubr, t), pbr, 'VEC'], writes=[tb2r], out=tb2[:], in0=ub[:, 4 + t * TT:4 + (t + 1) * TT],
                                  scalar=V(l, 'fw', cb_ * 3 + 2), in1=pb[:], op0=ALU.mult, op1=ALU.add)
                            em.op('dve', 'scalar_tensor_tensor', reads=[tb2r, aar, 'VEC'], writes=[('FA', jj, t)], out=FA[:, jj * S + t * TT:jj * S + (t + 1) * TT],
                                  in0=tb2[:], scalar=V(l, 'fb', cb_), in1=aa[:], op0=ALU.add, op1=ALU.mult)
                    resid_phase(lambda g, half=half: wdn_d[l][half * 1408:(half + 1) * 1408, :].rearrange("(k p) n -> p k n", p=128)[:, :, g * 256:(g + 1) * 256],
                                4, 256, 11, lambda kk, t: (FA[:, kk * S + t * TT:kk * S + (t + 1) * TT], ('FA', kk, t)), 5)

            arena.reset()
            XL = [arena.alloc(D, F32) for _ in range(2)]
            XT_ = [arena.alloc(D, F32) for _ in range(2)]
            for tt in range(16):
                xl = XL[tt % 2]
                xr = 'XL%d' % (tt % 2)
                em.op('sp', 'dma_start', writes=[xr], dma=True, out=xl.rearrange("p (c t) -> p c t", t=128),
                      in_=XS[:, tt * 128:(tt + 1) * 128].rearrange("(c p) t -> p c t", p=128))
                xt = XT_[tt % 2]
                xtr = 'XT%d' % (tt % 2)
                for hh in range(2):
                    p_, pr = ps()
                    for q in range(4):
                        c = hh * 4 + q
                        em.op('pe', 'transpose', reads=[xr, 'IDF'], writes=[pr], out=p_[:, q * 128:(q + 1) * 128],
                              in_=xl[:, c * 128:(c + 1) * 128], identity=IDF[:])
                    em.op('act' if hh == 0 else 'dve', 'activation' if hh == 0 else 'tensor_copy',
                          reads=[pr], writes=[xtr + 'h%d' % hh], out=xt[:, hh * 512:(hh + 1) * 512], in_=p_[:],
                          **({'func': AF.Copy} if hh == 0 else {}))
                em.op('sp', 'dma_start', reads=[xtr + 'h0', xtr + 'h1'], writes=[('OUT', tt)], dma=True,
                      out=out_d[b, tt * 128:(tt + 1) * 128, :], in_=xt)
            em.barrier()

        em.emit(st)
    return nc


def _consts():
    cmat = np.zeros((128, NCM * 128), np.float32)
    i = np.arange(128)
    cmat[:, CM_ID * 128:(CM_ID + 1) * 128] = np.eye(128)
    cmat[:, CM_ONES * 128:(CM_ONES + 1) * 128] = 1.0
    bd = (i[:, None] // 64 == i[None, :] // 64).astype(np.float32)
    cmat[:, CM_BD64 * 128:(CM_BD64 + 1) * 128] = bd
    grp = np.where(i < 64, 0, np.where(i < 96, 1, 2))
    cmat[:, CM_BDA * 128:(CM_BDA + 1) * 128] = (grp[:, None] == grp[None, :]).astype(np.float32)
    pi64 = (i // 64) * 64 + ((i % 64) + 32) % 64
    m = np.zeros((128, 128), np.float32)
    m[pi64, i] = 1.0
    cmat[:, CM_PSW64 * 128:(CM_PSW64 + 1) * 128] = m
    pia = i.copy()
    for j in range(64, 96):
        pia[j] = 64 + ((j - 64) + 16) % 32
    m = np.zeros((128, 128), np.float32)
    m[pia, i] = 1.0
    cmat[:, CM_PSWA * 128:(CM_PSWA + 1) * 128] = m
    cmat[:, CM_TRI * 128:(CM_TRI + 1) * 128] = (i[:, None] <= i[None, :]).astype(np.float32)
    ind = np.zeros((8, 1024), np.float32)
    for n in range(8):
        ind[n, n * 128:(n + 1) * 128] = 1.0
    pcol = np.zeros((128, NPC), np.float32)
    invb = 1.0 / (10000.0 ** (np.arange(0, 64, 2, dtype=np.float32) / 64.0))
    inva = 1.0 / (10000.0 ** (np.arange(0, 32, 2, dtype=np.float32) / 32.0))
    tw = 2.0 * math.pi
    fb = invb[i % 32]
    sgn = np.where((i % 64) < 32, -1.0, 1.0)
    pcol[:, PC_FBS] = sgn * fb / tw
    pcol[:, PC_PBS] = 0.0
    pcol[:, PC_FBC] = fb / tw
    pcol[:, PC_PBC] = 0.25
    fa = np.zeros(128)
    sa = np.zeros(128)
    for j in range(64, 96):
        fa[j] = inva[(j - 64) % 16]
        sa[j] = -1.0 if j < 80 else 1.0
    pcol[:, PC_FAS] = sa * fa / tw
    pcol[:, PC_PAS] = 0.0
    pcol[:, PC_FAC] = fa / tw
    pcol[:, PC_PAC] = 0.25
    pcol[:, PC_QSC] = np.where(i < 64, 1.0 / 64, 1.0 / 32)
    return cmat, ind, np.eye(128, dtype=np.float32), pcol


def _vecpack(inp, NL):
    vec = np.zeros((NL, 128, NV), np.float32)

    def put(l, name, arr2d):
        o = VOFF[name]
        vec[l, :arr2d.shape[0], o:o + arr2d.shape[1]] = arr2d
    for l in range(NL):
        put(l, 'mixg', inp['mix_norm_g'][l].reshape(8, 128).T)
        put(l, 'ffng', inp['ffn_norm_g'][l].reshape(8, 128).T)
        put(l, 'bmod', inp['b_mod'][l].reshape(48, 128).T)
        put(l, 'qlg', inp['mla_q_norm_g'][l].reshape(3, 128).T)
        put(l, 'kvg', inp['mla_kv_norm_g'][l].reshape(2, 128).T)
        put(l, 'gq', np.concatenate([inp['mla_qn_nope_g'][l], inp['mla_qn_rope_g'][l]])[:, None])
        put(l, 'gk', np.concatenate([inp['mla_kn_nope_g'][l], inp['mla_kn_rope_g'][l]])[:, None])
        put(l, 'mq', np.tile(inp['moba_qn_g'][l], 2)[:, None])
        put(l, 'mk', np.tile(inp['moba_kn_g'][l], 2)[:, None])
        put(l, 'cw', inp['conv_dw_w'][l].reshape(31, 4, 128).transpose(2, 1, 0).reshape(128, 124))
        put(l, 'cb', inp['conv_dw_b'][l].reshape(4, 128).T)
        put(l, 'lng', inp['conv_ln_g'][l].reshape(4, 128).T)
        put(l, 'lnb', inp['conv_ln_b'][l].reshape(4, 128).T)
        put(l, 'bpw', inp['b_conv_pw2'][l].reshape(8, 128).T)
        put(l, 'fw', inp['ffn_dw_w'][l].reshape(3, 44, 128).transpose(2, 1, 0).reshape(128, 132))
        put(l, 'fb', inp['ffn_dw_b'][l].reshape(44, 128).T)
    return vec


_WNAMES = ['w_mod', 'w_in', 'w_uq', 'w_ukv', 'w_mla_o', 'w_moba_o', 'w_conv_pw2', 'w_out', 'w_up', 'w_down']


def make_in_maps(inp, ncores, nseq, NL):
    cmat, ind, idf, pcol = _consts()
    vec = _vecpack(inp, NL)
    shared = {n: np.ascontiguousarray(np.asarray(inp[n], np.float32)[:NL]) for n in _WNAMES}
    shared.update(vec=vec, cmat=cmat, ind=ind, identf=idf, pcol=pcol)
    maps = []
    for i in range(ncores):
        b0 = i * nseq
        c = np.asarray(inp['c'], np.float32)[b0:b0 + nseq]
        cT = np.ascontiguousarray(c.reshape(nseq, 8, 128).transpose(2, 1, 0).reshape(128, 8 * nseq))
        m = dict(shared)
        m['x'] = np.ascontiguousarray(np.asarray(inp['x'], np.float32)[b0:b0 + nseq])
        m['cT'] = cT
        m['pos'] = np.ascontiguousarray(np.asarray(inp['positions'], np.int32)[b0:b0 + nseq])
        maps.append(m)
    return maps


def kernel(**inputs):
    nc = build_program(2, 2)
    maps = make_in_maps(inputs, NCORE, 2, 2)
    res = run_bass_kernel_spmd(nc, maps, core_ids=list(range(NCORE)))
    return np.concatenate([np.asarray(r["out"], np.float32) for r in res.results], axis=0)
```

```python
import math
import numpy as np
import concourse.bass as bass
import concourse.mybir as mybir
from contextlib import ExitStack
from concourse.bass_utils import run_bass_kernel_spmd

F32 = mybir.dt.float32
BF16 = mybir.dt.bfloat16
I32 = mybir.dt.int32
AF = mybir.ActivationFunctionType
ALU = mybir.AluOpType
AX = mybir.AxisListType

S = 2048
TT = 512
NT = 4
D = 1024
KC = 8
DIN = 6304
DFF = 2816
EPS = 1e-6
NCORE = 8
BIG = 30000.0
FFN_DVE_TAPS = 2
CP_SLACK = 50.0
NACC = 1
NSB = 3
TFX_MLA = 8
TFX_MOBA = 4
PROF = False
CRIT = None


class Em:
    ENGS = ('pe', 'act', 'dve', 'pool', 'sp')
    NDMA = 12

    def __init__(self, nc):
        self.nc = nc
        self.ops = []
        self.lastw = {}
        self.readers = {}
        self.per_eng = {e: [] for e in self.ENGS}
        self.bars = []

    def op(self, eng, name, reads=(), writes=(), dma=False, **kw):
        i = len(self.ops)
        deps = set()
        for r in list(reads) + list(writes):
            w = self.lastw.get(r)
            if w is not None:
                deps.add(w)
        for w in writes:
            for rd in self.readers.get(w, ()):
                deps.add(rd)
        for w in writes:
            self.lastw[w] = i
            self.readers[w] = []
        for r in reads:
            self.readers.setdefault(r, []).append(i)
        deps.discard(i)
        self.ops.append(dict(eng=eng, name=name, kw=kw, deps=deps, dma=dma,
                             pos=len(self.per_eng[eng])))
        self.per_eng[eng].append(i)
        return i

    def barrier(self):
        self.bars.append(len(self.ops))
        self.lastw = {k: v for k, v in self.lastw.items() if isinstance(k, str) and k.startswith('WS')}
        self.readers = {k: v for k, v in self.readers.items() if isinstance(k, str) and k.startswith('WS')}

    @staticmethod
    def _fsz(ap):
        sh = ap.shape
        n = 1
        for d in sh[1:]:
            n *= d
        return n

    def _cost(self, o):
        kw = o['kw']
        if o['dma']:
            ap = kw['out']
            nbytes = self._fsz(ap) * ap.shape[0] * 4
            occ = 1100.0 if o['eng'] == 'pool' else 150.0
            return occ, 2500.0 + nbytes / 100.0
        e = o['eng']
        if e == 'pe':
            nn = self._fsz(kw['rhs']) if 'rhs' in kw else 128
            t = max(64, nn) / 2.4 + 25.0
            return t, t + 200.0
        ap = kw.get('out', kw.get('ap'))
        nn = self._fsz(ap)
        if e == 'act':
            t = 200.0 + nn / 1.3
        elif e == 'dve':
            t = 150.0 + nn * (1.15 if o['name'] == 'scalar_tensor_tensor' else 1.0)
        else:
            t = 300.0 + nn * 2.0 if o['name'] != 'memset' else 100.0 + nn * 0.2
        return t, t + 250.0

    def schedule(self):
        import bisect
        ops = self.ops
        n = len(ops)
        bars = sorted(self.bars)
        seg = [bisect.bisect_right(bars, i) for i in range(n)]
        nseg = (seg[-1] if n else 0) + 1
        users = [[] for _ in range(n)]
        for i, o in enumerate(ops):
            for d in o['deps']:
                users[d].append(i)
        ndl = [len(o['deps']) for o in ops]
        hoist = [o['dma'] and o['eng'] == 'pool' for o in ops]
        ready = [0.0] * n
        sched = [False] * n
        costs = [self._cost(o) for o in ops]
        tail = [0.0] * n
        for i in range(n - 1, -1, -1):
            t = 0.0
            for u in users[i]:
                if seg[u] == seg[i] and tail[u] > t:
                    t = tail[u]
            tail[i] = t + costs[i][1]
        SLACK = CP_SLACK
        segrem = [0] * (nseg + 1)
        for i in range(n):
            segrem[seg[i]] += 1
        segfin = [0.0] * (nseg + 1)
        self._why = {}
        self._st = {}
        self._fin = [0.0] * n
        head = {e: 0 for e in self.ENGS}
        free = {e: 0.0 for e in self.ENGS}
        order = {e: [] for e in self.ENGS}
        WIN = {'pe': 500, 'act': 160, 'dve': 160, 'pool': 80, 'sp': 80}
        segdone_upto = 0
        segfin_cum = [0.0] * (nseg + 2)
        for _ in range(n):
            while segdone_upto < nseg and segrem[segdone_upto] == 0:
                segfin_cum[segdone_upto + 1] = max(segfin_cum[segdone_upto], segfin[segdone_upto])
                segdone_upto += 1
            best = None
            for e in self.ENGS:
                lst = self.per_eng[e]
                h = head[e]
                L = len(lst)
                while h < L and sched[lst[h]]:
                    h += 1
                head[e] = h
                cnt = 0
                k = h
                fe = free[e]
                ebest = None
                W = WIN[e]
                while k < L and cnt < W:
                    i = lst[k]
                    k += 1
                    if sched[i]:
                        continue
                    cnt += 1
                    if ndl[i] > 0:
                        continue
                    st = ready[i]
                    if not hoist[i]:
                        sg = seg[i]
                        if sg > segdone_upto:
                            break
                        if segfin_cum[sg] > st:
                            st = segfin_cum[sg]
                    if fe > st:
                        st = fe
                    if ebest is None or st < ebest[0] - SLACK or (st <= ebest[0] + SLACK and tail[i] > ebest[3]):
                        ebest = (st, e, i, tail[i])
                if ebest is not None and (best is None or ebest[0] < best[0] or (ebest[0] == best[0] and ebest[2] < best[2])):
                    best = ebest
            st, e, i = best[0], best[1], best[2]
            occ, lat = costs[i]
            if getattr(self, 'prof', None) is not None:
                why = ('eng', order[e][-1] if order[e] else None) if (free[e] >= st - 1e-9 and order[e]) else None
                if why is None:
                    bd = None
                    for d in ops[i]['deps']:
                        if self._fin[d] >= st - 1e-9:
                            bd = d
                    why = ('dep', bd)
                self._why[i] = why
                self._st[i] = st
                self._fin[i] = st + lat
            free[e] = st + occ
            f = st + lat
            sched[i] = True
            order[e].append(i)
            for u in users[i]:
                ndl[u] -= 1
                if f > ready[u]:
                    ready[u] = f
            sg = seg[i]
            segrem[sg] -= 1
            if f > segfin[sg]:
                segfin[sg] = f
            if getattr(self, 'prof', None) is not None:
                self.prof.setdefault(sg, {}).setdefault(e, [0.0, 0])
                self.prof[sg][e][0] += occ
                self.prof[sg][e][1] += 1
        self.per_eng = order
        for e in self.ENGS:
            for p, i in enumerate(order[e]):
                ops[i]['pos'] = p
        self.seg = seg
        self.hoist = hoist
        self.est_total = max(free.values())
        if getattr(self, 'prof', None) is not None and getattr(self, 'crit_seg', None) is not None:
            cs = self.crit_seg
            last = max((i for i in range(n) if seg[i] == cs), key=lambda i: self._fin[i])
            i = last
            stats = {}
            chain = []
            while i is not None and seg[i] == cs:
                kind, j = self._why[i]
                key = (kind, ops[i]['eng'], ops[i]['name'])
                t_prev = self._st[j] if (j is not None and j in self._st) else self._st[i]
                stats[key] = stats.get(key, 0.0) + (self._st[i] - t_prev)
                chain.append((i, ops[i]['eng'], ops[i]['name'], kind, round(self._st[i] / 1e3, 2)))
                i = j
            print('critical path seg', cs, 'len', len(chain))
            for k, v in sorted(stats.items(), key=lambda kv: -kv[1])[:14]:
                print('   %-40s %8.1f us' % (str(k), v / 1e3))
            print('   tail of chain:', chain[:40])
        if getattr(self, 'prof', None) is not None:
            prev = 0.0
            for sg in range(nseg):
                end = max(prev, segfin[sg])
                d = end - prev
                if d > 0:
                    print('seg %3d dur %8.1f us ' % (sg, d / 1e3) + ' '.join('%s %3d%%(%d)' % (e, 100 * self.prof.get(sg, {}).get(e, [0, 0])[0] / d, self.prof.get(sg, {}).get(e, [0, 0])[1]) for e in self.ENGS))
                prev = end
        print('[em] ops', n, {e: len(order[e]) for e in self.ENGS}, 'est_ms %.3f' % (self.est_total / 1e6), flush=True)

    def emit(self, stack):
        nc = self.nc
        ops = self.ops
        n = len(ops)
        self.schedule()
        seg, hoist = self.seg, self.hoist
        nseg = (max(seg) if n else 0) + 1
        seg_dmas = [[] for _ in range(nseg + 1)]
        for i, o in enumerate(ops):
            if o['dma'] and not hoist[i]:
                seg_dmas[seg[i]].append(i)
        last_comp = {}
        for e in self.ENGS:
            cur = {}
            for i in self.per_eng[e]:
                if not ops[i]['dma']:
                    cur[seg[i]] = i
            last_comp[e] = cur
        for e in self.ENGS:
            covered = 0
            for i in self.per_eng[e]:
                if hoist[i]:
                    continue
                sg = seg[i]
                if sg > covered:
                    for e2 in self.ENGS:
                        lc = last_comp[e2]
                        for s2 in range(covered, sg):
                            if s2 in lc:
                                ops[i]['deps'].add(lc[s2])
                    for s2 in range(covered, sg):
                        for j in seg_dmas[s2]:
                            ops[i]['deps'].add(j)
                    covered = sg
        need = [False] * n
        for i, o in enumerate(ops):
            for d in o['deps']:
                od = ops[d]
                if od['dma'] or od['eng'] != o['eng']:
                    need[d] = True
                elif od['eng'] == 'pe':
                    pass
                elif o['pos'] - od['pos'] <= 3:
                    need[d] = True
        sems = {e: stack.enter_context(nc.semaphore('s_' + e)) for e in self.ENGS}
        dsems = {e: [stack.enter_context(nc.semaphore('d_%s_%d' % (e, k))) for k in range(self.NDMA)]
                 for e in ('sp', 'pool', 'act')}
        cnt = {e: 0 for e in self.ENGS}
        dcnt = {e: [0] * self.NDMA for e in dsems}
        dnum = {e: 0 for e in dsems}
        dprev = {}
        for e in self.ENGS:
            for i in self.per_eng[e]:
                o = ops[i]
                if o['dma']:
                    k = dnum[e] % self.NDMA
                    dnum[e] += 1
                    dcnt[e][k] += 16
                    o['sig'] = (dsems[e][k], dcnt[e][k])
                    pk = (e, k)
                    if pk in dprev:
                        o['deps'].add(dprev[pk])
                    dprev[pk] = i
                elif need[i]:
                    cnt[e] += 1
                    o['sig'] = (sems[e], cnt[e])
                else:
                    o['sig'] = None
        block = stack.enter_context(nc.Block())
        em = self

        def run(ename):
            def body(engine):
                seen = {}
                for i in em.per_eng[ename]:
                    o = ops[i]
                    waits = {}
                    for d in o['deps']:
                        od = ops[d]
                        if od['sig'] is None:
                            continue
                        if (not od['dma']) and od['eng'] == ename and ename == 'pe':
                            continue
                        s, v = od['sig']
                        key = id(s)
                        if key not in waits or waits[key][1] < v:
                            waits[key] = (s, v)
                    for key, (s, v) in waits.items():
                        if seen.get(key, 0) >= v:
                            continue
                        seen[key] = v
                        engine.wait_ge(s, v)
                    try:
                        ins = getattr(engine, o['name'])(**o['kw'])
                    except Exception:
                        print('EMIT FAIL op', i, ename, o['name'], {k: (getattr(v, 'shape', v)) for k, v in o['kw'].items()})
                        raise
                    if o['sig'] is not None:
                        ins.then_inc(o['sig'][0], 16 if o['dma'] else 1)
                if ename in dsems:
                    for k in range(em.NDMA):
                        if dcnt[ename][k] > 0:
                            engine.wait_ge(dsems[ename][k], dcnt[ename][k])
            return body

        block.tensor(run('pe'))
        block.scalar(run('act'))
        block.vector(run('dve'))
        block.gpsimd(run('pool'))
        block.sync(run('sp'))


VOFF = {}
_o = 0
for _n, _w in [('mixg', 8), ('ffng', 8), ('bmod', 48), ('qlg', 3), ('kvg', 2), ('gq', 1), ('gk', 1),
               ('mq', 1), ('mk', 1), ('cw', 124), ('cb', 4), ('lng', 4), ('lnb', 4), ('bpw', 8),
               ('fw', 132), ('fb', 44)]:
    VOFF[_n] = _o
    _o += _w
NV = _o
CM_ID, CM_ONES, CM_BD64, CM_BDA, CM_PSW64, CM_PSWA, CM_TRI = range(7)
NCM = 7
PC_FBS, PC_PBS, PC_FBC, PC_PBC, PC_FAS, PC_PAS, PC_FAC, PC_PAC, PC_QSC = range(9)
NPC = 16


def build_program(NSEQ=2, NL=2, dbg=None):
    nc = bass.Bass("TRN2", target_bir_lowering=False)

    def din(name, shape, dt=F32):
        return nc.dram_tensor(name, list(shape), dt, kind="ExternalInput").ap()

    x_d = din("x", [NSEQ, S, D])
    cT_d = din("cT", [128, KC * NSEQ])
    pos_d = din("pos", [NSEQ, S], I32)
    wmod_d = din("w_mod", [NL, D, 6 * D])
    win_d = din("w_in", [NL, D, DIN])
    wuq_d = din("w_uq", [NL, 384, 768])
    wukv_d = din("w_ukv", [NL, 256, 1024])
    wmo_d = din("w_mla_o", [NL, 512, D])
    wbo_d = din("w_moba_o", [NL, 512, D])
    wpw_d = din("w_conv_pw2", [NL, 512, D])
    wout_d = din("w_out", [NL, D, D])
    wup_d = din("w_up", [NL, D, 2 * DFF])
    wdn_d = din("w_down", [NL, DFF, D])
    vec_d = din("vec", [NL, 128, NV])
    cmat_d = din("cmat", [128, NCM * 128])
    ind_d = din("ind", [8, 1024])
    idf_d = din("identf", [128, 128])
    pcol_d = din("pcol", [128, NPC])
    out_d = nc.dram_tensor("out", [NSEQ, S, D], F32, kind="ExternalOutput").ap()
    XS = nc.dram_tensor("xs_scr", [D, S], F32).ap()
    OD = nc.dram_tensor("od_scr", [128, 4 * S], BF16).ap()
    dbg_d = {}
    if dbg:
        for k, shp in dbg.items():
            dbg_d[k] = nc.dram_tensor("dbg_" + k, list(shp), F32, kind="ExternalOutput").ap()

    st = ExitStack()
    with st:
        def sb(name, shape, dt):
            return st.enter_context(nc.sbuf_tensor(name, list(shape), dt))

        VEC = sb("VEC", [128, NL * NV], F32)
        MODS = sb("MODS", [128, NL * 48 * NSEQ], F32)
        AB = sb("AB", [128, 16], F32)
        IDF = sb("IDF", [128, 128], F32)
        PCOL = sb("PCOL", [128, NPC], F32)
        CM = sb("CM", [128, NCM * 128], BF16)
        IND = sb("IND", [8, 1024], BF16)
        CACT = sb("CACT", [128, KC * NSEQ], BF16)
        CTF = sb("CTF", [128, KC * NSEQ], F32)
        NW = 3
        WS = [sb("WS%d" % i, [128, 4096], BF16) for i in range(NW)]
        NTF = 6
        TF = [sb("TF%d" % i, [128, 512], F32) for i in range(NTF)]
        NTB = 4
        TB = [sb("TB%d" % i, [128, 512], BF16) for i in range(NTB)]
        NTP = 4
        TP = [sb("TP%d" % i, [128, 512], BF16) for i in range(NTP)]
        NTL = 3
        TL = [sb("TL%d" % i, [128, 512], F32) for i in range(NTL)]
        HT = sb("HT", [128, KC * S], BF16)
        R1 = sb("R1", [128, KC * S], BF16)
        TAB = sb("TAB", [128, 4 * S], BF16)
        SM = sb("SM", [128, 64], F32)
        ARN = 33000
        AR = sb("AR", [128, ARN], BF16)
        PS = [st.enter_context(nc.psum_tensor("PS%d" % i, [128, 512], F32)) for i in range(8)]

        em = Em(nc)
        em.prof = {} if PROF else None
        em.crit_seg = CRIT
        ctr = {'w': 0, 'tf': 0, 'tb': 0, 'ps': 0, 'acc': 0, 'tl': 0, 'psg': 0, 'pss': 0, 'tp': 0}
        cfg = {'split': False}

        def cm(i, r0=0, r1=128, c0=0, c1=128):
            return CM[r0:r1, i * 128 + c0:i * 128 + c1]

        def nxt(kind, n):
            ctr[kind] = (ctr[kind] + 1) % n
            return ctr[kind]

        tfx = []

        def tf():
            i = nxt('tf', NTF + len(tfx))
            if i < NTF:
                return TF[i], 'TF%d' % i
            return tfx[i - NTF], 'TFX%d' % (i - NTF)

        def tl():
            i = nxt('tl', NTL)
            return TL[i], 'TL%d' % i

        def tb():
            i = nxt('tb', NTB)
            return TB[i], 'TB%d' % i

        def ps():
            if cfg['split']:
                i = NSB + nxt('psg', 8 - NACC - NSB)
            else:
                i = nxt('ps', 8 - NACC)
            return PS[i], 'PS%d' % i

        def sbank():
            i = nxt('pss', NSB)
            return PS[i], 'PS%d' % i

        def ptile():
            i = nxt('tp', NTP)
            return TP[i], 'TP%d' % i

        def acc():
            i = 8 - NACC + nxt('acc', NACC)
            return PS[i], 'PS%d' % i

        def ws():
            i = nxt('w', NW)
            return WS[i], 'WS%d' % i

        class Arena:
            def __init__(self):
                self.off = 0

            def reset(self):
                self.off = 0

            def alloc(self, ncols, dt=BF16):
                n = ncols * (2 if dt in (F32, I32) else 1)
                n = (n + 3) // 4 * 4
                a = AR[:, self.off:self.off + n]
                self.off += n
                assert self.off <= ARN, self.off
                if dt == F32:
                    a = a.bitcast(F32)
                elif dt == I32:
                    a = a.bitcast(I32)
                return a
        arena = Arena()

        def V(l, name, c=0, n=1, r0=0, r1=128):
            o = l * NV + VOFF[name] + c
            return VEC[r0:r1, o:o + n]

        def MD(l, k, c, b):
            o = (l * 48 + k * 8 + c) * NSEQ + b
            return MODS[:, o:o + 1]

        def wload(view, p, kcn, ncols, extra=None):
            w, r = ws()
            dst = w[0:p, 0:kcn * ncols].rearrange("p (k n) -> p k n", n=ncols)
            em.op('pool', 'dma_start', writes=[r], dma=True, out=dst, in_=view)
            return dst, r

        def mm(out, lhsT, rhs, start, stop, reads, writes):
            em.op('pe', 'matmul', reads=reads, writes=writes, out=out, lhsT=lhsT, rhs=rhs,
                  start=start, stop=stop)

        def rstd_from(psum_ap, rows, scale, pres, long=False):
            t1, r1 = tf()
            em.op('act', 'activation', reads=[pres], writes=[r1], out=t1[rows[0]:rows[1], :],
                  in_=psum_ap, func=AF.Ln, scale=scale, bias=EPS)
            t2, r2 = tl() if long else tf()
            em.op('act', 'activation', reads=[r1], writes=[r2], out=t2[rows[0]:rows[1], :],
                  in_=t1[rows[0]:rows[1], :], func=AF.Exp, scale=-0.5)
            return t2, r2

        def dbg_dump(name, ap_sb, res, rows=128):
            if name in dbg_d:
                t, r = tf()
                em.op('dve', 'tensor_copy', reads=[res], writes=[r], out=t[0:rows, :], in_=ap_sb)
                em.op('sp', 'dma_start', reads=[r], writes=['dbg_' + name], dma=True,
                      out=dbg_d[name][0:rows, :], in_=t[0:rows, :])

        em.op('sp', 'dma_start', writes=['VEC'], dma=True,
              out=VEC[:].rearrange("p (l n) -> p l n", n=NV), in_=vec_d.rearrange("l p n -> p l n"))
        em.op('sp', 'dma_start', writes=['IDF'], dma=True, out=IDF[:], in_=idf_d)
        em.op('sp', 'dma_start', writes=['PCOL'], dma=True, out=PCOL[:], in_=pcol_d)
        em.op('sp', 'dma_start', writes=['CTF'], dma=True, out=CTF[:], in_=cT_d)
        em.op('pool', 'dma_start', writes=['CM'], dma=True, out=CM[:], in_=cmat_d)
        em.op('pool', 'dma_start', writes=['IND'], dma=True, out=IND[:], in_=ind_d)
        em.op('act', 'activation', reads=['CTF'], writes=['CACT'], out=CACT[:], in_=CTF[:], func=AF.Silu)
        for l in range(NL):
            for g in range(12):
                view = wmod_d[l].rearrange("(kc p) n -> p kc n", p=128)[:, :, g * 512:(g + 1) * 512]
                w, wr = wload(view, 128, KC, 512)
                for j in range(4):
                    ch = g * 4 + j
                    p_, pr = ps()
                    for kc in range(KC):
                        mm(p_[:, 0:NSEQ], w[:, kc, j * 128:(j + 1) * 128], CACT[:, kc * NSEQ:(kc + 1) * NSEQ],
                           kc == 0, kc == KC - 1, [wr, 'CACT'], [pr])
                    o = (l * 48 + ch) * NSEQ
                    em.op('dve', 'tensor_scalar', reads=[pr, 'VEC'], writes=['MODS'], out=MODS[:, o:o + NSEQ],
                          in0=p_[:, 0:NSEQ], scalar1=V(l, 'bmod', ch), scalar2=None, op0=ALU.add)

        def xs_view(c, c0, c1):
            return XS[c * 128:(c + 1) * 128, c0:c1]

        for b in range(NSEQ):
            arena.reset()
            XL = [arena.alloc(D, F32) for _ in range(2)]
            XT_ = [arena.alloc(D, F32) for _ in range(2)]
            for tt in range(16):
                xl = XL[tt % 2]
                xr = 'XL%d' % (tt % 2)
                em.op('sp', 'dma_start', writes=[xr], dma=True, out=xl, in_=x_d[b, tt * 128:(tt + 1) * 128, :])
                xt = XT_[tt % 2]
                xtr = 'XT%d' % (tt % 2)
                for hh in range(2):
                    p_, pr = ps()
                    for q in range(4):
                        c = hh * 4 + q
                        em.op('pe', 'transpose', reads=[xr, 'IDF'], writes=[pr], out=p_[:, q * 128:(q + 1) * 128],
                              in_=xl[:, c * 128:(c + 1) * 128], identity=IDF[:])
                    em.op('act' if hh == 0 else 'dve', 'activation' if hh == 0 else 'tensor_copy',
                          reads=[pr], writes=[xtr + 'h%d' % hh], out=xt[:, hh * 512:(hh + 1) * 512], in_=p_[:],
                          **({'func': AF.Copy} if hh == 0 else {}))
                em.op('sp', 'dma_start', reads=[xtr + 'h0', xtr + 'h1'],
                      writes=[('XSld', tt)], dma=True,
                      out=XS[:, tt * 128:(tt + 1) * 128].rearrange("(c p) t -> p c t", p=128),
                      in_=xt.rearrange("p (c t) -> p c t", t=128))

            POSI = arena.alloc(S, I32)
            POSF = arena.alloc(S, F32)
            VV = arena.alloc(S, F32)
            KI = arena.alloc(S, I32)
            KF = arena.alloc(S, F32)
            M1 = arena.alloc(S, F32)
            em.op('sp', 'dma_start', writes=['POSI'], dma=True, out=POSI,
                  in_=pos_d[b:b + 1, :].partition_broadcast(128).rearrange("p o s -> p (o s)"))
            em.op('dve', 'tensor_copy', reads=['POSI'], writes=['POSF'], out=POSF, in_=POSI)
            for ti, (fc, pc) in enumerate([(PC_FBC, PC_PBC), (PC_FBS, PC_PBS), (PC_FAC, PC_PAC), (PC_FAS, PC_PAS)]):
                em.op('dve', 'tensor_scalar', reads=['POSF', 'PCOL'], writes=['VV'], out=VV, in0=POSF,
                      scalar1=PCOL[:, fc:fc + 1], scalar2=PCOL[:, pc:pc + 1], op0=ALU.mult, op1=ALU.add)
                em.op('dve', 'tensor_copy', reads=['VV'], writes=['KI'], out=KI, in_=VV)
                em.op('dve', 'tensor_copy', reads=['KI'], writes=['KF'], out=KF, in_=KI)
                em.op('dve', 'tensor_tensor', reads=['VV', 'KF'], writes=['VV'], out=VV, in0=VV, in1=KF, op=ALU.subtract)
                em.op('dve', 'tensor_scalar', reads=['VV'], writes=['M1'], out=M1, in0=VV, scalar1=0.5, scalar2=None, op0=ALU.is_gt)
                em.op('dve', 'tensor_tensor', reads=['VV', 'M1'], writes=['VV'], out=VV, in0=VV, in1=M1, op=ALU.subtract)
                em.op('dve', 'tensor_scalar', reads=['VV'], writes=['M1'], out=M1, in0=VV, scalar1=-0.5, scalar2=None, op0=ALU.is_lt)
                em.op('dve', 'tensor_tensor', reads=['VV', 'M1'], writes=['VV'], out=VV, in0=VV, in1=M1, op=ALU.add)
                em.op('act', 'activation', reads=['VV'], writes=[('TAB', ti)], out=TAB[:, ti * S:(ti + 1) * S],
                      in_=VV, func=AF.Sin, scale=6.283185)
            em.barrier()

            def tabv(ti, r0, r1, c0, c1):
                return TAB[r0:r1, ti * S + c0:ti * S + c1]

            for l in range(NL):
                def md8(k):
                    o = (l * 48 + k * 8) * NSEQ + b
                    return MODS[:, o:o + 8 * NSEQ].rearrange("p (c s) -> p c s", s=NSEQ)[:, :, 0:1].rearrange("p c s -> p (c s)")
                em.op('dve', 'scalar_tensor_tensor', reads=['MODS', 'VEC'], writes=['AB'], out=AB[:, 0:8], in0=md8(1),
                      scalar=1.0, in1=V(l, 'mixg', 0, 8), op0=ALU.add, op1=ALU.mult)
                em.op('dve', 'scalar_tensor_tensor', reads=['MODS', 'VEC'], writes=['AB'], out=AB[:, 8:16], in0=md8(4),
                      scalar=1.0, in1=V(l, 'ffng', 0, 8), op0=ALU.add, op1=ALU.mult)

                def norm_phase(acol, shk):
                    arena.reset()
                    XLn = [arena.alloc(KC * TT, F32) for _ in range(2)]
                    for t in range(NT):
                        xl = XLn[t % 2]
                        xr = 'XLn%d' % (t % 2)
                        em.op('sp', 'dma_start', reads=[('XS', c, t) for c in range(KC)], writes=[xr], dma=True,
                              out=xl.rearrange("p (c t) -> p c t", t=TT),
                              in_=XS[:, t * TT:(t + 1) * TT].rearrange("(c p) t -> p c t", p=128))
                        pss, pssr = ps()
                        for c in range(KC):
                            sq, sqr = tb()
                            em.op('act', 'activation', reads=[xr], writes=[sqr], out=sq[:], in_=xl[:, c * TT:(c + 1) * TT], func=AF.Square)
                            mm(pss[:], cm(CM_ONES), sq[:], c == 0, c == KC - 1, [sqr, 'CM'], [pssr])
                        rs, rsr = rstd_from(pss[:], (0, 128), 1.0 / D, pssr, long=True)
                        for c in range(KC):
                            t1, t1r = tf()
                            em.op('dve', 'scalar_tensor_tensor', reads=[xr, rsr, 'AB'], writes=[t1r], out=t1[:],
                                  in0=xl[:, c * TT:(c + 1) * TT], scalar=AB[:, acol + c:acol + c + 1], in1=rs[:],
                                  op0=ALU.mult, op1=ALU.mult)
                            em.op('act', 'activation', reads=[t1r, 'MODS'], writes=[('HT', c, t)],
                                  out=HT[:, c * S + t * TT:c * S + (t + 1) * TT], in_=t1[:], func=AF.Identity,
                                  bias=MD(l, shk, c, b), scale=1.0)
                    em.barrier()

                def hT(kc, t):
                    return HT[:, kc * S + t * TT:kc * S + (t + 1) * TT], ('HT', kc, t)

                def dense(view, p, kcn, ncols, rhs_fn, out_fn, msub=128, tiles=range(NT)):
                    w, wr = wload(view, p, kcn, ncols)
                    for j in range((ncols + msub - 1) // msub):
                        m0, m1 = j * msub, min(ncols, (j + 1) * msub)
                        for t in tiles:
                            p_, pr = ps()
                            for kc in range(kcn):
                                ra, rr = rhs_fn(kc, t)
                                mm(p_[0:m1 - m0, :], w[:, kc, m0:m1], ra, kc == 0, kc == kcn - 1, [wr, rr], [pr])
                            out_fn(j, t, p_, pr)

                def win_view(c0, c1):
                    return win_d[l].rearrange("(kc p) n -> p kc n", p=128)[:, :, c0:c1]

                evac_ctr = [0]

                def evac_copy(dst, src, reads, writes):
                    evac_ctr[0] += 1
                    if evac_ctr[0] % 2:
                        em.op('act', 'activation', reads=reads, writes=writes, out=dst, in_=src, func=AF.Copy)
                    else:
                        em.op('dve', 'tensor_copy', reads=reads, writes=writes, out=dst, in_=src)

                def finalize_o(accp, accr, h, qt):
                    os_, osr = tf()
                    em.op('dve', 'tensor_copy', reads=[accr], writes=[osr], out=os_[0:65, :], in_=accp[0:65, :])
                    l1, l1r = tf()
                    em.op('act', 'activation', reads=[osr], writes=[l1r], out=l1[64:65, :], in_=os_[64:65, :], func=AF.Ln)
                    rb, rbr = tb()
                    em.op('act', 'activation', reads=[l1r], writes=[rbr], out=rb[64:65, :], in_=l1[64:65, :], func=AF.Exp, scale=-1.0)
                    p_, pr = ps()
                    mm(p_[0:64, :], cm(CM_ONES, 64, 65, 0, 64), rb[64:65, :], True, True, [rbr, 'CM'], [pr])
                    ot, otr = tb()
                    em.op('dve', 'tensor_tensor', reads=[osr, pr], writes=[otr], out=ot[0:64, :], in0=os_[0:64, :], in1=p_[0:64, :], op=ALU.mult)
                    tg = getattr(attention, 'tag', None)
                    if tg and h == 0 and qt == 2:
                        dbg_dump(tg, ot[0:64, :], otr, 64)
                    em.op('sp', 'dma_start', reads=[otr], writes=[('OD', h, qt)], dma=True,
                          out=OD[(h % 2) * 64:(h % 2) * 64 + 64, (h // 2) * S + qt * TT:(h // 2) * S + (qt + 1) * TT], in_=ot[0:64, :])

                def attention(h, kf, qf, kdim, vap, scale, bias=None):
                    for qt in range(NT):
                        accp, accr = acc()
                        nkt = 4 * qt + 4
                        for kt in range(nkt):
                            j0 = max(0, kt - 4 * qt) * 128
                            p_, pr = sbank()
                            ka, kr = kf(kt)
                            qa, qr = qf(qt * TT + j0, (qt + 1) * TT)
                            mm(p_[:, j0:TT], ka, qa, True, bias is None, list(kr) + [qr], [pr])
                            if bias is not None:
                                n = kt // 2
                                mm(p_[:, j0:TT], IND[0:8, n * 128:(n + 1) * 128], bias[0][0:8, qt * TT + j0:(qt + 1) * TT],
                                   False, True, ['IND'] + [(bias[1], x) for x in range(qt * 4, qt * 4 + 4)], [pr])
                            pt, ptr = ptile()
                            em.op('act', 'activation', reads=[pr], writes=[ptr], out=pt[:, j0:TT], in_=p_[:, j0:TT], func=AF.Exp, scale=scale)
                            if kt >= 4 * qt:
                                em.op('dve', 'tensor_tensor', reads=[ptr, 'CM'], writes=[ptr], out=pt[:, j0:j0 + 128],
                                      in0=pt[:, j0:j0 + 128], in1=cm(CM_TRI), op=ALU.mult)
                            va, vr = vap(kt)
                            mm(accp[0:65, j0:TT], va, pt[:, j0:TT], kt == 0, kt == nkt - 1, [vr, ptr], [accr])
                        finalize_o(accp, accr, h, qt)

                def merge_phase(bi, first):
                    arena.reset()
                    if bi < 2:
                        OA = arena.alloc(4 * S)
                        for c4 in range(4):
                            for qt in range(NT):
                                em.op('sp', 'dma_start', reads=[('OD', 2 * c4, qt), ('OD', 2 * c4 + 1, qt)], writes=[('OA', c4, qt)], dma=True,
                                      out=OA[:, c4 * S + qt * TT:c4 * S + (qt + 1) * TT],
                                      in_=OD[:, c4 * S + qt * TT:c4 * S + (qt + 1) * TT])
                        wsrc = (wmo_d, wbo_d)[bi][l].rearrange("(k p) n -> p k n", p=128)
                        pp, pk = 128, 4

                        def prhs(kk, t):
                            return OA[:, kk * S + t * TT:kk * S + (t + 1) * TT], ('OA', kk, t)
                    else:
                        wsrc = wpw_d[l].rearrange("(k p) n -> p k n", p=128)
                        pp, pk = 128, 4
                        UCb = merge_phase.UC

                        def prhs(kk, t):
                            return UCb[:, kk * S + t * TT:kk * S + (t + 1) * TT], ('UC', kk, t)
                    for g in range(2):
                        wg, wgr = wload(win_view(3232 + bi * 1024 + g * 512, 3232 + bi * 1024 + (g + 1) * 512), 128, KC, 512)
                        wp, wpr = wload(wsrc[:, :, g * 512:(g + 1) * 512], pp, pk, 512)
                        for j in range(4):
                            c = g * 4 + j
                            for t in range(NT):
                                pg, pgr = ps()
                                for kc in range(KC):
                                    ra, rr = hT(kc, t)
                                    mm(pg[:], wg[:, kc, j * 128:(j + 1) * 128], ra, kc == 0, kc == KC - 1, [wgr, rr], [pgr])
                                pq, pqr = ps()
                                for kk in range(pk):
                                    ra, rr = prhs(kk, t)
                                    mm(pq[:], wp[0:pp, kk, j * 128:(j + 1) * 128], ra, kk == 0, kk == pk - 1, [wpr, rr], [pqr])
                                sg, sgr = tf()
                                em.op('act', 'activation', reads=[pgr], writes=[sgr], out=sg[:], in_=pg[:], func=AF.Sigmoid)
                                mdst = R1[:, c * S + t * TT:c * S + (t + 1) * TT]
                                if first:
                                    em.op('dve', 'tensor_tensor', reads=[pqr, sgr], writes=[('M', c, t)], out=mdst, in0=pq[:], in1=sg[:], op=ALU.mult)
                                else:
                                    t1, t1r = tf()
                                    if bi == 2:
                                        em.op('dve', 'scalar_tensor_tensor', reads=[pqr, sgr, 'VEC'], writes=[t1r], out=t1[:], in0=pq[:],
                                              scalar=V(l, 'bpw', c), in1=sg[:], op0=ALU.add, op1=ALU.mult)
                                    else:
                                        em.op('dve', 'tensor_tensor', reads=[pqr, sgr], writes=[t1r], out=t1[:], in0=pq[:], in1=sg[:], op=ALU.mult)
                                    em.op('pool', 'tensor_tensor', reads=[t1r, ('M', c, t)], writes=[('M', c, t)], out=mdst, in0=mdst, in1=t1[:], op=ALU.add)
                    em.barrier()

                def resid_phase(wview_fn, ngroups, gcols, kcn, rhs_fn, gk):
                    for g in range(ngroups):
                        w, wr = wload(wview_fn(g), 128, kcn, gcols)
                        for j in range(gcols // 128):
                            c = g * (gcols // 128) + j
                            for t in range(NT):
                                p_, pr = ps()
                                for kc in range(kcn):
                                    ra, rr = rhs_fn(kc, t)
                                    mm(p_[:], w[:, kc, j * 128:(j + 1) * 128], ra, kc == 0, kc == kcn - 1, [wr, rr], [pr])
                                xl, xlr = tf()
                                em.op('sp', 'dma_start', reads=[('XS', c, t)], writes=[xlr], dma=True, out=xl[:], in_=xs_view(c, t * TT, (t + 1) * TT))
                                xn, xnr = tf()
                                em.op('dve', 'scalar_tensor_tensor', reads=[pr, xlr, 'MODS'], writes=[xnr], out=xn[:], in0=p_[:],
                                      scalar=MD(l, gk, c, b), in1=xl[:], op0=ALU.mult, op1=ALU.add)
                                em.op('sp', 'dma_start', reads=[xnr], writes=[('XS', c, t)], dma=True, out=xs_view(c, t * TT, (t + 1) * TT), in_=xn[:])
                    em.barrier()

                norm_phase(0, 0)
                D0 = bool(dbg) and l == 0 and b == 0
                if D0:
                    dbg_dump('ht', HT[:, 0:512], ('HT', 0, 0))
                    dbg_dump('ht1', HT[:, S:S + 512], ('HT', 1, 0))
                    dbg_dump('ht7', HT[:, 7 * S + 1536:8 * S], ('HT', 7, 3))

                arena.reset()
                del tfx[:]
                tfx.extend([R1[:, i * 1024:(i + 1) * 1024].bitcast(F32) for i in range(TFX_MLA)])
                LAT = arena.alloc(6 * S)
                KPE = arena.alloc(S)
                VA = arena.alloc(16 * 8 * 65)
                QH = [arena.alloc(S) for _ in range(2)]
                KH = [arena.alloc(S) for _ in range(2)]

                def lat(c, t, r0=0, r1=128):
                    return LAT[r0:r1, c * S + t * TT:c * S + (t + 1) * TT]

                def lat_out(cbase, rows=128):
                    def f(j, t, p_, pr):
                        evac_copy(lat(cbase + j, t, 0, rows), p_[0:rows, :], [pr], [('LAT', cbase + j, t)])
                        if D0 and cbase + j == 0 and t == 0:
                            dbg_dump('zraw', lat(0, 0), ('LAT', 0, 0))
                    return f
                dense(win_view(0, 512), 128, KC, 512, hT, lat_out(0))
                dense(win_view(512, 640), 128, KC, 128, hT, lat_out(4))
                dense(win_view(576, 672), 128, KC, 96, hT, lat_out(5, 96), msub=96)
                for (c0, ncn, gname) in [(0, 3, 'qlg'), (3, 2, 'kvg')]:
                    for t in range(NT):
                        pss, pssr = ps()
                        for c in range(ncn):
                            sq, sqr = tb()
                            em.op('act', 'activation', reads=[('LAT', c0 + c, t)], writes=[sqr], out=sq[:], in_=lat(c0 + c, t), func=AF.Square)
                            mm(pss[:], cm(CM_ONES), sq[:], c == 0, c == ncn - 1, [sqr, 'CM'], [pssr])
                        rs, rsr = rstd_from(pss[:], (0, 128), 1.0 / (ncn * 128), pssr)
                        for c in range(ncn):
                            em.op('dve', 'scalar_tensor_tensor', reads=[('LAT', c0 + c, t), rsr, 'VEC'], writes=[('LAT', c0 + c, t)],
                                  out=lat(c0 + c, t), in0=lat(c0 + c, t), scalar=V(l, gname, c), in1=rs[:], op0=ALU.mult, op1=ALU.mult)

                def head_norm_rope(src_ps, src_res, rows, bdm, gcol, dst, dst_res, t, rope):
                    r0, r1 = rows
                    sq, sqr = tb()
                    em.op('act', 'activation', reads=[src_res], writes=[sqr], out=sq[r0:r1, :], in_=src_ps, func=AF.Square)
                    p2, p2r = ps()
                    mm(p2[r0:r1, :], cm(bdm, r0, r1, r0, r1), sq[r0:r1, :], True, True, [sqr, 'CM'], [p2r])
                    t1, t1r = tf()
                    em.op('act', 'activation', reads=[p2r, 'PCOL'], writes=[t1r], out=t1[r0:r1, :], in_=p2[r0:r1, :], func=AF.Ln,
                          scale=PCOL[r0:r1, PC_QSC:PC_QSC + 1] if bdm == CM_BDA else 1.0 / 64, bias=EPS)
                    rs, rsr = tf()
                    em.op('act', 'activation', reads=[t1r], writes=[rsr], out=rs[r0:r1, :], in_=t1[r0:r1, :], func=AF.Exp, scale=-0.5)
                    if not rope:
                        em.op('dve', 'scalar_tensor_tensor', reads=[src_res, rsr, 'VEC'], writes=[dst_res], out=dst, in0=src_ps,
                              scalar=gcol, in1=rs[r0:r1, :], op0=ALU.mult, op1=ALU.mult)
                        return
                    ctab, stab, psw = rope
                    qn, qnr = tb()
                    em.op('dve', 'scalar_tensor_tensor', reads=[src_res, rsr, 'VEC'], writes=[qnr], out=qn[r0:r1, :], in0=src_ps,
                          scalar=gcol, in1=rs[r0:r1, :], op0=ALU.mult, op1=ALU.mult)
                    p3, p3r = ps()
                    mm(p3[r0:r1, :], cm(psw, r0, r1, r0, r1), qn[r0:r1, :], True, True, [qnr, 'CM'], [p3r])
                    a1, a1r = tf()
                    em.op('dve', 'tensor_tensor', reads=[qnr, ('TAB', ctab)], writes=[a1r], out=a1[r0:r1, :], in0=qn[r0:r1, :],
                          in1=tabv(ctab, r0, r1, t * TT, (t + 1) * TT), op=ALU.mult)
                    a2, a2r = tf()
                    em.op('dve', 'tensor_tensor', reads=[p3r, ('TAB', stab)], writes=[a2r], out=a2[r0:r1, :], in0=p3[r0:r1, :],
                          in1=tabv(stab, r0, r1, t * TT, (t + 1) * TT), op=ALU.mult)
                    em.op('pool', 'tensor_tensor', reads=[a1r, a2r], writes=[dst_res], out=dst, in0=a1[r0:r1, :], in1=a2[r0:r1, :], op=ALU.add)

                if D0:
                    dbg_dump('qn', lat(0, 0), ('LAT', 0, 0))
                    dbg_dump('kvn', lat(3, 0), ('LAT', 3, 0))
                for t in range(NT):
                    head_norm_rope(lat(5, t, 64, 96), ('LAT', 5, t), (64, 96), CM_BDA, V(l, 'gk', 0, 1, 64, 96),
                                   KPE[64:96, t * TT:(t + 1) * TT], ('KPE', t), t, (2, 3, CM_PSWA))
                if D0:
                    dbg_dump('kpe', KPE[64:96, 0:512], ('KPE', 0), 32)
                em.op('pool', 'memset', writes=['VAones'], ap=VA.rearrange("p (n d) -> p n d", d=65)[:, :, 64:65], constant=1.0)
                wkv, wkvr = wload(wukv_d[l].rearrange("(kc p) n -> p kc n", p=128), 128, 2, 1024)
                for tt in range(16):
                    p_, pr = ps()
                    for kc in range(2):
                        mm(p_[:].rearrange("p (h d) -> p h d", d=64), LAT[:, (3 + kc) * S + tt * 128:(3 + kc) * S + (tt + 1) * 128],
                           wkv[:, kc, :].rearrange("p (h d) -> p h d", d=128)[:, :, 64:128], kc == 0, kc == 1,
                           [wkvr, ('LAT', 3 + kc, tt // 4)], [pr])
                    evac_copy(VA[:, tt * 520:(tt + 1) * 520].rearrange("p (h d) -> p h d", d=65)[:, :, 0:64],
                              p_[:].rearrange("p (h d) -> p h d", d=64), [pr, 'VAones'], [('VA', tt)])
                wq, wqr = wload(wuq_d[l].rearrange("(kc p) n -> p kc n", p=128), 128, 3, 768)
                cfg['split'] = True
                for h in range(8):
                    qh, kh = QH[h % 2], KH[h % 2]
                    qres, kres = 'QH%d' % (h % 2), 'KH%d' % (h % 2)
                    for t in range(NT):
                        p_, pr = ps()
                        for kc in range(3):
                            mm(p_[0:96, :], wq[:, kc, h * 96:(h + 1) * 96], lat(kc, t), kc == 0, kc == 2, [wqr, ('LAT', kc, t)], [pr])
                        head_norm_rope(p_[0:96, :], pr, (0, 96), CM_BDA, V(l, 'gq', 0, 1, 0, 96), qh[0:96, t * TT:(t + 1) * TT], (qres, t), t,
                                       (2, 3, CM_PSWA))
                        p_, pr = ps()
                        for kc in range(2):
                            mm(p_[0:64, :], wkv[:, kc, h * 128:h * 128 + 64], lat(3 + kc, t), kc == 0, kc == 1, [wkvr, ('LAT', 3 + kc, t)], [pr])
                        head_norm_rope(p_[0:64, :], pr, (0, 64), CM_BD64, V(l, 'gk', 0, 1, 0, 64), kh[0:64, t * TT:(t + 1) * TT], (kres, t, 'n'), t, None)
                        em.op('dve', 'tensor_copy', reads=[('KPE', t)], writes=[(kres, t, 'r')], out=kh[64:96, t * TT:(t + 1) * TT],
                              in_=KPE[64:96, t * TT:(t + 1) * TT])
                    if D0 and h == 0:
                        dbg_dump('qh0', qh[0:96, 0:512], (qres, 0), 96)
                        dbg_dump('kh0', kh[0:96, 0:512], (kres, 0, 'n'), 96)
                        dbg_dump('va0', VA[:, 0:512], ('VA', 0))
                    attention.tag = 'oa' if D0 else None
                    attention(h,
                              lambda kt, kh=kh, kres=kres: (kh[0:96, kt * 128:(kt + 1) * 128], [(kres, kt // 4, 'n'), (kres, kt // 4, 'r')]),
                              lambda c0, c1, qh=qh, qres=qres: (qh[0:96, c0:c1], (qres, c0 // TT)),
                              96,
                              lambda kt, h=h: (VA[:, kt * 520 + h * 65:kt * 520 + h * 65 + 65], ('VA', kt)),
                              96.0 ** -0.5)
                em.barrier()
                cfg['split'] = False
                del tfx[:]
                merge_phase(0, True)

                arena.reset()
                QKB = arena.alloc(8 * S)
                VB = arena.alloc(16 * 8 * 65)
                BIA = [arena.alloc(S) for _ in range(2)]
                KMH = arena.alloc(8)
                KML = arena.alloc(8)
                KMF = arena.alloc(8, F32)
                KMD = arena.alloc(8, F32)
                del tfx[:]
                tfx.extend([arena.alloc(512, F32) for _ in range(TFX_MOBA)])

                def qkb(c, t, r0=0, r1=128):
                    return QKB[r0:r1, c * S + t * TT:c * S + (t + 1) * TT]

                def qk_out(cbase):
                    def f(j, t, p_, pr):
                        c = cbase + j
                        head_norm_rope(p_[:], pr, (0, 128), CM_BD64, V(l, 'mq' if c < 4 else 'mk', 0, 1), qkb(c, t), ('QKB', c, t), t,
                                       (0, 1, CM_PSW64))
                    return f
                dense(win_view(672, 1184), 128, KC, 512, hT, qk_out(0))
                dense(win_view(1184, 1696), 128, KC, 512, hT, qk_out(4))
                em.op('pool', 'memset', writes=['VBones'], ap=VB.rearrange("p (n d) -> p n d", d=65)[:, :, 64:65], constant=1.0)
                wv, wvr = wload(win_view(1696, 2208), 128, KC, 512)
                for tt in range(16):
                    p_, pr = ps()
                    for kc in range(KC):
                        mm(p_[:], HT[:, kc * S + tt * 128:kc * S + (tt + 1) * 128], wv[:, kc, :], kc == 0, kc == KC - 1,
                           [wvr, ('HT', kc, tt // 4)], [pr])
                    evac_copy(VB[:, tt * 520:(tt + 1) * 520].rearrange("p (h d) -> p h d", d=65)[:, :, 0:64],
                              p_[:].rearrange("p (h d) -> p h d", d=64), [pr, 'VBones'], [('VB', tt)])
                cfg['split'] = True
                for h in range(8):
                    c = h // 2
                    r0 = (h % 2) * 64
                    r1 = r0 + 64
                    bia = BIA[h % 2]
                    bres = 'BIA%d' % (h % 2)
                    em.op('dve', 'tensor_reduce', reads=[('QKB', 4 + c, t) for t in range(NT)], writes=['KMF'], out=KMF[r0:r1, :],
                          in_=QKB[r0:r1, (4 + c) * S:(5 + c) * S].rearrange("p (n j) -> p n j", j=256), axis=AX.X, op=ALU.add)
                    em.op('dve', 'tensor_copy', reads=['KMF'], writes=['KMH'], out=KMH[r0:r1, :], in_=KMF[r0:r1, :])
                    em.op('dve', 'tensor_tensor', reads=['KMF', 'KMH'], writes=['KMD'], out=KMD[r0:r1, :], in0=KMF[r0:r1, :], in1=KMH[r0:r1, :], op=ALU.subtract)
                    em.op('dve', 'tensor_copy', reads=['KMD'], writes=['KML'], out=KML[r0:r1, :], in_=KMD[r0:r1, :])
                    for tt in range(16):
                        qb = tt // 2
                        sc = SM[:, 0:8]
                        bt = SM[:, 16:24]
                        m8 = SM[:, 32:40]
                        if qb >= 4:
                            p_, pr = ps()
                            mm(p_[:, 0:8], QKB[r0:r1, c * S + tt * 128:c * S + (tt + 1) * 128], KMH[r0:r1, :], True, False,
                               [('QKB', c, tt // 4), 'KMH'], [pr])
                            mm(p_[:, 0:8], QKB[r0:r1, c * S + tt * 128:c * S + (tt + 1) * 128], KML[r0:r1, :], False, True,
                               [('QKB', c, tt // 4), 'KML'], [pr])
                            em.op('dve', 'memset', writes=['SC'], ap=sc, constant=-1e30)
                            em.op('dve', 'tensor_copy', reads=[pr, 'SC'], writes=['SC'], out=sc[:, 0:qb], in_=p_[:, 0:qb])
                            em.op('dve', 'max', reads=['SC'], writes=['M8'], out=m8, in_=sc)
                            em.op('dve', 'tensor_scalar', reads=['SC', 'M8'], writes=['BT'], out=bt, in0=sc, scalar1=m8[:, 2:3], scalar2=None, op0=ALU.is_ge)
                            em.op('dve', 'tensor_scalar', reads=['BT'], writes=['BT'], out=bt, in0=bt, scalar1=1.0, scalar2=BIG, op0=ALU.subtract, op1=ALU.mult)
                            em.op('dve', 'memset', reads=['BT'], writes=['BT'], ap=bt[:, qb:qb + 1], constant=0.0)
                        else:
                            em.op('dve', 'memset', writes=['BT'], ap=bt, constant=-BIG)
                            em.op('dve', 'memset', reads=['BT'], writes=['BT'], ap=bt[:, 0:qb + 1], constant=0.0)
                        p2, p2r = ps()
                        em.op('pe', 'transpose', reads=['BT', 'IDF'], writes=[p2r], out=p2[0:8, 0:128], in_=bt, identity=IDF[:])
                        em.op('act', 'activation', reads=[p2r], writes=[(bres, tt)], out=bia[0:8, tt * 128:(tt + 1) * 128], in_=p2[0:8, 0:128], func=AF.Copy)
                    if D0 and h == 0:
                        dbg_dump('qb0', qkb(0, 2), ('QKB', 0, 2))
                        dbg_dump('kb0', qkb(4, 0), ('QKB', 4, 0))
                        dbg_dump('bia', bia[0:8, 1024:1536], (bres, 8), 8)
                    attention.tag = 'ob' if D0 else None
                    attention(h,
                              lambda kt, c=c, r0=r0, r1=r1: (QKB[r0:r1, (4 + c) * S + kt * 128:(4 + c) * S + (kt + 1) * 128], [('QKB', 4 + c, kt // 4)]),
                              lambda c0, c1, c=c, r0=r0, r1=r1: (QKB[r0:r1, c * S + c0:c * S + c1], ('QKB', c, c0 // TT)),
                              64,
                              lambda kt, h=h: (VB[:, kt * 520 + h * 65:kt * 520 + h * 65 + 65], ('VB', kt)),
                              0.125, bias=(bia, bres))
                em.barrier()
                cfg['split'] = False
                del tfx[:]
                merge_phase(1, False)

                arena.reset()
                UW = S + 32
                UU = arena.alloc(4 * UW)
                CV = arena.alloc(4 * S)
                UC = arena.alloc(4 * S)
                DG = [arena.alloc(31 * 128) for _ in range(2)]
                merge_phase.UC = UC
                for c in range(4):
                    em.op('pool', 'memset', writes=[('UUh', c)], ap=UU[:, c * UW:c * UW + 32], constant=0.0)
                    w, wr = ws()
                    w3 = w[:, 0:KC * 256].rearrange("p (k n) -> p k n", n=256)
                    em.op('pool', 'dma_start', writes=[wr], dma=True, out=w3[:, :, 0:128], in_=win_view(2208 + c * 128, 2208 + (c + 1) * 128))
                    em.op('pool', 'dma_start', reads=[wr], writes=[wr], dma=True, out=w3[:, :, 128:256], in_=win_view(2720 + c * 128, 2720 + (c + 1) * 128))
                    for t in range(NT):
                        pa, par = ps()
                        pg, pgr = ps()
                        for kc in range(KC):
                            ra, rr = hT(kc, t)
                            mm(pa[:], w3[:, kc, 0:128], ra, kc == 0, kc == KC - 1, [wr, rr], [par])
                        for kc in range(KC):
                            ra, rr = hT(kc, t)
                            mm(pg[:], w3[:, kc, 128:256], ra, kc == 0, kc == KC - 1, [wr, rr], [pgr])
                        sg, sgr = tf()
                        em.op('act', 'activation', reads=[pgr], writes=[sgr], out=sg[:], in_=pg[:], func=AF.Sigmoid)
                        em.op('dve', 'tensor_tensor', reads=[par, sgr], writes=[('UU', c, t)], out=UU[:, c * UW + 32 + t * TT:c * UW + 32 + (t + 1) * TT],
                              in0=pa[:], in1=sg[:], op=ALU.mult)
                for c in range(4):
                    dg = DG[c % 2]
                    dgr = 'DG%d' % (c % 2)
                    for k in range(31):
                        em.op('dve', 'tensor_scalar', reads=['CM', 'VEC'], writes=[(dgr, k)], out=dg[:, k * 128:(k + 1) * 128], in0=cm(CM_ID),
                              scalar1=V(l, 'cw', c * 31 + k), scalar2=None, op0=ALU.mult)
                    for t in range(NT):
                        p_, pr = ps()
                        for k in range(31):
                            o = c * UW + 2 + t * TT + k
                            rds = [(dgr, k), ('UU', c, t)]
                            if t > 0:
                                rds.append(('UU', c, t - 1))
                            else:
                                rds.append(('UUh', c))
                            mm(p_[:], dg[:, k * 128:(k + 1) * 128], UU[:, o:o + TT], k == 0, k == 30, rds, [pr])
                        em.op('act', 'activation', reads=[pr, 'VEC'], writes=[('CV', c, t)], out=CV[:, c * S + t * TT:c * S + (t + 1) * TT], in_=p_[:],
                              func=AF.Identity, bias=V(l, 'cb', c), scale=1.0)
                for t in range(NT):
                    pm, pmr = ps()
                    pq, pqr = ps()
                    for c in range(4):
                        cv = CV[:, c * S + t * TT:c * S + (t + 1) * TT]
                        mm(pm[:], cm(CM_ONES), cv, c == 0, c == 3, [('CV', c, t), 'CM'], [pmr])
                        sq, sqr = tb()
                        em.op('act', 'activation', reads=[('CV', c, t)], writes=[sqr], out=sq[:], in_=cv, func=AF.Square)
                        mm(pq[:], cm(CM_ONES), sq[:], c == 0, c == 3, [sqr, 'CM'], [pqr])
                    mean, meanr = tl()
                    em.op('act', 'activation', reads=[pmr], writes=[meanr], out=mean[:], in_=pm[:], func=AF.Copy, scale=1.0 / 512)
                    msq, msqr = tf()
                    em.op('dve', 'tensor_tensor', reads=[meanr], writes=[msqr], out=msq[:], in0=mean[:], in1=mean[:], op=ALU.mult)
                    var, varr = tf()
                    em.op('dve', 'scalar_tensor_tensor', reads=[pqr, msqr], writes=[varr], out=var[:], in0=pq[:], scalar=1.0 / 512, in1=msq[:],
                          op0=ALU.mult, op1=ALU.subtract)
                    rs, rsr = rstd_from(var[:], (0, 128), 1.0, varr, long=True)
                    for c in range(4):
                        cv = CV[:, c * S + t * TT:c * S + (t + 1) * TT]
                        d1, d1r = tf()
                        em.op('dve', 'tensor_tensor', reads=[('CV', c, t), meanr], writes=[d1r], out=d1[:], in0=cv, in1=mean[:], op=ALU.subtract)
                        d2, d2r = tf()
                        em.op('dve', 'tensor_tensor', reads=[d1r, rsr], writes=[d2r], out=d2[:], in0=d1[:], in1=rs[:], op=ALU.mult)
                        em.op('act', 'activation', reads=[d2r, 'VEC'], writes=[('UC', c, t)], out=UC[:, c * S + t * TT:c * S + (t + 1) * TT], in_=d2[:],
                              func=AF.Silu, scale=V(l, 'lng', c), bias=V(l, 'lnb', c))
                if D0:
                    dbg_dump('uu0', UU[:, 32:544], ('UU', 0, 0))
                    dbg_dump('cv0', CV[:, 0:512], ('CV', 0, 0))
                    dbg_dump('uc0', UC[:, 0:512], ('UC', 0, 0))
                em.barrier()
                merge_phase(2, False)
                if D0:
                    dbg_dump('m0', R1[:, 0:512], ('M', 0, 0))
                    em.barrier()

                arena.reset()
                XLn = [arena.alloc(KC * TT, F32) for _ in range(2)]
                wo = []
                for g in range(2):
                    wo.append(wload(wout_d[l].rearrange("(kc p) n -> p kc n", p=128)[:, :, g * 512:(g + 1) * 512], 128, KC, 512))
                for t in range(NT):
                    xb = XLn[t % 2]
                    xbr = 'XLn%d' % (t % 2)
                    pss, pssr = acc()
                    for c in range(KC):
                        w, wr = wo[c // 4]
                        j = c % 4
                        p_, pr = ps()
                        for kc in range(KC):
                            mm(p_[:], w[:, kc, j * 128:(j + 1) * 128], R1[:, kc * S + t * TT:kc * S + (t + 1) * TT], kc == 0, kc == KC - 1,
                               [wr, ('M', kc, t)], [pr])
                        xl, xlr = tf()
                        em.op('sp', 'dma_start', reads=[('XS', c, t)], writes=[xlr], dma=True, out=xl[:], in_=xs_view(c, t * TT, (t + 1) * TT))
                        dst = xb[:, c * TT:(c + 1) * TT]
                        em.op('dve', 'scalar_tensor_tensor', reads=[pr, xlr, 'MODS'], writes=[(xbr, c)], out=dst, in0=p_[:],
                              scalar=MD(l, 2, c, b), in1=xl[:], op0=ALU.mult, op1=ALU.add)
                        em.op('sp', 'dma_start', reads=[(xbr, c)], writes=[('XS', c, t)], dma=True, out=xs_view(c, t * TT, (t + 1) * TT), in_=dst)
                        sq, sqr = tb()
                        em.op('act', 'activation', reads=[(xbr, c)], writes=[sqr], out=sq[:], in_=dst, func=AF.Square)
                        mm(pss[:], cm(CM_ONES), sq[:], c == 0, c == KC - 1, [sqr, 'CM'], [pssr])
                    rs, rsr = rstd_from(pss[:], (0, 128), 1.0 / D, pssr, long=True)
                    for c in range(KC):
                        t1, t1r = tf()
                        em.op('dve', 'scalar_tensor_tensor', reads=[(xbr, c), rsr, 'AB'], writes=[t1r], out=t1[:],
                              in0=xb[:, c * TT:(c + 1) * TT], scalar=AB[:, 8 + c:8 + c + 1], in1=rs[:],
                              op0=ALU.mult, op1=ALU.mult)
                        em.op('act', 'activation', reads=[t1r, 'MODS'], writes=[('HT', c, t)],
                              out=HT[:, c * S + t * TT:c * S + (t + 1) * TT], in_=t1[:], func=AF.Identity,
                              bias=MD(l, 3, c, b), scale=1.0)
                em.barrier()

                for half in range(2):
                    arena.reset()
                    FA = arena.alloc(11 * S)
                    UAB = [[arena.alloc(S + 4) for _ in range(2)] for _ in range(2)]
                    DGF = [arena.alloc(6 * 128) for _ in range(2)]
                    for jj in range(11):
                        ca = half * 11 + jj
                        cb_ = 22 + ca
                        ua, ub = UAB[jj % 2]
                        uar, ubr = 'UA%d' % (jj % 2), 'UB%d' % (jj % 2)
                        dgf = DGF[jj % 2]
                        dgfr = 'DGF%d' % (jj % 2)
                        em.op('pool', 'memset', writes=[(uar, 'h')], ap=ua[:, 0:4], constant=0.0)
                        em.op('pool', 'memset', writes=[(ubr, 'h')], ap=ub[:, 0:4], constant=0.0)
                        NPF = 3 - FFN_DVE_TAPS
                        for k in range(NPF):
                            em.op('dve', 'tensor_scalar', reads=['CM', 'VEC'], writes=[(dgfr, k)], out=dgf[:, k * 128:(k + 1) * 128], in0=cm(CM_ID),
                                  scalar1=V(l, 'fw', ca * 3 + k), scalar2=None, op0=ALU.mult)
                            em.op('dve', 'tensor_scalar', reads=['CM', 'VEC'], writes=[(dgfr, 3 + k)], out=dgf[:, (3 + k) * 128:(4 + k) * 128], in0=cm(CM_ID),
                                  scalar1=V(l, 'fw', cb_ * 3 + k), scalar2=None, op0=ALU.mult)
                        w, wr = ws()
                        w3 = w[:, 0:KC * 256].rearrange("p (k n) -> p k n", n=256)
                        wup_v = wup_d[l].rearrange("(kc p) n -> p kc n", p=128)
                        em.op('pool', 'dma_start', writes=[wr], dma=True, out=w3[:, :, 0:128], in_=wup_v[:, :, ca * 128:(ca + 1) * 128])
                        em.op('pool', 'dma_start', reads=[wr], writes=[wr], dma=True, out=w3[:, :, 128:256], in_=wup_v[:, :, cb_ * 128:(cb_ + 1) * 128])
                        for t in range(NT):
                            pa, par = ps()
                            pb, pbr = ps()
                            for kc in range(KC):
                                ra, rr = hT(kc, t)
                                mm(pa[:], w3[:, kc, 0:128], ra, kc == 0, kc == KC - 1, [wr, rr], [par])
                            for kc in range(KC):
                                ra, rr = hT(kc, t)
                                mm(pb[:], w3[:, kc, 128:256], ra, kc == 0, kc == KC - 1, [wr, rr], [pbr])
                            em.op('act', 'activation', reads=[par], writes=[(uar, t)], out=ua[:, 4 + t * TT:4 + (t + 1) * TT], in_=pa[:], func=AF.Copy)
                            em.op('dve', 'tensor_copy', reads=[pbr], writes=[(ubr, t)], out=ub[:, 4 + t * TT:4 + (t + 1) * TT], in_=pb[:])
                        for t in range(NT):
                            pa, par = ps()
                            pb, pbr = ps()
                            hrd = lambda ur: [(ur, t), (ur, t - 1) if t > 0 else (ur, 'h')]
                            for (pp_, ppr, u, ur, k0) in [(pa, par, ua, uar, 0), (pb, pbr, ub, ubr, 3)]:
                                for k in range(NPF):
                                    o = 2 + t * TT + k
                                    mm(pp_[:], dgf[:, (k0 + k) * 128:(k0 + k + 1) * 128], u[:, o:o + TT], k == 0, k == NPF - 1, [(dgfr, k0 + k)] + hrd(ur), [ppr])
                            srcs = []
                            for (pp_, ppr, u, ur, ch) in [(pa, par, ua, uar, ca), (pb, pbr, ub, ubr, cb_)]:
                                src, srcr = pp_, ppr
                                for k in range(NPF, 3):
                                    o = 2 + t * TT + k
                                    tk, tkr = tf()
                                    em.op('dve', 'scalar_tensor_tensor', reads=hrd(ur) + [srcr, 'VEC'], writes=[tkr], out=tk[:], in0=u[:, o:o + TT],
                                          scalar=V(l, 'fw', ch * 3 + k), in1=src[:], op0=ALU.mult, op1=ALU.add)
                                    src, srcr = tk, tkr
                                srcs.append((src, srcr))
                            aa, aar = tb()
                            em.op('act', 'activation', reads=[srcs[0][1], 'VEC'], writes=[aar], out=aa[:], in_=srcs[0][0][:], func=AF.Silu, bias=V(l, 'fb', ca), scale=1.0)
                            em.op('dve', 'scalar_tensor_tensor', reads=[srcs[1][1], aar, 'VEC'], writes=[('FA', jj, t)], out=FA[:, jj * S + t * TT:jj * S + (t + 1) * TT],
                                  in0=srcs[1][0][:], scalar=V(l, 'fb', cb_), in1=aa[:], op0=ALU.add, op1=ALU.mult)
                    resid_phase(lambda g, half=half: wdn_d[l][half * 1408:(half + 1) * 1408, :].rearrange("(k p) n -> p k n", p=128)[:, :, g * 256:(g + 1) * 256],
                                4, 256, 11, lambda kk, t: (FA[:, kk * S + t * TT:kk * S + (t + 1) * TT], ('FA', kk, t)), 5)

            arena.reset()
            XL = [arena.alloc(D, F32) for _ in range(2)]
            XT_ = [arena.alloc(D, F32) for _ in range(2)]
            for tt in range(16):
                xl = XL[tt % 2]
                xr = 'XL%d' % (tt % 2)
                em.op('sp', 'dma_start', writes=[xr], dma=True, out=xl.rearrange("p (c t) -> p c t", t=128),
                      in_=XS[:, tt * 128:(tt + 1) * 128].rearrange("(c p) t -> p c t", p=128))
                xt = XT_[tt % 2]
                xtr = 'XT%d' % (tt % 2)
                for hh in range(2):
                    p_, pr = ps()
                    for q in range(4):
                        c = hh * 4 + q
                        em.op('pe', 'transpose', reads=[xr, 'IDF'], writes=[pr], out=p_[:, q * 128:(q + 1) * 128],
                              in_=xl[:, c * 128:(c + 1) * 128], identity=IDF[:])
                    em.op('act' if hh == 0 else 'dve', 'activation' if hh == 0 else 'tensor_copy',
                          reads=[pr], writes=[xtr + 'h%d' % hh], out=xt[:, hh * 512:(hh + 1) * 512], in_=p_[:],
                          **({'func': AF.Copy} if hh == 0 else {}))
                em.op('sp', 'dma_start', reads=[xtr + 'h0', xtr + 'h1'], writes=[('OUT', tt)], dma=True,
                      out=out_d[b, tt * 128:(tt + 1) * 128, :], in_=xt)
            em.barrier()

        em.emit(st)
    return nc


def _consts():
    cmat = np.zeros((128, NCM * 128), np.float32)
    i = np.arange(128)
    cmat[:, CM_ID * 128:(CM_ID + 1) * 128] = np.eye(128)
    cmat[:, CM_ONES * 128:(CM_ONES + 1) * 128] = 1.0
    bd = (i[:, None] // 64 == i[None, :] // 64).astype(np.float32)
    cmat[:, CM_BD64 * 128:(CM_BD64 + 1) * 128] = bd
    grp = np.where(i < 64, 0, np.where(i < 96, 1, 2))
    cmat[:, CM_BDA * 128:(CM_BDA + 1) * 128] = (grp[:, None] == grp[None, :]).astype(np.float32)
    pi64 = (i // 64) * 64 + ((i % 64) + 32) % 64
    m = np.zeros((128, 128), np.float32)
    m[pi64, i] = 1.0
    cmat[:, CM_PSW64 * 128:(CM_PSW64 + 1) * 128] = m
    pia = i.copy()
    for j in range(64, 96):
        pia[j] = 64 + ((j - 64) + 16) % 32
    m = np.zeros((128, 128), np.float32)
    m[pia, i] = 1.0
    cmat[:, CM_PSWA * 128:(CM_PSWA + 1) * 128] = m
    cmat[:, CM_TRI * 128:(CM_TRI + 1) * 128] = (i[:, None] <= i[None, :]).astype(np.float32)
    ind = np.zeros((8, 1024), np.float32)
    for n in range(8):
        ind[n, n * 128:(n + 1) * 128] = 1.0
    pcol = np.zeros((128, NPC), np.float32)
    invb = 1.0 / (10000.0 ** (np.arange(0, 64, 2, dtype=np.float32) / 64.0))
    inva = 1.0 / (10000.0 ** (np.arange(0, 32, 2, dtype=np.float32) / 32.0))
    tw = 2.0 * math.pi
    fb = invb[i % 32]
    sgn = np.where((i % 64) < 32, -1.0, 1.0)
    pcol[:, PC_FBS] = sgn * fb / tw
    pcol[:, PC_PBS] = 0.0
    pcol[:, PC_FBC] = fb / tw
    pcol[:, PC_PBC] = 0.25
    fa = np.zeros(128)
    sa = np.zeros(128)
    for j in range(64, 96):
        fa[j] = inva[(j - 64) % 16]
        sa[j] = -1.0 if j < 80 else 1.0
    pcol[:, PC_FAS] = sa * fa / tw
    pcol[:, PC_PAS] = 0.0
    pcol[:, PC_FAC] = fa / tw
    pcol[:, PC_PAC] = 0.25
    pcol[:, PC_QSC] = np.where(i < 64, 1.0 / 64, 1.0 / 32)
    return cmat, ind, np.eye(128, dtype=np.float32), pcol


def _vecpack(inp, NL):
    vec = np.zeros((NL, 128, NV), np.float32)

    def put(l, name, arr2d):
        o = VOFF[name]
        vec[l, :arr2d.shape[0], o:o + arr2d.shape[1]] = arr2d
    for l in range(NL):
        put(l, 'mixg', inp['mix_norm_g'][l].reshape(8, 128).T)
        put(l, 'ffng', inp['ffn_norm_g'][l].reshape(8, 128).T)
        put(l, 'bmod', inp['b_mod'][l].reshape(48, 128).T)
        put(l, 'qlg', inp['mla_q_norm_g'][l].reshape(3, 128).T)
        put(l, 'kvg', inp['mla_kv_norm_g'][l].reshape(2, 128).T)
        put(l, 'gq', np.concatenate([inp['mla_qn_nope_g'][l], inp['mla_qn_rope_g'][l]])[:, None])
        put(l, 'gk', np.concatenate([inp['mla_kn_nope_g'][l], inp['mla_kn_rope_g'][l]])[:, None])
        put(l, 'mq', np.tile(inp['moba_qn_g'][l], 2)[:, None])
        put(l, 'mk', np.tile(inp['moba_kn_g'][l], 2)[:, None])
        put(l, 'cw', inp['conv_dw_w'][l].reshape(31, 4, 128).transpose(2, 1, 0).reshape(128, 124))
        put(l, 'cb', inp['conv_dw_b'][l].reshape(4, 128).T)
        put(l, 'lng', inp['conv_ln_g'][l].reshape(4, 128).T)
        put(l, 'lnb', inp['conv_ln_b'][l].reshape(4, 128).T)
        put(l, 'bpw', inp['b_conv_pw2'][l].reshape(8, 128).T)
        put(l, 'fw', inp['ffn_dw_w'][l].reshape(3, 44, 128).transpose(2, 1, 0).reshape(128, 132))
        put(l, 'fb', inp['ffn_dw_b'][l].reshape(44, 128).T)
    return vec


_WNAMES = ['w_mod', 'w_in', 'w_uq', 'w_ukv', 'w_mla_o', 'w_moba_o', 'w_conv_pw2', 'w_out', 'w_up', 'w_down']


def make_in_maps(inp, ncores, nseq, NL):
    cmat, ind, idf, pcol = _consts()
    vec = _vecpack(inp, NL)
    shared = {n: np.ascontiguousarray(np.asarray(inp[n], np.float32)[:NL]) for n in _WNAMES}
    shared.update(vec=vec, cmat=cmat, ind=ind, identf=idf, pcol=pcol)
    maps = []
    for i in range(ncores):
        b0 = i * nseq
        c = np.asarray(inp['c'], np.float32)[b0:b0 + nseq]
        cT = np.ascontiguousarray(c.reshape(nseq, 8, 128).transpose(2, 1, 0).reshape(128, 8 * nseq))
        m = dict(shared)
        m['x'] = np.ascontiguousarray(np.asarray(inp['x'], np.float32)[b0:b0 + nseq])
        m['cT'] = cT
        m['pos'] = np.ascontiguousarray(np.asarray(inp['positions'], np.int32)[b0:b0 + nseq])
        maps.append(m)
    return maps


def kernel(**inputs):
    nc = build_program(2, 2)
    maps = make_in_maps(inputs, NCORE, 2, 2)
    res = run_bass_kernel_spmd(nc, maps, core_ids=list(range(NCORE)))
    return np.concatenate([np.asarray(r["out"], np.float32) for r in res.results], axis=0)
```
